# Optimizing a Trainium2 kernel written in Bass

```python
import math
import jax, jax.numpy as jnp
from jax import lax
import numpy as np

D_MODEL = 1024
BATCH = 16
SEQ = 2048
DEPTH = 2

CTX_LEN = 256
GRID_W = 64
EPS = 1e-6

MLSTM_HEADS = 4
MLSTM_HEAD_DIM = D_MODEL // 8
MLSTM_WIDTH = MLSTM_HEADS * MLSTM_HEAD_DIM
MLSTM_CHUNK = 128
FORGET_BIAS = 4.0
SGU_GROUPS = 4
SGU_WIDTH = D_MODEL // 4
SGU_CHUNK = 128
CONV_WIDTH = D_MODEL // 4
CONV_K = 31

MIX_WIDTH = MLSTM_WIDTH + SGU_WIDTH + CONV_WIDTH
FFN_HIDDEN = int(math.ceil(8 * D_MODEL / 3 / 256)) * 256

Q_OFF = 0
K_OFF = MLSTM_WIDTH
V_OFF = 2 * MLSTM_WIDTH
O_OFF = 3 * MLSTM_WIDTH
GATE_OFF = 4 * MLSTM_WIDTH
SGU_OFF = GATE_OFF + 4 * MLSTM_HEADS
CONV_OFF = SGU_OFF + 2 * SGU_WIDTH
IN_COLS = CONV_OFF + 2 * CONV_WIDTH

kernel_name = 'hybrid_mlstm_sgu_conv_dit_block'


def rmsnorm(t, g):
    t32 = t.astype(jnp.float32)
    y = t32 * lax.rsqrt(jnp.mean(t32 * t32, axis=-1, keepdims=True) + EPS)
    return (y * g).astype(t.dtype)


def layernorm(t, g, b):
    t32 = t.astype(jnp.float32)
    mu = jnp.mean(t32, axis=-1, keepdims=True)
    d = t32 - mu
    var = jnp.mean(d * d, axis=-1, keepdims=True)
    return (d * lax.rsqrt(var + EPS) * g + b).astype(t.dtype)


def swiglu(h, w_gu, w_down):
    gate, up = jnp.split(h @ w_gu, 2, axis=-1)
    return (jax.nn.silu(gate) * up) @ w_down


def mlstm_inputs(p):
    B, L, _ = p.shape
    def heads(t):
        return t.reshape(B, L, MLSTM_HEADS, MLSTM_HEAD_DIM).transpose(0, 2, 1, 3).astype(jnp.float32)
    q = heads(p[..., Q_OFF:K_OFF])
    k = heads(p[..., K_OFF:V_OFF]) * (MLSTM_HEAD_DIM ** -0.5)
    v = heads(p[..., V_OFF:O_OFF])
    g = p[..., GATE_OFF:SGU_OFF].astype(jnp.float32).reshape(B, L, 4, MLSTM_HEADS).transpose(2, 0, 3, 1)
    gates = (g[0], jax.nn.log_sigmoid(g[1]), g[2], jax.nn.log_sigmoid(g[3]))
    return q, k, v, gates


def zero_state(batch):
    return (jnp.zeros((batch, MLSTM_HEADS, MLSTM_HEAD_DIM, MLSTM_HEAD_DIM), jnp.float32),
            jnp.zeros((batch, MLSTM_HEADS, MLSTM_HEAD_DIM), jnp.float32),
            jnp.zeros((batch, MLSTM_HEADS), jnp.float32))


def mlstm_scan(q, k, v, logi, logf, state, with_output):
    B, H, L, d = q.shape
    nc = L // MLSTM_CHUNK
    def to_chunks(t):
        t = t.reshape((B, H, nc, MLSTM_CHUNK) + t.shape[3:])
        return jnp.moveaxis(t, 2, 0)
    xs = tuple(to_chunks(t) for t in (q, k, v, logi, logf))
    order = jnp.tril(jnp.ones((MLSTM_CHUNK, MLSTM_CHUNK), bool))

    def step(carry, inp):
        C, n, m = carry
        qc, kc, vc, ic, fc = inp
        b = jnp.cumsum(fc, axis=-1)
        b_last = b[..., -1]
        w_log = b_last[..., None] - b + ic
        m_new = jnp.maximum(b_last + m, jnp.max(w_log, axis=-1))
        decay = jnp.exp(b_last + m - m_new)
        ks = kc * jnp.exp(w_log - m_new[..., None])[..., None]
        C_new = decay[..., None, None] * C + jnp.einsum('bhsk,bhsv->bhkv', ks, vc)
        n_new = decay[..., None] * n + jnp.sum(ks, axis=2)
        if not with_output:
            return (C_new, n_new, m_new), None
        a = b + m[..., None]
        Dlog = b[..., :, None] - b[..., None, :] + ic[..., None, :]
        Dlog = jnp.where(order, Dlog, -jnp.inf)
        m_t = jnp.maximum(a, jnp.max(Dlog, axis=-1))
        w_inter = jnp.exp(a - m_t)
        s = jnp.einsum('bhtk,bhsk->bhts', qc, kc) * jnp.exp(Dlog - m_t[..., None])
        num = w_inter[..., None] * jnp.einsum('bhtk,bhkv->bhtv', qc, C) + jnp.einsum('bhts,bhsv->bhtv', s, vc)
        den = w_inter * jnp.einsum('bhtk,bhk->bht', qc, n) + jnp.sum(s, axis=-1)
        h = num / jnp.maximum(jnp.abs(den), jnp.exp(-m_t))[..., None]
        return (C_new, n_new, m_new), h

    state, hs = lax.scan(step, state, xs)
    if not with_output:
        return None, state
    h = jnp.moveaxis(hs, 0, 2).reshape(B, H, L, d)
    return h, state


def _flip(t):
    return jnp.flip(t, axis=2)


def mlstm_bidir(q, k, v, gates, init_f, init_b, with_output):
    logi_f, logf_f, logi_b, logf_b = gates
    h_f, st_f = mlstm_scan(q, k, v, logi_f, logf_f, init_f, with_output)
    h_b, st_b = mlstm_scan(_flip(q), _flip(k), _flip(v), _flip(logi_b), _flip(logf_b), init_b, with_output)
    h = h_f + _flip(h_b) if with_output else None
    return h, st_f, st_b


def mlstm_post(h, o_pre, g):
    B, H, L, d = h.shape
    h = h * lax.rsqrt(jnp.mean(h * h, axis=-1, keepdims=True) + EPS)
    h = h.transpose(0, 2, 1, 3).reshape(B, L, H * d) * g
    return (h * jax.nn.sigmoid(o_pre.astype(jnp.float32))).astype(o_pre.dtype)


def chunk_sgu(z, ln_g, ln_b, w_s, b_s):
    z = jax.nn.gelu(z)
    u, v = jnp.split(z, 2, axis=-1)
    v = layernorm(v, ln_g, ln_b)
    B, L, C = v.shape
    vg = v.reshape(B, L // SGU_CHUNK, SGU_CHUNK, SGU_GROUPS, C // SGU_GROUPS)
    mixed = jnp.einsum('gpq,bnqgc->bnpgc', w_s, vg) + b_s.T[:, :, None]
    return u * mixed.reshape(B, L, C)


def dwconv1d(y, w, b):
    kw, ch = w.shape
    out = lax.conv_general_dilated(y, w.reshape(kw, 1, ch).astype(y.dtype), window_strides=(1,),
                                   padding=[(kw // 2, kw // 2)], dimension_numbers=('NWC', 'WIO', 'NWC'),
                                   feature_group_count=ch)
    return out + b


def axial_dwconv(y, w, b):
    B, L, C = y.shape
    rows = L // GRID_W
    half = C // 2
    yr = dwconv1d(y[..., :half].reshape(B * rows, GRID_W, half), w[:, :half], b[:half]).reshape(B, L, half)
    yc = y[..., half:].reshape(B, rows, GRID_W, half).transpose(0, 2, 1, 3).reshape(B * GRID_W, rows, half)
    yc = dwconv1d(yc, w[:, half:], b[half:]).reshape(B, GRID_W, rows, half).transpose(0, 2, 1, 3).reshape(B, L, half)
    return jnp.concatenate([yr, yc], axis=-1)


def conformer_conv(z, w_dw, b_dw, ln_g, ln_b, grid):
    a, g = jnp.split(z, 2, axis=-1)
    y = a * jax.nn.sigmoid(g)
    y = axial_dwconv(y, w_dw, b_dw) if grid else dwconv1d(y, w_dw, b_dw)
    return jax.nn.silu(layernorm(y, ln_g, ln_b))


def mix_groups(p, h_mlstm, mlstm_g, sgu_ln_g, sgu_ln_b, sgu_w, sgu_b, conv_w, conv_b, conv_ln_g, conv_ln_b, grid):
    y_a = mlstm_post(h_mlstm, p[..., O_OFF:GATE_OFF], mlstm_g)
    y_b = chunk_sgu(p[..., SGU_OFF:CONV_OFF], sgu_ln_g, sgu_ln_b, sgu_w, sgu_b)
    y_c = conformer_conv(p[..., CONV_OFF:IN_COLS], conv_w, conv_b, conv_ln_g, conv_ln_b, grid)
    return jnp.concatenate([y_a, y_b, y_c], axis=-1)


def setup_inputs(seed: int = 0) -> dict:
    key = jax.random.key(seed)
    ks = jax.random.split(key, 24)
    def nrm(k, shape, scale):
        return jax.random.normal(k, shape, jnp.float32) * scale
    fb = np.zeros((IN_COLS,), np.float32)
    fb[GATE_OFF + MLSTM_HEADS:GATE_OFF + 2 * MLSTM_HEADS] = FORGET_BIAS
    fb[GATE_OFF + 3 * MLSTM_HEADS:GATE_OFF + 4 * MLSTM_HEADS] = FORGET_BIAS
    return {
        'x': nrm(ks[0], (BATCH, SEQ, D_MODEL), 1.0),
        'c': nrm(ks[1], (BATCH, D_MODEL), 1.0),
        'ctx': nrm(ks[2], (BATCH, CTX_LEN, D_MODEL), 1.0),
        'c_ctx': nrm(ks[3], (D_MODEL,), 1.0),
        'w_mod': nrm(ks[4], (DEPTH, D_MODEL, 6 * D_MODEL), 0.5 * D_MODEL ** -0.5),
        'b_mod': nrm(ks[5], (DEPTH, 6 * D_MODEL), 0.02),
        'norm1_g': 1.0 + nrm(ks[6], (DEPTH, D_MODEL), 0.02),
        'w_in': nrm(ks[7], (DEPTH, D_MODEL, IN_COLS), D_MODEL ** -0.5),
        'b_in': nrm(ks[8], (DEPTH, IN_COLS), 0.02) + jnp.asarray(fb),
        'mlstm_g': 1.0 + nrm(ks[9], (DEPTH, MLSTM_WIDTH), 0.02),
        'sgu_ln_g': 1.0 + nrm(ks[10], (DEPTH, SGU_WIDTH), 0.02),
        'sgu_ln_b': nrm(ks[11], (DEPTH, SGU_WIDTH), 0.02),
        'sgu_w': nrm(ks[12], (DEPTH, SGU_GROUPS, SGU_CHUNK, SGU_CHUNK), SGU_CHUNK ** -0.5),
        'sgu_b': nrm(ks[13], (DEPTH, SGU_GROUPS, SGU_CHUNK), 0.02),
        'conv_w': nrm(ks[14], (DEPTH, CONV_K, CONV_WIDTH), CONV_K ** -0.5),
        'conv_b': nrm(ks[15], (DEPTH, CONV_WIDTH), 0.02),
        'conv_ln_g': 1.0 + nrm(ks[16], (DEPTH, CONV_WIDTH), 0.02),
        'conv_ln_b': nrm(ks[17], (DEPTH, CONV_WIDTH), 0.02),
        'w_out': nrm(ks[18], (DEPTH, MIX_WIDTH, D_MODEL), MIX_WIDTH ** -0.5),
        'norm2_g': 1.0 + nrm(ks[19], (DEPTH, D_MODEL), 0.02),
        'w_gu': nrm(ks[20], (DEPTH, D_MODEL, 2 * FFN_HIDDEN), D_MODEL ** -0.5),
        'w_down': nrm(ks[21], (DEPTH, FFN_HIDDEN, D_MODEL), FFN_HIDDEN ** -0.5),
        'final_g': 1.0 + nrm(ks[22], (D_MODEL,), 0.02),
    }


def reference(x, c, ctx, c_ctx, w_mod, b_mod, norm1_g, w_in, b_in, mlstm_g, sgu_ln_g, sgu_ln_b, sgu_w, sgu_b,
              conv_w, conv_b, conv_ln_g, conv_ln_b, w_out, norm2_g, w_gu, w_down, final_g):
    batch = x.shape[0]
    c_act = jax.nn.silu(c)
    cc_act = jax.nn.silu(c_ctx)
    xc = ctx
    for l in range(DEPTH):
        last = l == DEPTH - 1
        mod = (c_act @ w_mod[l] + b_mod[l])[:, None, :]
        modc = cc_act @ w_mod[l] + b_mod[l]
        sh1, sc1, g1, sh2, sc2, g2 = jnp.split(mod, 6, axis=-1)
        shc1, scc1, gc1, shc2, scc2, gc2 = jnp.split(modc, 6, axis=-1)

        px = (rmsnorm(x, norm1_g[l]) * (1.0 + sc1) + sh1) @ w_in[l] + b_in[l]
        pc = (rmsnorm(xc, norm1_g[l]) * (1.0 + scc1) + shc1) @ w_in[l] + b_in[l]
        qx, kx, vx, gtx = mlstm_inputs(px)
        qc, kc, vc, gtc = mlstm_inputs(pc)
        zs = zero_state(batch)
        h_c, st_f, st_b = mlstm_bidir(qc, kc, vc, gtc, zs, zs, not last)
        h_x, _, _ = mlstm_bidir(qx, kx, vx, gtx, st_f, st_b, True)

        y_x = mix_groups(px, h_x, mlstm_g[l], sgu_ln_g[l], sgu_ln_b[l], sgu_w[l], sgu_b[l],
                         conv_w[l], conv_b[l], conv_ln_g[l], conv_ln_b[l], True)
        x = x + g1 * (y_x @ w_out[l])
        x = x + g2 * swiglu(rmsnorm(x, norm2_g[l]) * (1.0 + sc2) + sh2, w_gu[l], w_down[l])

        if not last:
            y_c = mix_groups(pc, h_c, mlstm_g[l], sgu_ln_g[l], sgu_ln_b[l], sgu_w[l], sgu_b[l],
                             conv_w[l], conv_b[l], conv_ln_g[l], conv_ln_b[l], False)
            xc = xc + gc1 * (y_c @ w_out[l])
            xc = xc + gc2 * swiglu(rmsnorm(xc, norm2_g[l]) * (1.0 + scc2) + shc2, w_gu[l], w_down[l])
    return rmsnorm(x, final_g)
```

```python
import contextlib
import types
import numpy as np
import concourse.bass as bass
import concourse.mybir as mybir
from concourse.bass_utils import run_bass_kernel_spmd

F32 = mybir.dt.float32
BF16 = mybir.dt.bfloat16
AF = mybir.ActivationFunctionType
ALU = mybir.AluOpType

D = 1024
SEQ = 2048
CTX = 256
T = SEQ + CTX
NCH = T // 128
INC = 3088
FH = 2816
EPS = 1e-6
KS = 128 ** -0.5
DBG = {"mods", "norm1", "conv", "sgu", "gates", "heads", "ffn", "h_fwd", "h_out", "h_bwd", "h_wout", "h_ms1", "h_fm", "h_tm", "h_ms2"}


class Op:
    __slots__ = ("eng", "fn", "deps", "is_dma", "stream", "sig", "sigval")

    def __init__(self, eng, fn, is_dma=False, stream=None):
        self.eng = eng
        self.fn = fn
        self.deps = []
        self.is_dma = is_dma
        self.stream = stream
        self.sig = is_dma
        self.sigval = None


def _freeze(fn):
    if fn.__closure__ is None:
        return fn
    cells = []
    for c in fn.__closure__:
        try:
            cells.append(types.CellType(c.cell_contents))
        except ValueError:
            cells.append(c)
    g = types.FunctionType(fn.__code__, fn.__globals__, fn.__name__, fn.__defaults__, tuple(cells))
    g.__kwdefaults__ = fn.__kwdefaults__
    return g


class Prog:
    ENGS = ("pe", "act", "dve", "pool", "sp")

    def __init__(self, nc):
        self.nc = nc
        self.ops = {e: [] for e in self.ENGS}
        self.last_write = {}
        self.readers = {}
        self.streams = {}
        self.alias = {}
        self.bar = None
        self.bar_pending = set()

    def _add_deps(self, op, reads, writes):
        reads = [self.alias.get(k, k) for k in reads]
        writes = [self.alias.get(k, k) for k in writes]
        psb = []
        for k in reads + writes:
            if isinstance(k, tuple) and k and k[0] == "PSB" and k not in psb:
                psb.append(k)
        reads = [k for k in reads if k not in psb]
        writes = [k for k in writes if k not in psb]
        for b in psb:
            w = self.last_write.get(b)
            if w is not None and w is not op and (w.eng != op.eng):
                w.sig = True
                op.deps.append(w)
            self.last_write[b] = op
        deps = []
        for b in reads:
            w = self.last_write.get(b)
            if w is not None:
                deps.append((w, True))
        for b in writes:
            w = self.last_write.get(b)
            if w is not None:
                deps.append((w, True))
            for r in self.readers.get(b, {}).values():
                deps.append((r, False))
        if self.bar is not None and op.eng in self.bar_pending:
            self.bar_pending.discard(op.eng)
            for d in self.bar:
                deps.append((d, True))
        for d, true_dep in deps:
            if d is op:
                continue
            if d.is_dma or op.is_dma or d.eng != op.eng or (true_dep and op.eng != "pe"):
                if not d.is_dma:
                    d.sig = True
                op.deps.append(d)
        for b in reads:
            key = ("dma", id(op)) if op.is_dma else op.eng
            self.readers.setdefault(b, {})[key] = op
        for b in writes:
            self.last_write[b] = op
            self.readers[b] = {}

    def op(self, eng, fn, reads=(), writes=()):
        o = Op(eng, _freeze(fn))
        self._add_deps(o, reads, writes)
        self.ops[eng].append(o)
        return o

    def dma(self, eng, out, in_, reads=(), writes=(), **kw):
        stream = tuple(writes) if writes else tuple(reads)
        o = Op(eng, None, is_dma=True, stream=stream)
        o.fn = lambda e, out=out, in_=in_, kw=kw: e.dma_start(out=out, in_=in_, **kw)
        self._add_deps(o, reads, writes)
        self.ops[eng].append(o)
        self.streams.setdefault(stream, []).append(o)
        o.sigval = 16 * len(self.streams[stream])
        return o

    def barrier(self):
        bar = []
        for e in self.ENGS:
            comp = [o for o in self.ops[e] if not o.is_dma]
            if comp:
                bar.append(comp[-1])
        for lst in self.streams.values():
            bar.append(lst[-1])
        self.bar = bar
        self.bar_pending = set(self.ENGS)

    def emit(self):
        nc = self.nc
        with contextlib.ExitStack() as es:
            esem = {e: es.enter_context(nc.semaphore("s_" + e)) for e in self.ENGS}
            ssem = {}
            for i, k in enumerate(self.streams):
                ssem[k] = es.enter_context(nc.semaphore("d%d" % i))
            for e in self.ENGS:
                c = 0
                for o in self.ops[e]:
                    if o.is_dma:
                        continue
                    if o.sig:
                        c += 1
                        o.sigval = c
            block = es.enter_context(nc.Block())
            engobj = {"pe": block.tensor, "act": block.scalar, "dve": block.vector,
                      "pool": block.gpsimd, "sp": block.sync}

            def make(ename):
                ops = self.ops[ename]

                def body(eng):
                    seen = {}
                    for o in ops:
                        need = {}
                        for d in o.deps:
                            s = ssem[d.stream] if d.is_dma else esem[d.eng]
                            key = id(s)
                            if seen.get(key, 0) >= d.sigval:
                                continue
                            if key not in need or need[key][1] < d.sigval:
                                need[key] = (s, d.sigval)
                        for key, (s, v) in need.items():
                            eng.wait_ge(s, v)
                            seen[key] = v
                        ins = o.fn(eng)
                        if o.is_dma:
                            ins.then_inc(ssem[o.stream], 16)
                        elif o.sig:
                            ins.then_inc(esem[ename], 1)
                    if ename == "sp":
                        for k, lst in self.streams.items():
                            eng.wait_ge(ssem[k], 16 * len(lst))
                return body

            for e in self.ENGS:
                engobj[e](make(e))


def build_nc(nseq=2, depth=2):
    nc = bass.Bass("TRN2", target_bir_lowering=False)
    L = depth

    def din(name, shape):
        return nc.dram_tensor(name, shape, F32, kind="ExternalInput").ap()
    x_d = din("x", [nseq, SEQ, D])
    c_d = din("c", [nseq, D])
    ctx_d = din("ctx", [nseq, CTX, D])
    cctx_d = din("c_ctx", [D])
    w_mod = din("w_mod", [L, D, 6 * D])
    b_mod = din("b_mod", [L, 6 * D])
    norm1_g = din("norm1_g", [L, D])
    w_in = din("w_in", [L, D, INC])
    b_in = din("b_in", [L, INC])
    mlstm_g = din("mlstm_g", [L, 512])
    sgu_ln_g = din("sgu_ln_g", [L, 256])
    sgu_ln_b = din("sgu_ln_b", [L, 256])
    sgu_w = din("sgu_w", [L, 4, 128, 128])
    sgu_b = din("sgu_b", [L, 4, 128])
    conv_w = din("conv_w", [L, 31, 256])
    conv_b = din("conv_b", [L, 256])
    conv_ln_g = din("conv_ln_g", [L, 256])
    conv_ln_b = din("conv_ln_b", [L, 256])
    w_out = din("w_out", [L, D, D])
    norm2_g = din("norm2_g", [L, D])
    w_gu = din("w_gu", [L, D, 2 * FH])
    w_down = din("w_down", [L, FH, D])
    final_g = din("final_g", [D])
    out_d = nc.dram_tensor("out", [nseq, SEQ, D], F32, kind="ExternalOutput").ap()

    es = contextlib.ExitStack()
    with es:
        def TS(name, shape, dt):
            return es.enter_context(nc.sbuf_tensor(name, shape, dt))

        def PS(name, shape, dt):
            return es.enter_context(nc.psum_tensor(name, shape, dt))

        p = Prog(nc)

        xT = TS("xT", [128, 8, T], F32)
        hT = TS("hT", [128, 8, T], BF16)
        WSL = [TS("wsl%d" % i, [128, 4224], BF16) for i in range(3)]
        ARENA_B = 45056
        arena = TS("arena", [128, ARENA_B // 2], BF16)
        identf = TS("identf", [128, 128], F32)
        identb = TS("identb", [128, 128], BF16)
        maskU = TS("maskU", [128, 128], F32)
        maskL = TS("maskL", [128, 128], F32)
        onesf = TS("onesf", [128, 128], F32)
        onesb = TS("onesb", [128, 128], BF16)
        onesZ = TS("onesZ", [128, 2, 128], BF16)
        NCT = ((nseq + 1) * 8 + 127) // 128
        NSTG = NCT + 2 * L
        pT = TS("pT", [128, NSTG * 128], F32)
        cact = TS("cact", [128, 8, 2], BF16)
        modT = TS("modT", [128, L * 48, 2], F32)
        A1 = TS("A1", [128, L * 8, 2], F32)
        A2 = TS("A2", [128, L * 8, 2], F32)
        bks = TS("bks", [128, L * 4], F32)
        browt = [TS("brow%d" % i, [1, 256], BF16) for i in range(2)]
        browf = TS("browf", [1, 256], F32)
        gbias = TS("gbias", [128, L, 16], F32)
        gates = TS("gates", [128, NCH, 16], F32)
        SP = TS("SP", [128, 2, NCH, 4], F32)
        EB = TS("EB", [128, 2, NCH, 4], F32)
        ETOT = TS("ETOT", [128, 2, NCH, 4], F32)
        WS = TS("WS", [128, 2, NCH, 4], F32)
        WsT = TS("WsT", [128, 4, 128], BF16)
        sb_bc = TS("sb_bc", [128, 2, 128], F32)
        rstd = TS("rstd", [128, 512], F32)
        tmpA = [TS("tmpA%d" % i, [128, 512], F32) for i in range(2)]
        tmpB = [TS("tmpB%d" % i, [128, 512], F32) for i in range(2)]
        sq = [TS("sq%d" % i, [128, 512], BF16) for i in range(2)]
        sm = TS("sm", [128, 32], F32)

        def av(off_b, shape, dt):
            esz = 2 if dt == BF16 else 4
            n = int(np.prod(shape))
            assert off_b % 4 == 0 and off_b + n * esz <= ARENA_B, (off_b, shape)
            a = arena[:, off_b // 2: off_b // 2 + n * esz // 2]
            if dt == F32:
                a = a.bitcast(F32)
            if len(shape) == 2:
                a = a.rearrange("p (a b) -> p a b", a=shape[0])
            elif len(shape) == 3:
                a = a.rearrange("p (a b c) -> p a b c", a=shape[0], b=shape[1])
            return a
        qT = av(0, [T], BF16)
        kT = av(4608, [T], BF16)
        ktm = av(9216, [NCH, 128], BF16)
        vaug_full = av(13824, [NCH * 130], BF16)
        vaug = av(13824, [NCH, 130], BF16)[:, :, 0:129]
        Dfb = av(18504, [NCH, 130], BF16)[:, :, 0:129]
        sigoT = av(23184, [T], BF16)
        yaT = av(27792, [T], BF16)
        ypad0 = av(0, [32, 94], BF16)
        ypad1 = av(6016, [62, 64], BF16)
        yctx = av(13952, [2, 286], BF16)
        ycT = av(15096, [2, T], BF16)
        diag = av(24312, [62, 128], BF16)
        uT = av(0, [2, T], BF16)
        ybT = av(9216, [2, T], BF16)
        actT = av(0, [22, 1024], BF16)
        xin = [av(i * 4096, [D], F32) for i in range(2)]
        xo = [av(8192 + i * 4096, [D], F32) for i in range(2)]
        xo2 = [av(16384 + i * 4096, [D], F32) for i in range(2)]
        fgb = av(24576, [D], F32)
        _o = 32768
        Dst = [av(_o + i * 520, [129], F32) for i in range(2)]
        Dbb = av(_o + 1040, [130], BF16)[:, 0:129]
        kk = [av(_o + 1304 + i * 256, [128], BF16) for i in range(2)]
        Aft = [av(_o + 1816 + i * 256, [128], BF16) for i in range(2)]
        Abt = [av(_o + 2328 + i * 256, [128], BF16) for i in range(2)]
        hsum = av(_o + 2840, [128], F32)
        hn = [av(_o + 3352 + i * 256, [128], BF16) for i in range(2)]
        vg = av(18432, [256], F32)
        vhatZ = av(19456, [4, 128], BF16)
        cvb = [av(40192 + i * 1024, [512], BF16) for i in range(2)]
        WSE = SP
        ADD = sb_bc
        wsw = tmpA[0][:, 0:128]
        junk = tmpA[1][:, 0:256]
        stg = av(0, [NSTG, 128], F32)

        MM = [PS("mm%d" % i, [128, 512], F32) for i in range(3)]
        ST0 = PS("st0", [128, 512], F32)
        AUX = [PS("aux%d" % i, [128, 512], F32) for i in range(3)]
        TP = PS("tp0", [128, 1024], BF16)
        mm_i = [0]
        for i_ in range(3):
            p.alias[("mm", i_)] = ("PSB", "mm", i_)
        p.alias["st0"] = ("PSB", "st0")
        p.alias["aux0"] = ("PSB", "aux0")
        p.alias[("aux0s", 0)] = ("PSB", "aux0")
        p.alias[("aux0s", 1)] = ("PSB", "aux0")
        p.alias["aux1"] = ("PSB", "aux1")
        p.alias[("auxo", 0)] = ("PSB", "aux1")
        p.alias[("auxob", 0)] = ("PSB", "aux1")
        p.alias[("auxo", 1)] = ("PSB", "aux2")
        p.alias[("auxob", 1)] = ("PSB", "aux2")
        p.alias[("aux2", 0)] = ("PSB", "aux2")
        p.alias[("aux2", 1)] = ("PSB", "aux2")
        p.alias[("tp", 0)] = ("PSB", "tp")
        p.alias[("tp", 1)] = ("PSB", "tp")

        def next_mm():
            i = mm_i[0] % 3
            mm_i[0] += 1
            return MM[i], ("mm", i)

        p.op("pool", lambda e: e.memset(identf[:], 0.0), writes=["identf"])
        p.op("pool", lambda e: e.affine_select(out=identf[:], in_=identf[:], compare_op=ALU.not_equal, fill=1.0,
                                               base=0, pattern=[[-1, 128]], channel_multiplier=1),
             reads=["identf"], writes=["identf"])
        p.op("pool", lambda e: e.tensor_copy(identb[:], identf[:]), reads=["identf"], writes=["identb"])
        p.op("pool", lambda e: e.memset(maskU[:], 1.0), writes=["maskU"])
        p.op("pool", lambda e: e.affine_select(out=maskU[:], in_=maskU[:], compare_op=ALU.is_ge, fill=0.0,
                                               base=0, pattern=[[1, 128]], channel_multiplier=-1),
             reads=["maskU"], writes=["maskU"])
        p.op("pool", lambda e: e.memset(maskL[:], 1.0), writes=["maskL"])
        p.op("pool", lambda e: e.affine_select(out=maskL[:], in_=maskL[:], compare_op=ALU.is_ge, fill=0.0,
                                               base=0, pattern=[[-1, 128]], channel_multiplier=1),
             reads=["maskL"], writes=["maskL"])
        p.op("pool", lambda e: e.memset(onesf[:], 1.0), writes=["onesf"])
        p.op("pool", lambda e: e.memset(onesb[:], 1.0), writes=["onesb"])
        p.op("pool", lambda e: e.memset(onesZ[:], 0.0), writes=["onesZ"])
        p.op("pool", lambda e: e.memset(onesZ[:, 0, 0:64], 1.0), reads=["onesZ"], writes=["onesZ"])
        p.op("pool", lambda e: e.memset(onesZ[:, 1, 64:128], 1.0), reads=["onesZ"], writes=["onesZ"])
        p.op("pool", lambda e: e.memset(stg, 0.0), writes=["stg"])

        pcol = {}
        rowptr = [0]

        def prow(key, src, nrows):
            r0 = rowptr[0]
            if r0 // 128 != (r0 + nrows - 1) // 128:
                r0 = (r0 // 128 + 1) * 128
            t, r = divmod(r0, 128)
            p.dma("sp", stg[r:r + nrows, t, :], src, writes=["stg"])
            pcol[key] = r0
            rowptr[0] = r0 + nrows

        for s in range(nseq):
            prow(("c", s), c_d[s].rearrange("(r q) -> r q", q=128), 8)
        prow("cctx", cctx_d.rearrange("(r q) -> r q", q=128), 8)
        rowptr[0] = NCT * 128
        for l in range(L):
            rv = lambda v, a, b: v[l, a:b].rearrange("(r q) -> r q", q=128)
            prow((l, "n1g"), rv(norm1_g, 0, D), 8)
            prow((l, "bq"), rv(b_in, 0, 512), 4)
            prow((l, "bk"), rv(b_in, 512, 1024), 4)
            prow((l, "bo"), rv(b_in, 1536, 2048), 4)
            prow((l, "bsu"), rv(b_in, 2064, 2320), 2)
            prow((l, "bcv"), rv(b_in, 2576, 3088), 4)
            prow((l, "mg"), rv(mlstm_g, 0, 512), 4)
            prow((l, "slg"), rv(sgu_ln_g, 0, 256), 2)
            prow((l, "slb"), rv(sgu_ln_b, 0, 256), 2)
            prow((l, "cw"), conv_w[l].rearrange("k (t q) -> (k t) q", q=128), 62)
            prow((l, "cb"), rv(conv_b, 0, 256), 2)
            prow((l, "clg"), rv(conv_ln_g, 0, 256), 2)
            prow((l, "clb"), rv(conv_ln_b, 0, 256), 2)
            prow((l, "n2g"), rv(norm2_g, 0, D), 8)
            prow((l, "bmod"), rv(b_mod, 0, 6 * D), 48)
            rowptr[0] = (NCT + 2 * (l + 1)) * 128
        for t in range(NSTG):
            ps, pk = next_mm()
            p.op("pe", lambda e, t=t, ps=ps: e.transpose(ps[:, 0:128], stg[:, t, :], identf[:]),
                 reads=["stg", "identf"], writes=[pk])
            p.op("dve", lambda e, t=t, ps=ps: e.tensor_copy(pT[:, t * 128:(t + 1) * 128], ps[:, 0:128]),
                 reads=[pk], writes=["pT"])

        def pc(key, j=0, n=1):
            c0 = pcol[key] + j
            return pT[:, c0:c0 + n]

        for l in range(L):
            p.dma("sp", gbias[:, l, :], b_in[l, 2048:2064].partition_broadcast(128), writes=["gbias"])
        brow_i = [0]

        def load_brow(l, blocks):
            i = brow_i[0] % 2
            brow_i[0] += 1
            o = 0
            for c0, n in blocks:
                p.dma("sp", browf[0:1, o:o + n], b_in[l:l + 1, c0:c0 + n], writes=["browf"])
                o += n
            p.op("dve", lambda e: e.tensor_copy(browt[i][0:1, :], browf[0:1, :]), reads=["browf"], writes=[("brow", i)])
            return i

        wl_i = [0]

        def wload(parts):
            i = wl_i[0] % 3
            wl_i[0] += 1
            for dst_fn, src in parts:
                p.dma("pool", dst_fn(WSL[i]), src, writes=[("W", i)])
            return i

        def wview(i, a, b):
            return WSL[i][:, 0:a * b].rearrange("p (a b) -> p a b", a=a)

        def cols_src(w, l, c0, n):
            return w[l, :, c0:c0 + n].rearrange("(k q) n -> q k n", q=128)

        def wload_cols(w, l, blocks):
            tot = sum(n for _, n in blocks)
            parts = []
            o = 0
            for c0, n in blocks:
                parts.append((lambda t, o=o, n=n, tot=tot: t[:, 0:8 * tot].rearrange("p (k n) -> p k n", k=8)[:, :, o:o + n],
                              cols_src(w, l, c0, n)))
                o += n
            return wload(parts), tot

        def run_tasks(tasks, ahead=2):
            loaded = []
            nxt = 0
            for i in range(len(tasks)):
                while nxt < len(tasks) and nxt <= i + ahead - 1:
                    loaded.append(tasks[nxt][0]())
                    nxt += 1
                tasks[i][1](loaded[i])

        TT_ALL = [(0, 256, True)] + [(256 + 512 * i, 512, False) for i in range(4)]

        def mm_group(ps_ap, pairs, reads, wkey):
            def fn(e):
                n = len(pairs)
                for i, (a, b) in enumerate(pairs):
                    ins = e.matmul(ps_ap, lhsT=a, rhs=b, start=(i == 0), stop=(i == n - 1))
                return ins
            p.op("pe", fn, reads=reads, writes=[wkey])

        def norm_to_hT(s, l, A, shj, tiles):
            for (c0, n, is_ctx) in tiles:
                w = 1 if is_ctx else 0
                for k in range(8):
                    p.op("act", lambda e, k=k: e.activation(out=sq[k % 2][:, 0:n], in_=xT[:, k, c0:c0 + n], func=AF.Square),
                         reads=[("xT", k, c0)], writes=[("sq", k % 2)])
                    p.op("pe", lambda e, k=k: e.matmul(ST0[:, 0:n], lhsT=onesb[:], rhs=sq[k % 2][:, 0:n], start=(k == 0), stop=(k == 7)),
                         reads=[("sq", k % 2), "onesb"], writes=["st0"])
                p.op("act", lambda e: e.activation(out=rstd[:, 0:n], in_=ST0[:, 0:n], func=AF.Sqrt, bias=D * EPS, scale=1.0),
                     reads=["st0"], writes=["rstd"])
                p.op("dve", lambda e: e.reciprocal(out=rstd[:, 0:n], in_=rstd[:, 0:n]), reads=["rstd"], writes=["rstd"])
                for k in range(8):
                    tb = tmpA[k % 2]
                    p.op("dve", lambda e, k=k, tb=tb: e.scalar_tensor_tensor(out=tb[:, 0:n], in0=xT[:, k, c0:c0 + n], scalar=A[:, l * 8 + k, w:w + 1],
                                                                              in1=rstd[:, 0:n], op0=ALU.mult, op1=ALU.mult),
                         reads=[("xT", k, c0), "rstd", "A"], writes=[("tmpA", k % 2)])
                    p.op("act", lambda e, k=k, tb=tb: e.activation(out=hT[:, k, c0:c0 + n], in_=tb[:, 0:n], func=AF.Identity,
                                                                    bias=modT[:, l * 48 + shj + k, w:w + 1], scale=1.0),
                         reads=[("tmpA", k % 2), "modT"], writes=[("hT", c0)])

        def resid_update(s, l, gj, ps, pk, m, c0, n, is_ctx):
            w = 1 if is_ctx else 0
            p.op("dve", lambda e: e.scalar_tensor_tensor(out=xT[:, m, c0:c0 + n], in0=ps[:, 0:n], scalar=modT[:, l * 48 + gj + m, w:w + 1],
                                                         in1=xT[:, m, c0:c0 + n], op0=ALU.mult, op1=ALU.add),
                 reads=[pk, "modT", ("xT", m, c0)], writes=[("xT", m, c0)])

        def wout_part(s, l, r0, nkt, yfn, ykeys, tiles):
            def load():
                return wload([(lambda t: t[:, 0:nkt * 1024].rearrange("p (k n) -> p k n", k=nkt),
                               w_out[l, r0:r0 + 128 * nkt, :].rearrange("(k q) n -> q k n", q=128))])

            def comp(i):
                wv = WSL[i][:, 0:nkt * 1024].rearrange("p (k n) -> p k n", k=nkt)
                for (c0, n, is_ctx) in tiles:
                    for m in range(8):
                        ps, pk = next_mm()
                        mm_group(ps[:, 0:n], [(wv[:, kt, m * 128:(m + 1) * 128], yfn(kt, c0, n)) for kt in range(nkt)],
                                 [("W", i)] + ykeys, pk)
                        resid_update(s, l, 16, ps, pk, m, c0, n, is_ctx)
            return (load, comp)

        for s in range(nseq):
            p.barrier()
            for blk in range(NCH):
                xi = xin[blk % 2]
                src = ctx_d[s, blk * 128:(blk + 1) * 128, :] if blk < 2 else x_d[s, (blk - 2) * 128:(blk - 1) * 128, :]
                p.dma("sp", xi, src, writes=[("xin", blk % 2)])
                for half in range(2):
                    ps, pk = next_mm()

                    def tr(e, ps=ps, xi=xi, half=half):
                        for j in range(4):
                            k = half * 4 + j
                            ins = e.transpose(ps[:, j * 128:(j + 1) * 128], xi[:, k * 128:(k + 1) * 128], identf[:])
                        return ins
                    p.op("pe", tr, reads=[("xin", blk % 2), "identf"], writes=[pk])
                    eng = "act" if half == 0 else "dve"
                    outv = xT[:, half * 4:half * 4 + 4, blk * 128:(blk + 1) * 128]
                    inv = ps[:].rearrange("p (j n) -> p j n", j=4)
                    c0t = 0 if blk < 2 else 256 + ((blk - 2) // 4) * 512
                    wk = [("xT", half * 4 + j, c0t) for j in range(4)]
                    if eng == "act":
                        p.op("act", lambda e, outv=outv, inv=inv: e.copy(outv, inv), reads=[pk], writes=wk)
                    else:
                        p.op("dve", lambda e, outv=outv, inv=inv: e.tensor_copy(outv, inv), reads=[pk], writes=wk)
            p.barrier()
            for j, key in enumerate((("c", s), "cctx")):
                p.op("act", lambda e, j=j, key=key: e.activation(out=cact[:, :, j], in_=pc(key, 0, 8), func=AF.Silu),
                     reads=["pT"], writes=["cact"])
            for l in range(L):
                if s == 0 or True:
                    tasks = []
                    for g in range(12):
                        def load(g=g, l=l):
                            return wload_cols(w_mod, l, [(g * 512, 512)])

                        def comp(info, g=g, l=l):
                            i, tot = info
                            wv = wview(i, 8, 512)
                            for m in range(4):
                                j = g * 4 + m
                                mm_group(AUX[0][:, 0:2], [(wv[:, k, m * 128:(m + 1) * 128], cact[:, k, :]) for k in range(8)],
                                         [("W", i), "cact"], "aux0")
                                p.op("act", lambda e, j=j, l=l: e.activation(out=modT[:, l * 48 + j, :], in_=AUX[0][:, 0:2], func=AF.Identity,
                                                                        bias=pc((l, "bmod"), j), scale=1.0),
                                     reads=["aux0", "pT"], writes=["modT"])
                        tasks.append((load, comp))
                    run_tasks(tasks)
                    for (Atile, gkey, scj) in ((A1, "n1g", 8), (A2, "n2g", 32)):
                        p.op("dve", lambda e, Atile=Atile, scj=scj, l=l: e.tensor_scalar(out=Atile[:, l * 8:(l + 1) * 8, :], in0=modT[:, l * 48 + scj:l * 48 + scj + 8, :], scalar1=1.0, scalar2=32.0,
                                                                                   op0=ALU.add, op1=ALU.mult),
                             reads=["modT"], writes=["A"])
                        p.op("dve", lambda e, Atile=Atile, gkey=gkey, l=l: e.tensor_tensor(out=Atile[:, l * 8:(l + 1) * 8, :], in0=Atile[:, l * 8:(l + 1) * 8, :],
                                                                                     in1=pc((l, gkey), 0, 8).unsqueeze(2).broadcast_to([128, 8, 2]),
                                                                                     op=ALU.mult),
                             reads=["A", "pT"], writes=["A"])
                    p.op("dve", lambda e, l=l: e.tensor_scalar(out=bks[:, l * 4:l * 4 + 4], in0=pc((l, "bk"), 0, 4), scalar1=KS, scalar2=None, op0=ALU.mult),
                         reads=["pT"], writes=["bks"])


            for l in range(L):
                last = (l == L - 1)
                FM_T = TT_ALL[1:] if last else TT_ALL
                fm_chunks = range(2, NCH) if last else range(NCH)

                if "norm1" in DBG:
                    norm_to_hT(s, l, A1, 0, TT_ALL)

                p.barrier()
                p.op("pool", lambda e: e.memset(ypad0[:], 0.0), writes=["ypad0"])
                p.op("pool", lambda e: e.memset(ypad1[:], 0.0), writes=["ypad1"])
                p.op("pool", lambda e: e.memset(yctx[:], 0.0), writes=["yctx"])
                for j in range(62):
                    p.op("pool" if j % 2 else "dve", lambda e, j=j: e.tensor_scalar(out=diag[:, j, :], in0=identb[:], scalar1=pc((l, "cw"), j), scalar2=None, op0=ALU.mult),
                         reads=["identb", "pT"], writes=[("diag", j)])

                def conv_load():
                    return wload_cols(w_in, l, [(2576, 512)])

                def conv_comp(info):
                    i, tot = info
                    wv = wview(i, 8, 512)
                    for ti, (c0, n, is_ctx) in enumerate(FM_T):
                        for ct in range(2):
                            psa, pka = next_mm()
                            mm_group(psa[:, 0:n], [(wv[:, k, ct * 128:(ct + 1) * 128], hT[:, k, c0:c0 + n]) for k in range(8)],
                                     [("W", i), ("hT", c0)], pka)
                            psg, pkg = next_mm()
                            mm_group(psg[:, 0:n], [(wv[:, k, 256 + ct * 128:256 + (ct + 1) * 128], hT[:, k, c0:c0 + n]) for k in range(8)],
                                     [("W", i), ("hT", c0)], pkg)
                            tb = tmpB[ct]
                            p.op("act", lambda e, tb=tb, psg=psg, ct=ct: e.activation(out=tb[:, 0:n], in_=psg[:, 0:n], func=AF.Sigmoid,
                                                                                     bias=pc((l, "bcv"), 2 + ct), scale=1.0),
                                 reads=[pkg, "pT"], writes=[("tmpB", ct)])
                            if is_ctx:
                                outv = yctx[:, ct, 15:15 + 256]
                                in0 = psa[:, 0:n]
                                in1 = tb[:, 0:n]
                                wk = "yctx"
                            elif ct == 0:
                                r0 = (c0 - 256) // 64
                                outv = ypad0[:, r0:r0 + 8, 15:79]
                                in0 = psa[:].rearrange("p (r c) -> p r c", c=64)
                                in1 = tb[:].rearrange("p (r c) -> p r c", c=64)
                                wk = "ypad0"
                            else:
                                r0 = (c0 - 256) // 64
                                outv = ypad1[:, 15 + r0:15 + r0 + 8, :]
                                in0 = psa[:].rearrange("p (r c) -> p r c", c=64)
                                in1 = tb[:].rearrange("p (r c) -> p r c", c=64)
                                wk = "ypad1"
                            p.op("dve", lambda e, outv=outv, in0=in0, in1=in1, ct=ct: e.scalar_tensor_tensor(
                                out=outv, in0=in0, scalar=pc((l, "bcv"), ct), in1=in1, op0=ALU.add, op1=ALU.mult),
                                reads=[pka, ("tmpB", ct), "pT", wk], writes=[wk])
                    for ti, (c0, n, is_ctx) in enumerate(FM_T):
                        pss = []
                        for ct in range(2):
                            ps, pk = next_mm()
                            pairs = []
                            for k in range(31):
                                if is_ctx:
                                    rhs = yctx[:, ct, k:k + 256]
                                elif ct == 0:
                                    r0 = (c0 - 256) // 64
                                    rhs = ypad0[:, r0:r0 + 8, k:k + 64]
                                else:
                                    r0 = (c0 - 256) // 64
                                    rhs = ypad1[:, r0 + k:r0 + k + 8, :]
                                pairs.append((diag[:, 2 * k + ct, :], rhs))
                            mm_group(ps[:, 0:n], pairs, ["yctx", "ypad0", "ypad1"] + [("diag", 2 * k + ct) for k in range(31)], pk)
                            pss.append((ps, pk))
                        for ct in range(2):
                            ps, pk = pss[ct]
                            p.op("act", lambda e, ps=ps, ct=ct: e.activation(out=cvb[ct][:, 0:n], in_=ps[:, 0:n], func=AF.Identity,
                                                                             bias=pc((l, "cb"), ct), scale=1.0),
                                 reads=[pk, "pT"], writes=[("cvb", ct)])
                            p.op("act", lambda e, ps=ps, ct=ct: e.activation(out=sq[ct][:, 0:n], in_=ps[:, 0:n], func=AF.Square,
                                                                             bias=pc((l, "cb"), ct), scale=1.0),
                                 reads=[pk, "pT"], writes=[("sq", ct)])
                        mm_group(ST0[:, 0:n], [(onesb[:], cvb[ct][:, 0:n]) for ct in range(2)], [("cvb", 0), ("cvb", 1), "onesb"], "st0")
                        mm_group(AUX[0][:, 0:n], [(onesb[:], sq[ct][:, 0:n]) for ct in range(2)], [("sq", 0), ("sq", 1), "onesb"], "aux0")
                        mean, msq = tmpA[0], tmpA[1]
                        p.op("dve", lambda e: e.tensor_scalar(out=mean[:, 0:n], in0=ST0[:, 0:n], scalar1=1.0 / 256, scalar2=None, op0=ALU.mult),
                             reads=["st0"], writes=[("tmpA", 0)])
                        p.op("dve", lambda e: e.tensor_tensor(out=msq[:, 0:n], in0=mean[:, 0:n], in1=mean[:, 0:n], op=ALU.mult),
                             reads=[("tmpA", 0)], writes=[("tmpA", 1)])
                        p.op("dve", lambda e: e.scalar_tensor_tensor(out=msq[:, 0:n], in0=AUX[0][:, 0:n], scalar=1.0 / 256, in1=msq[:, 0:n],
                                                                     op0=ALU.mult, op1=ALU.subtract),
                             reads=["aux0", ("tmpA", 1)], writes=[("tmpA", 1)])
                        p.op("act", lambda e: e.activation(out=rstd[:, 0:n], in_=msq[:, 0:n], func=AF.Sqrt, bias=EPS, scale=1.0),
                             reads=[("tmpA", 1)], writes=["rstd"])
                        p.op("dve", lambda e: e.reciprocal(out=rstd[:, 0:n], in_=rstd[:, 0:n]), reads=["rstd"], writes=["rstd"])
                        for ct in range(2):
                            tb = tmpB[ct]
                            p.op("dve", lambda e, tb=tb, ct=ct: e.tensor_tensor(out=tb[:, 0:n], in0=cvb[ct][:, 0:n], in1=mean[:, 0:n], op=ALU.subtract),
                                 reads=[("cvb", ct), ("tmpA", 0)], writes=[("tmpB", ct)])
                            p.op("dve", lambda e, tb=tb: e.tensor_tensor(out=tb[:, 0:n], in0=tb[:, 0:n], in1=rstd[:, 0:n], op=ALU.mult),
                                 reads=[("tmpB", ct), "rstd"], writes=[("tmpB", ct)])
                            p.op("act", lambda e, tb=tb, ct=ct: e.activation(out=ycT[:, ct, c0:c0 + n], in_=tb[:, 0:n], func=AF.Silu,
                                                                             bias=pc((l, "clb"), ct), scale=pc((l, "clg"), ct)),
                                 reads=[("tmpB", ct), "pT"], writes=[("ycT", c0)])

                if "conv" in DBG:
                  run_tasks([(conv_load, conv_comp),
                           wout_part(s, l, 768, 2, lambda kt, c0, n: ycT[:, kt, c0:c0 + n], [("ycT", t[0]) for t in FM_T], FM_T)])

                p.barrier()
                p.op("pool", lambda e: e.memset(vhatZ, 0.0), writes=["vhatZ"])
                for g in range(4):
                    p.dma("sp", wsw, sgu_w[l, g], writes=[("tmpA", 0)])
                    ps, pk = next_mm()
                    p.op("pe", lambda e, ps=ps: e.transpose(ps[:, 0:128], wsw, identf[:]), reads=[("tmpA", 0), "identf"], writes=[pk])
                    p.op("dve", lambda e, ps=ps, g=g: e.tensor_copy(WsT[:, g, :], ps[:, 0:128]), reads=[pk], writes=["WsT"])
                    p.dma("sp", sb_bc[(g % 2) * 64:(g % 2) * 64 + 64, g // 2, :], sgu_b[l, g, :].partition_broadcast(64), writes=["sb_bc"])
                for j in range(2):
                    mm_group(AUX[0][:, 0:128], [(onesZ[:, e_, :], WsT[:, 2 * j + e_, :]) for e_ in range(2)], ["onesZ", "WsT"], "aux0")
                    p.op("dve", lambda e, j=j: e.scalar_tensor_tensor(out=ADD[:, j, :], in0=AUX[0][:, 0:128], scalar=pc((l, "slb"), j),
                                                                      in1=sb_bc[:, j, :], op0=ALU.mult, op1=ALU.add),
                         reads=["aux0", "pT", "sb_bc"], writes=["sb_bc"])

                def sgu_load():
                    return wload_cols(w_in, l, [(2048, 528)])

                def sgu_comp(info):
                    i, tot = info
                    wv = wview(i, 8, 528)
                    bri = load_brow(l, [(2320, 256)])
                    for (c0, n, is_ctx) in FM_T:
                        for m in range(2):
                            ps, pk = next_mm()
                            mm_group(ps[:, 0:n], [(wv[:, k, 16 + m * 128:16 + (m + 1) * 128], hT[:, k, c0:c0 + n]) for k in range(8)],
                                     [("W", i), ("hT", c0)], pk)
                            p.op("act", lambda e, ps=ps, m=m: e.activation(out=uT[:, m, c0:c0 + n], in_=ps[:, 0:n], func=AF.Gelu,
                                                                           bias=pc((l, "bsu"), m), scale=1.0),
                                 reads=[pk, "pT"], writes=[("uT", c0)])
                    for c in range(NCH):
                        cc = c * 128
                        c0t = 0 if c < 2 else 256 + ((c - 2) // 4) * 512
                        mm_group(AUX[1][:, 0:16], [(hT[:, k, cc:cc + 128], wv[:, k, 0:16]) for k in range(8)], [("W", i), ("hT", c0t)], "aux1")
                        p.op("dve", lambda e, c=c: e.tensor_tensor(out=gates[:, c, :], in0=AUX[1][:, 0:16], in1=gbias[:, l, :], op=ALU.add),
                             reads=["aux1", "gbias"], writes=["gates"])
                        if c not in fm_chunks:
                            continue
                        ps, pk = next_mm()
                        mm_group(ps[:, 0:256], [(hT[:, k, cc:cc + 128], wv[:, k, 272:528]) for k in range(8)] + [(onesb[0:1, :], browt[bri][0:1, 0:256])],
                                 [("W", i), ("hT", c0t), ("brow", bri), "onesb"], pk)
                        p.op("act", lambda e, ps=ps: e.activation(out=vg, in_=ps[:, 0:256], func=AF.Gelu), reads=[pk], writes=["vg"])
                        p.op("act", lambda e: e.activation(out=junk, in_=vg, func=AF.Identity, accum_out=sm[:, 0:1]), reads=["vg"], writes=[("tmpA", 1), ("sm", 0)])
                        p.op("act", lambda e: e.activation(out=junk, in_=vg, func=AF.Square, accum_out=sm[:, 1:2]), reads=["vg", ("tmpA", 1)], writes=[("tmpA", 1), ("sm", 1)])
                        p.op("dve", lambda e: e.tensor_scalar(out=sm[:, 2:3], in0=sm[:, 0:1], scalar1=1.0 / 256, scalar2=None, op0=ALU.mult),
                             reads=[("sm", 0)], writes=[("sm", 2)])
                        p.op("dve", lambda e: e.tensor_tensor(out=sm[:, 3:4], in0=sm[:, 2:3], in1=sm[:, 2:3], op=ALU.mult),
                             reads=[("sm", 2)], writes=[("sm", 3)])
                        p.op("dve", lambda e: e.scalar_tensor_tensor(out=sm[:, 3:4], in0=sm[:, 1:2], scalar=1.0 / 256, in1=sm[:, 3:4], op0=ALU.mult, op1=ALU.subtract),
                             reads=[("sm", 1), ("sm", 3)], writes=[("sm", 3)])
                        p.op("act", lambda e: e.activation(out=sm[:, 4:5], in_=sm[:, 3:4], func=AF.Sqrt, bias=EPS, scale=1.0), reads=[("sm", 3)], writes=[("sm", 4)])
                        p.op("dve", lambda e: e.reciprocal(out=sm[:, 5:6], in_=sm[:, 4:5]), reads=[("sm", 4)], writes=[("sm", 5)])
                        p.op("dve", lambda e: e.scalar_tensor_tensor(out=sm[:, 6:7], in0=sm[:, 2:3], scalar=-1.0, in1=sm[:, 5:6], op0=ALU.mult, op1=ALU.mult),
                             reads=[("sm", 2), ("sm", 5)], writes=[("sm", 6)])
                        zv = vhatZ
                        zap = bass.AP(zv.tensor, zv.offset, [list(zv.ap[0]), [256, 2], [192, 2], [1, 64]])
                        p.op("act", lambda e, zap=zap: e.activation(out=zap, in_=vg.rearrange("p (a b c) -> p a b c", a=2, b=2), func=AF.Identity,
                                                                    scale=sm[:, 5:6], bias=sm[:, 6:7]),
                             reads=["vg", ("sm", 5), ("sm", 6), "vhatZ"], writes=["vhatZ"])
                        for j in range(2):
                            mm_group(AUX[2][:, j * 128:(j + 1) * 128], [(vhatZ[:, 2 * j + e_, :], WsT[:, 2 * j + e_, :]) for e_ in range(2)],
                                     ["vhatZ", "WsT"], ("aux2", j))
                            tb = tmpB[j]
                            p.op("dve", lambda e, j=j, tb=tb: e.scalar_tensor_tensor(out=tb[:, 0:128], in0=AUX[2][:, j * 128:(j + 1) * 128], scalar=pc((l, "slg"), j),
                                                                                    in1=ADD[:, j, :], op0=ALU.mult, op1=ALU.add),
                                 reads=[("aux2", j), "pT", "sb_bc"], writes=[("tmpB", j)])
                            p.op("pool", lambda e, j=j, tb=tb, cc=cc: e.tensor_tensor(out=ybT[:, j, cc:cc + 128], in0=tb[:, 0:128], in1=uT[:, j, cc:cc + 128], op=ALU.mult),
                                 reads=[("tmpB", j), ("uT", c0t)], writes=[("ybT", c0t)])

                if "sgu" in DBG:
                  run_tasks([(sgu_load, sgu_comp),
                           wout_part(s, l, 512, 2, lambda kt, c0, n: ybT[:, kt, c0:c0 + n], [("ybT", t[0]) for t in FM_T], FM_T)])

                gv = gates[:]
                if "gates" not in DBG:
                    continue
                _bf = gates[:, 0, 4:5]
                _bi = gates[:, 0, 0:1]
                gap_f = bass.AP(_bf.tensor, _bf.offset, [list(_bf.ap[0]), [8, 2], [16, NCH], [1, 4]])
                gap_i = bass.AP(_bi.tensor, _bi.offset, [list(_bi.ap[0]), [8, 2], [16, NCH], [1, 4]])
                p.op("act", lambda e: e.activation(out=SP[:], in_=gap_f, func=AF.Exp, scale=-1.0), reads=["gates"], writes=["SP"])
                p.op("act", lambda e: e.activation(out=SP[:], in_=SP[:], func=AF.Ln, bias=1.0, scale=1.0), reads=["SP"], writes=["SP"])
                spf = SP[:].rearrange("p d c h -> p (d c h)")
                p.op("pe", lambda e: e.matmul(AUX[0][:, 0:72], lhsT=maskU[:], rhs=spf[:, 0:72], start=True, stop=True), reads=["SP", "maskU"], writes=["aux0"])
                p.op("pe", lambda e: e.matmul(AUX[0][:, 72:144], lhsT=maskL[:], rhs=spf[:, 72:144], start=True, stop=True), reads=["SP", "maskL"], writes=["aux0"])
                p.op("pe", lambda e: e.matmul(AUX[0][:, 144:288], lhsT=onesf[:], rhs=spf[:, 0:144], start=True, stop=True), reads=["SP", "onesf"], writes=["aux0"])
                flat = lambda t: t[:].rearrange("p d c h -> p (d c h)")
                p.op("act", lambda e: e.activation(out=flat(EB), in_=AUX[0][:, 0:144], func=AF.Exp, scale=-1.0), reads=["aux0"], writes=["EB"])
                p.op("act", lambda e: e.activation(out=flat(ETOT), in_=AUX[0][:, 144:288], func=AF.Exp, scale=-1.0), reads=["aux0"], writes=["ETOT"])
                p.op("dve", lambda e: e.tensor_tensor(out=WS[:], in0=AUX[0][:, 0:144].rearrange("p (d c h) -> p d c h", d=2, c=NCH), in1=gap_i, op=ALU.add),
                     reads=["aux0", "gates"], writes=["WS"])
                p.op("act", lambda e: e.activation(out=flat(WS), in_=flat(WS), func=AF.Exp), reads=["WS"], writes=["WS"])
                p.op("dve", lambda e: e.tensor_tensor(out=flat(WSE), in0=flat(WS), in1=flat(ETOT), op=ALU.mult), reads=["WS", "ETOT", "SP"], writes=["WSE", "SP"])

                for h in range(4):
                    p.barrier()
                    if "h_ms1" in DBG:
                        p.op("pool", lambda e: e.memset(vaug_full, 1.0), writes=["vaug1"] + [("vaug", c_) for c_ in range(NCH)])

                    def head_load(h=h):
                        parts = []
                        for b_, c0_ in enumerate((h * 128, 512 + h * 128, 1024 + h * 128, 1536 + h * 128)):
                            parts.append((lambda t, b_=b_: t[:, b_ * 1024:(b_ + 1) * 1024].rearrange("p (k n) -> p k n", k=8), cols_src(w_in, l, c0_, 128)))
                        return wload(parts), 512

                    def head_comp(info, h=h):
                        i, tot = info
                        wb = [WSL[i][:, b_ * 1024:(b_ + 1) * 1024].rearrange("p (k n) -> p k n", k=8) for b_ in range(4)]
                        for (c0, n, is_ctx) in (FM_T if "h_fm" in DBG else []):
                            ps, pk = next_mm()
                            mm_group(ps[:, 0:n], [(wb[0][:, k, :], hT[:, k, c0:c0 + n]) for k in range(8)], [("W", i), ("hT", c0)], pk)
                            p.op("act", lambda e, ps=ps: e.activation(out=qT[:, c0:c0 + n], in_=ps[:, 0:n], func=AF.Identity, bias=pc((l, "bq"), h), scale=1.0),
                                 reads=[pk, "pT"], writes=[("qT", c0)])
                            ps, pk = next_mm()
                            mm_group(ps[:, 0:n], [(wb[1][:, k, :], hT[:, k, c0:c0 + n]) for k in range(8)], [("W", i), ("hT", c0)], pk)
                            p.op("act", lambda e, ps=ps: e.activation(out=kT[:, c0:c0 + n], in_=ps[:, 0:n], func=AF.Identity, bias=bks[:, l * 4 + h:l * 4 + h + 1], scale=KS),
                                 reads=[pk, "bks"], writes=[("kT", c0)])
                            ps, pk = next_mm()
                            mm_group(ps[:, 0:n], [(wb[3][:, k, :], hT[:, k, c0:c0 + n]) for k in range(8)], [("W", i), ("hT", c0)], pk)
                            p.op("act", lambda e, ps=ps: e.activation(out=sigoT[:, c0:c0 + n], in_=ps[:, 0:n], func=AF.Sigmoid, bias=pc((l, "bo"), h), scale=1.0),
                                 reads=[pk, "pT"], writes=[("sigoT", c0)])
                        bri = load_brow(l, [(512 + h * 128, 128), (1024 + h * 128, 128)])
                        for c in range(NCH if "h_tm" in DBG else 0):
                            cc = c * 128
                            c0t = 0 if c < 2 else 256 + ((c - 2) // 4) * 512
                            ps, pk = next_mm()
                            mm_group(ps[:, 0:128], [(hT[:, k, cc:cc + 128], wb[1][:, k, :]) for k in range(8)] + [(onesb[0:1, :], browt[bri][0:1, 0:128])],
                                     [("W", i), ("hT", c0t), ("brow", bri), "onesb"], pk)
                            mm_group(ps[:, 128:256], [(hT[:, k, cc:cc + 128], wb[2][:, k, :]) for k in range(8)] + [(onesb[0:1, :], browt[bri][0:1, 128:256])],
                                     [("W", i), ("hT", c0t), ("brow", bri), "onesb"], pk)
                            p.op("act", lambda e, ps=ps, c=c: e.activation(out=ktm[:, c, :], in_=ps[:, 0:128], func=AF.Identity, scale=KS),
                                 reads=[pk], writes=[("ktm", c)])
                            p.op("dve", lambda e, ps=ps, c=c: e.tensor_copy(vaug[:, c, 0:128], ps[:, 128:256]), reads=[pk], writes=[("vaug", c)])

                        Df = Dst[0]
                        if "h_ms2" in DBG:
                            p.op("pool", lambda e: e.memset(Df, 0.0), writes=["Df"])
                        for c in range(NCH if "h_fwd" in DBG else 0):
                            p.op("act", lambda e, c=c: e.copy(Dfb[:, c, :], Df), reads=["Df"], writes=[("Dfb", c)])
                            if c == NCH - 1:
                                break
                            kb = kk[c % 2]
                            p.op("dve", lambda e, c=c, kb=kb: e.tensor_scalar(out=kb, in0=ktm[:, c, :], scalar1=WSE[:, 0, c, h:h + 1], scalar2=None, op0=ALU.mult),
                                 reads=[("ktm", c), "WSE"], writes=[("kk", c % 2)])
                            ps_, pk_ = next_mm()
                            sl = ps_[:, 0:129]
                            mm_group(sl, [(kb, vaug[:, c, :])], [("kk", c % 2), ("vaug", c), "vaug1"], pk_)
                            p.op("dve", lambda e, c=c, sl=sl: e.scalar_tensor_tensor(out=Df, in0=Df, scalar=ETOT[:, 0, c, h:h + 1], in1=sl,
                                                                                    op0=ALU.mult, op1=ALU.add),
                                 reads=["Df", "ETOT", pk_], writes=["Df"])

                        Db = Dst[1]
                        if "h_ms2" in DBG:
                            p.op("pool", lambda e: e.memset(Db, 0.0), writes=["Db"])
                        order = [1, 0] + list(range(NCH - 1, 1, -1))
                        for it, c in enumerate(order if ("h_out" in DBG or "h_bwd" in DBG) else []):
                            cc = c * 128
                            c0t = 0 if c < 2 else 256 + ((c - 2) // 4) * 512
                            need_out = (c in fm_chunks) and ("h_out" in DBG)
                            if need_out:
                                p.op("act", lambda e: e.copy(Dbb, Db), reads=["Db"], writes=["Dbb"])
                                a = it % 2
                                mm_group(AUX[0][:, a * 128:(a + 1) * 128], [(kT[:, cc:cc + 128], qT[:, cc:cc + 128])], [("kT", c0t), ("qT", c0t)], ("aux0s", a))
                                p.op("dve", lambda e, a=a, c=c: e.scalar_tensor_tensor(out=Aft[a], in0=AUX[0][:, a * 128:(a + 1) * 128], scalar=WS[:, 0, c, h:h + 1],
                                                                                      in1=maskU[:], op0=ALU.mult, op1=ALU.mult),
                                     reads=[("aux0s", a), "WS", "maskU"], writes=[("Aft", a)])
                                p.op("dve", lambda e, a=a, c=c: e.scalar_tensor_tensor(out=Abt[a], in0=AUX[0][:, a * 128:(a + 1) * 128], scalar=WS[:, 1, c, h:h + 1],
                                                                                      in1=maskL[:], op0=ALU.mult, op1=ALU.mult),
                                     reads=[("aux0s", a), "WS", "maskL"], writes=[("Abt", a)])
                                po = AUX[1 + a]
                                pko = ("auxo", a)
                                mm_group(po[:, 0:129], [(Aft[a], vaug[:, c, :]), (qT[:, cc:cc + 128], Dfb[:, c, :])],
                                         [("Aft", a), ("vaug", c), "vaug1", ("qT", c0t), ("Dfb", c)], pko)
                                p.op("pe", lambda e, po=po, a=a, c=c, cc=cc: [e.matmul(po[:, 256:385], lhsT=Abt[a], rhs=vaug[:, c, :], start=True, stop=False),
                                                                              e.matmul(po[:, 256:385], lhsT=qT[:, cc:cc + 128], rhs=Dbb, start=False, stop=True)][-1],
                                     reads=[("Abt", a), ("vaug", c), "vaug1", ("qT", c0t), "Dbb"], writes=[("auxob", a)])
                                ebv = EB[:, :, c, h]
                                _b = po[:, 128:129]
                                denv = bass.AP(_b.tensor, _b.offset, [list(_b.ap[0]), [256, 2]])
                                p.op("dve", lambda e, ebv=ebv, denv=denv: e.tensor_tensor(out=sm[:, 8:10], in0=denv, in1=ebv, op=ALU.mult),
                                     reads=[pko, ("auxob", a), "EB"], writes=[("sm", 8)])
                                p.op("dve", lambda e: e.tensor_scalar(out=sm[:, 10:12], in0=sm[:, 8:10], scalar1=-1.0, scalar2=None, op0=ALU.mult),
                                     reads=[("sm", 8)], writes=[("sm", 10)])
                                p.op("dve", lambda e: e.tensor_tensor(out=sm[:, 10:12], in0=sm[:, 10:12], in1=sm[:, 8:10], op=ALU.max),
                                     reads=[("sm", 8), ("sm", 10)], writes=[("sm", 10)])
                                p.op("dve", lambda e: e.tensor_scalar_max(out=sm[:, 10:12], in0=sm[:, 10:12], scalar1=1.0), reads=[("sm", 10)], writes=[("sm", 10)])
                                p.op("dve", lambda e: e.reciprocal(out=sm[:, 10:12], in_=sm[:, 10:12]), reads=[("sm", 10)], writes=[("sm", 10)])
                                p.op("dve", lambda e, ebv=ebv: e.tensor_tensor(out=sm[:, 12:14], in0=sm[:, 10:12], in1=ebv, op=ALU.mult),
                                     reads=[("sm", 10), "EB"], writes=[("sm", 12)])
                                tb = tmpA[0]
                                p.op("act", lambda e, po=po, tb=tb: e.activation(out=tb[:, 0:128], in_=po[:, 0:128], func=AF.Identity, scale=sm[:, 12:13]),
                                     reads=[pko, ("sm", 12)], writes=[("tmpA", 0)])
                                p.op("dve", lambda e, po=po, tb=tb: e.scalar_tensor_tensor(out=hsum, in0=po[:, 256:384], scalar=sm[:, 13:14], in1=tb[:, 0:128],
                                                                                          op0=ALU.mult, op1=ALU.add),
                                     reads=[("auxob", a), ("sm", 12), ("tmpA", 0)], writes=["hsum"])
                                p.op("act", lambda e: e.activation(out=junk[:, 0:128], in_=hsum, func=AF.Square, accum_out=sm[:, 14:15]),
                                     reads=["hsum"], writes=[("tmpA", 1), ("sm", 14)])
                                p.op("act", lambda e: e.activation(out=sm[:, 15:16], in_=sm[:, 14:15], func=AF.Sqrt, bias=EPS, scale=1.0 / 128),
                                     reads=[("sm", 14)], writes=[("sm", 15)])
                                p.op("dve", lambda e: e.reciprocal(out=sm[:, 16:17], in_=sm[:, 15:16]), reads=[("sm", 15)], writes=[("sm", 16)])
                                p.op("act", lambda e, a=a: e.activation(out=hn[a], in_=hsum, func=AF.Identity, scale=sm[:, 16:17]),
                                     reads=["hsum", ("sm", 16)], writes=[("hn", a)])
                                p.op("pe", lambda e, a=a: e.transpose(TP[:, a * 128:(a + 1) * 128], hn[a], identb[:]), reads=[("hn", a), "identb"], writes=[("tp", a)])
                                p.op("dve", lambda e, a=a, cc=cc: e.scalar_tensor_tensor(out=yaT[:, cc:cc + 128], in0=TP[:, a * 128:(a + 1) * 128], scalar=pc((l, "mg"), h),
                                                                                        in1=sigoT[:, cc:cc + 128], op0=ALU.mult, op1=ALU.mult),
                                     reads=[("tp", a), "pT", ("sigoT", c0t)], writes=[("yaT", c0t)])
                            if it == len(order) - 1 or "h_bwd" not in DBG:
                                continue
                            kb = kk[it % 2]
                            p.op("dve", lambda e, c=c, kb=kb: e.tensor_scalar(out=kb, in0=ktm[:, c, :], scalar1=WSE[:, 1, c, h:h + 1], scalar2=None, op0=ALU.mult),
                                 reads=[("ktm", c), "WSE"], writes=[("kk", it % 2)])
                            ps_, pk_ = next_mm()
                            sl = ps_[:, 0:129]
                            mm_group(sl, [(kb, vaug[:, c, :])], [("kk", it % 2), ("vaug", c), "vaug1"], pk_)
                            p.op("dve", lambda e, c=c, sl=sl: e.scalar_tensor_tensor(out=Db, in0=Db, scalar=ETOT[:, 1, c, h:h + 1], in1=sl,
                                                                                    op0=ALU.mult, op1=ALU.add),
                                 reads=["Db", "ETOT", pk_, "Dbb"], writes=["Db"])

                    if "heads" in DBG:
                      run_tasks([(head_load, head_comp)] +
                               ([wout_part(s, l, h * 128, 1, lambda kt, c0, n: yaT[:, c0:c0 + n], [("yaT", t[0]) for t in FM_T], FM_T)] if "h_wout" in DBG else []))

                p.barrier()
                if "ffn" in DBG:
                    norm_to_hT(s, l, A2, 24, FM_T)
                if last:
                    supers = [FM_T[0:2], FM_T[2:4]]
                else:
                    supers = [TT_ALL[0:2], TT_ALL[2:4], TT_ALL[4:5]]
                for st_tiles in (supers if "ffn" in DBG else []):
                    offs = []
                    o = 0
                    for (c0, n, is_ctx) in st_tiles:
                        offs.append(o)
                        o += n
                    tasks = []
                    for fg in range(11):
                        nf = 2

                        def load(fg=fg, nf=nf):
                            return wload_cols(w_gu, l, [(fg * 256, nf * 128), (FH + fg * 256, nf * 128)])

                        def comp(info, fg=fg, nf=nf):
                            i, tot = info
                            wv = wview(i, 8, tot)
                            for ti, (c0, n, is_ctx) in enumerate(st_tiles):
                                for fi in range(nf):
                                    f = fg * 2 + fi
                                    psg, pkg = next_mm()
                                    mm_group(psg[:, 0:n], [(wv[:, k, fi * 128:(fi + 1) * 128], hT[:, k, c0:c0 + n]) for k in range(8)], [("W", i), ("hT", c0)], pkg)
                                    psu, pku = next_mm()
                                    mm_group(psu[:, 0:n], [(wv[:, k, (nf + fi) * 128:(nf + fi + 1) * 128], hT[:, k, c0:c0 + n]) for k in range(8)], [("W", i), ("hT", c0)], pku)
                                    tb = tmpB[f % 2]
                                    p.op("act", lambda e, psg=psg, tb=tb: e.activation(out=tb[:, 0:n], in_=psg[:, 0:n], func=AF.Silu), reads=[pkg], writes=[("tmpB", f % 2)])
                                    p.op("dve", lambda e, psu=psu, tb=tb, f=f, ti=ti: e.tensor_tensor(out=actT[:, f, offs[ti]:offs[ti] + n], in0=psu[:, 0:n], in1=tb[:, 0:n], op=ALU.mult),
                                         reads=[pku, ("tmpB", f % 2)], writes=[("actT", f, ti)])
                        tasks.append((load, comp))
                    for m in range(8):
                        def load(m=m):
                            return wload([(lambda t: t[:, 0:22 * 128].rearrange("p (k n) -> p k n", k=22),
                                           w_down[l, :, m * 128:(m + 1) * 128].rearrange("(k q) n -> q k n", q=128))])

                        def comp(i, m=m):
                            wv = WSL[i][:, 0:22 * 128].rearrange("p (k n) -> p k n", k=22)
                            for ti, (c0, n, is_ctx) in enumerate(st_tiles):
                                ps, pk = next_mm()
                                mm_group(ps[:, 0:n], [(wv[:, f, :], actT[:, f, offs[ti]:offs[ti] + n]) for f in range(22)],
                                         [("W", i)] + [("actT", f, ti) for f in range(22)], pk)
                                resid_update(s, l, 40, ps, pk, m, c0, n, is_ctx)
                        tasks.append((load, comp))
                    run_tasks(tasks)

            p.barrier()
            p.dma("sp", fgb, final_g.partition_broadcast(128), writes=["fgb"])
            for blk in range(16):
                cc = 256 + blk * 128
                b2 = blk % 2
                for half in range(2):
                    ps, pk = next_mm()

                    def tr(e, ps=ps, half=half, cc=cc):
                        for j in range(4):
                            k = half * 4 + j
                            ins = e.transpose(ps[:, j * 128:(j + 1) * 128], xT[:, k, cc:cc + 128], identf[:])
                        return ins
                    c0t = 256 + (blk // 4) * 512
                    p.op("pe", tr, reads=[("xT", half * 4 + j, c0t) for j in range(4)] + ["identf"], writes=[pk])
                    p.op("act", lambda e, ps=ps, half=half, b2=b2: e.copy(xo[b2][:, half * 512:(half + 1) * 512], ps[:]), reads=[pk], writes=[("xo", b2, half)])
                p.op("act", lambda e, b2=b2: e.activation(out=xo2[b2][:], in_=xo[b2][:], func=AF.Square, accum_out=sm[:, 20 + b2:21 + b2]),
                     reads=[("xo", b2, 0), ("xo", b2, 1), ("xo2", b2)], writes=[("xo2", b2), ("sm", 20 + b2)])
                p.op("act", lambda e, b2=b2: e.activation(out=sm[:, 22 + b2:23 + b2], in_=sm[:, 20 + b2:21 + b2], func=AF.Sqrt, bias=EPS, scale=1.0 / D),
                     reads=[("sm", 20 + b2)], writes=[("sm", 22 + b2)])
                p.op("dve", lambda e, b2=b2: e.reciprocal(out=sm[:, 24 + b2:25 + b2], in_=sm[:, 22 + b2:23 + b2]), reads=[("sm", 22 + b2)], writes=[("sm", 24 + b2)])
                p.op("dve", lambda e, b2=b2: e.scalar_tensor_tensor(out=xo2[b2][:], in0=xo[b2][:], scalar=sm[:, 24 + b2:25 + b2], in1=fgb, op0=ALU.mult, op1=ALU.mult),
                     reads=[("xo", b2, 0), ("xo", b2, 1), ("sm", 24 + b2), "fgb", ("xo2", b2)], writes=[("xo2", b2)])
                p.dma("sp", out_d[s, blk * 128:(blk + 1) * 128, :], xo2[b2][:], reads=[("xo2", b2)])

        p.emit()
    return nc


_NC_CACHE = {}


def kernel(**inputs):
    n_cores = 1
    nseq = 16
    names = ["x", "c", "ctx", "c_ctx", "w_mod", "b_mod", "norm1_g", "w_in", "b_in", "mlstm_g", "sgu_ln_g", "sgu_ln_b",
             "sgu_w", "sgu_b", "conv_w", "conv_b", "conv_ln_g", "conv_ln_b", "w_out", "norm2_g", "w_gu", "w_down", "final_g"]
    arrs = {k: np.ascontiguousarray(np.asarray(inputs[k], dtype=np.float32)) for k in names}
    if "nc" not in _NC_CACHE:
        _NC_CACHE["nc"] = build_nc(nseq=nseq, depth=2)
    nc = _NC_CACHE["nc"]
    in_maps = []
    for i in range(n_cores):
        m = dict(arrs)
        m["x"] = arrs["x"][i * nseq:(i + 1) * nseq]
        m["c"] = arrs["c"][i * nseq:(i + 1) * nseq]
        m["ctx"] = arrs["ctx"][i * nseq:(i + 1) * nseq]
        in_maps.append(m)
    res = run_bass_kernel_spmd(nc, in_maps, core_ids=list(range(n_cores)))
    return np.concatenate([r["out"] for r in res.results], axis=0)
```

```python
import contextlib
import types
import numpy as np
import concourse.bass as bass
import concourse.mybir as mybir
from concourse.bass_utils import run_bass_kernel_spmd

F32 = mybir.dt.float32
BF16 = mybir.dt.bfloat16
AF = mybir.ActivationFunctionType
ALU = mybir.AluOpType

D = 1024
SEQ = 2048
CTX = 256
T = SEQ + CTX
NCH = T // 128
INC = 3088
FH = 2816
EPS = 1e-6
KS = 128 ** -0.5
DBG = {"mods", "norm1", "conv", "sgu", "gates", "heads", "ffn", "h_fwd", "h_out", "h_bwd", "h_wout", "h_ms1", "h_fm", "h_tm", "h_ms2"}


class Op:
    __slots__ = ("eng", "fn", "deps", "is_dma", "stream", "sig", "sigval")

    def __init__(self, eng, fn, is_dma=False, stream=None):
        self.eng = eng
        self.fn = fn
        self.deps = []
        self.is_dma = is_dma
        self.stream = stream
        self.sig = is_dma
        self.sigval = None


def _freeze(fn):
    if fn.__closure__ is None:
        return fn
    cells = []
    for c in fn.__closure__:
        try:
            cells.append(types.CellType(c.cell_contents))
        except ValueError:
            cells.append(c)
    g = types.FunctionType(fn.__code__, fn.__globals__, fn.__name__, fn.__defaults__, tuple(cells))
    g.__kwdefaults__ = fn.__kwdefaults__
    return g


class Prog:
    ENGS = ("pe", "act", "dve", "pool", "sp")

    def __init__(self, nc):
        self.nc = nc
        self.ops = {e: [] for e in self.ENGS}
        self.last_write = {}
        self.readers = {}
        self.streams = {}
        self.alias = {}
        self.bar = None
        self.bar_pending = set()

    def _add_deps(self, op, reads, writes):
        reads = [self.alias.get(k, k) for k in reads]
        writes = [self.alias.get(k, k) for k in writes]
        psb = []
        for k in reads + writes:
            if isinstance(k, tuple) and k and k[0] == "PSB" and k not in psb:
                psb.append(k)
        reads = [k for k in reads if k not in psb]
        writes = [k for k in writes if k not in psb]
        for b in psb:
            w = self.last_write.get(b)
            if w is not None and w is not op and (w.eng != op.eng):
                w.sig = True
                op.deps.append(w)
            self.last_write[b] = op
        deps = []
        for b in reads:
            w = self.last_write.get(b)
            if w is not None:
                deps.append((w, True))
        for b in writes:
            w = self.last_write.get(b)
            if w is not None:
                deps.append((w, True))
            for r in self.readers.get(b, {}).values():
                deps.append((r, False))
        if self.bar is not None and op.eng in self.bar_pending:
            self.bar_pending.discard(op.eng)
            for d in self.bar:
                deps.append((d, True))
        for d, true_dep in deps:
            if d is op:
                continue
            if d.is_dma or op.is_dma or d.eng != op.eng or (true_dep and op.eng != "pe"):
                if not d.is_dma:
                    d.sig = True
                op.deps.append(d)
        for b in reads:
            key = ("dma", id(op)) if op.is_dma else op.eng
            self.readers.setdefault(b, {})[key] = op
        for b in writes:
            self.last_write[b] = op
            self.readers[b] = {}

    def op(self, eng, fn, reads=(), writes=()):
        o = Op(eng, _freeze(fn))
        self._add_deps(o, reads, writes)
        self.ops[eng].append(o)
        return o

    def dma(self, eng, out, in_, reads=(), writes=(), **kw):
        stream = tuple(writes) if writes else tuple(reads)
        o = Op(eng, None, is_dma=True, stream=stream)
        o.fn = lambda e, out=out, in_=in_, kw=kw: e.dma_start(out=out, in_=in_, **kw)
        self._add_deps(o, reads, writes)
        self.ops[eng].append(o)
        self.streams.setdefault(stream, []).append(o)
        o.sigval = 16 * len(self.streams[stream])
        return o

    def barrier(self):
        bar = []
        for e in self.ENGS:
            comp = [o for o in self.ops[e] if not o.is_dma]
            if comp:
                bar.append(comp[-1])
        for lst in self.streams.values():
            bar.append(lst[-1])
        self.bar = bar
        self.bar_pending = set(self.ENGS)

    def emit(self):
        nc = self.nc
        with contextlib.ExitStack() as es:
            esem = {e: es.enter_context(nc.semaphore("s_" + e)) for e in self.ENGS}
            ssem = {}
            for i, k in enumerate(self.streams):
                ssem[k] = es.enter_context(nc.semaphore("d%d" % i))
            for e in self.ENGS:
                c = 0
                for o in self.ops[e]:
                    if o.is_dma:
                        continue
                    if o.sig:
                        c += 1
                        o.sigval = c
            block = es.enter_context(nc.Block())
            engobj = {"pe": block.tensor, "act": block.scalar, "dve": block.vector,
                      "pool": block.gpsimd, "sp": block.sync}

            def make(ename):
                ops = self.ops[ename]

                def body(eng):
                    seen = {}
                    for o in ops:
                        need = {}
                        for d in o.deps:
                            s = ssem[d.stream] if d.is_dma else esem[d.eng]
                            key = id(s)
                            if seen.get(key, 0) >= d.sigval:
                                continue
                            if key not in need or need[key][1] < d.sigval:
                                need[key] = (s, d.sigval)
                        for key, (s, v) in need.items():
                            eng.wait_ge(s, v)
                            seen[key] = v
                        ins = o.fn(eng)
                        if o.is_dma:
                            ins.then_inc(ssem[o.stream], 16)
                        elif o.sig:
                            ins.then_inc(esem[ename], 1)
                    if ename == "sp":
                        for k, lst in self.streams.items():
                            eng.wait_ge(ssem[k], 16 * len(lst))
                return body

            for e in self.ENGS:
                engobj[e](make(e))


def build_nc(nseq=2, depth=2):
    nc = bass.Bass("TRN2", target_bir_lowering=False)
    L = depth

    def din(name, shape):
        return nc.dram_tensor(name, shape, F32, kind="ExternalInput").ap()
    x_d = din("x", [nseq, SEQ, D])
    c_d = din("c", [nseq, D])
    ctx_d = din("ctx", [nseq, CTX, D])
    cctx_d = din("c_ctx", [D])
    w_mod = din("w_mod", [L, D, 6 * D])
    b_mod = din("b_mod", [L, 6 * D])
    norm1_g = din("norm1_g", [L, D])
    w_in = din("w_in", [L, D, INC])
    b_in = din("b_in", [L, INC])
    mlstm_g = din("mlstm_g", [L, 512])
    sgu_ln_g = din("sgu_ln_g", [L, 256])
    sgu_ln_b = din("sgu_ln_b", [L, 256])
    sgu_w = din("sgu_w", [L, 4, 128, 128])
    sgu_b = din("sgu_b", [L, 4, 128])
    conv_w = din("conv_w", [L, 31, 256])
    conv_b = din("conv_b", [L, 256])
    conv_ln_g = din("conv_ln_g", [L, 256])
    conv_ln_b = din("conv_ln_b", [L, 256])
    w_out = din("w_out", [L, D, D])
    norm2_g = din("norm2_g", [L, D])
    w_gu = din("w_gu", [L, D, 2 * FH])
    w_down = din("w_down", [L, FH, D])
    final_g = din("final_g", [D])
    out_d = nc.dram_tensor("out", [nseq, SEQ, D], F32, kind="ExternalOutput").ap()

    es = contextlib.ExitStack()
    with es:
        def TS(name, shape, dt):
            return es.enter_context(nc.sbuf_tensor(name, shape, dt))

        def PS(name, shape, dt):
            return es.enter_context(nc.psum_tensor(name, shape, dt))

        p = Prog(nc)

        xT = TS("xT", [128, 8, T], F32)
        hT = TS("hT", [128, 8, T], BF16)
        WSL = [TS("wsl%d" % i, [128, 4224], BF16) for i in range(3)]
        ARENA_B = 45056
        arena = TS("arena", [128, ARENA_B // 2], BF16)
        identf = TS("identf", [128, 128], F32)
        identb = TS("identb", [128, 128], BF16)
        maskU = TS("maskU", [128, 128], F32)
        maskL = TS("maskL", [128, 128], F32)
        onesf = TS("onesf", [128, 128], F32)
        onesb = TS("onesb", [128, 128], BF16)
        onesZ = TS("onesZ", [128, 2, 128], BF16)
        NCT = ((nseq + 1) * 8 + 127) // 128
        NSTG = NCT + 2 * L
        pT = TS("pT", [128, NSTG * 128], F32)
        cact = TS("cact", [128, 8, 2], BF16)
        modT = TS("modT", [128, L * 48, 2], F32)
        A1 = TS("A1", [128, L * 8, 2], F32)
        A2 = TS("A2", [128, L * 8, 2], F32)
        bks = TS("bks", [128, L * 4], F32)
        browt = [TS("brow%d" % i, [1, 256], BF16) for i in range(2)]
        browf = TS("browf", [1, 256], F32)
        gbias = TS("gbias", [128, L, 16], F32)
        gates = TS("gates", [128, NCH, 16], F32)
        SP = TS("SP", [128, 2, NCH, 4], F32)
        EB = TS("EB", [128, 2, NCH, 4], F32)
        ETOT = TS("ETOT", [128, 2, NCH, 4], F32)
        WS = TS("WS", [128, 2, NCH, 4], F32)
        WsT = TS("WsT", [128, 4, 128], BF16)
        sb_bc = TS("sb_bc", [128, 2, 128], F32)
        rstd = TS("rstd", [128, 512], F32)
        tmpA = [TS("tmpA%d" % i, [128, 512], F32) for i in range(2)]
        tmpB = [TS("tmpB%d" % i, [128, 512], F32) for i in range(2)]
        sq = [TS("sq%d" % i, [128, 512], BF16) for i in range(2)]
        sm = TS("sm", [128, 32], F32)

        def av(off_b, shape, dt):
            esz = 2 if dt == BF16 else 4
            n = int(np.prod(shape))
            assert off_b % 4 == 0 and off_b + n * esz <= ARENA_B, (off_b, shape)
            a = arena[:, off_b // 2: off_b // 2 + n * esz // 2]
            if dt == F32:
                a = a.bitcast(F32)
            if len(shape) == 2:
                a = a.rearrange("p (a b) -> p a b", a=shape[0])
            elif len(shape) == 3:
                a = a.rearrange("p (a b c) -> p a b c", a=shape[0], b=shape[1])
            return a
        qT = av(0, [T], BF16)
        kT = av(4608, [T], BF16)
        ktm = av(9216, [NCH, 128], BF16)
        vaug_full = av(13824, [NCH * 130], BF16)
        vaug = av(13824, [NCH, 130], BF16)[:, :, 0:129]
        Dfb = av(18504, [NCH, 130], BF16)[:, :, 0:129]
        sigoT = av(23184, [T], BF16)
        yaT = av(27792, [T], BF16)
        ypad0 = av(0, [32, 94], BF16)
        ypad1 = av(6016, [62, 64], BF16)
        yctx = av(13952, [2, 286], BF16)
        ycT = av(15096, [2, T], BF16)
        diag = av(24312, [62, 128], BF16)
        uT = av(0, [2, T], BF16)
        ybT = av(9216, [2, T], BF16)
        actT = av(0, [22, 1024], BF16)
        xin = [av(i * 4096, [D], F32) for i in range(2)]
        xo = [av(8192 + i * 4096, [D], F32) for i in range(2)]
        xo2 = [av(16384 + i * 4096, [D], F32) for i in range(2)]
        fgb = av(24576, [D], F32)
        _o = 32768
        Dst = [av(_o + i * 520, [129], F32) for i in range(2)]
        Dbb = av(_o + 1040, [130], BF16)[:, 0:129]
        kk = [av(_o + 1304 + i * 256, [128], BF16) for i in range(2)]
        Aft = [av(_o + 1816 + i * 256, [128], BF16) for i in range(2)]
        Abt = [av(_o + 2328 + i * 256, [128], BF16) for i in range(2)]
        hsum = av(_o + 2840, [128], F32)
        hn = [av(_o + 3352 + i * 256, [128], BF16) for i in range(2)]
        vg = av(18432, [256], F32)
        vhatZ = av(19456, [4, 128], BF16)
        cvb = [av(40192 + i * 1024, [512], BF16) for i in range(2)]
        WSE = SP
        ADD = sb_bc
        wsw = tmpA[0][:, 0:128]
        junk = tmpA[1][:, 0:256]
        stg = av(0, [NSTG, 128], F32)

        MM = [PS("mm%d" % i, [128, 512], F32) for i in range(3)]
        ST0 = PS("st0", [128, 512], F32)
        AUX = [PS("aux%d" % i, [128, 512], F32) for i in range(3)]
        TP = PS("tp0", [128, 1024], BF16)
        mm_i = [0]
        for i_ in range(3):
            p.alias[("mm", i_)] = ("PSB", "mm", i_)
        p.alias["st0"] = ("PSB", "st0")
        p.alias["aux0"] = ("PSB", "aux0")
        p.alias[("aux0s", 0)] = ("PSB", "aux0")
        p.alias[("aux0s", 1)] = ("PSB", "aux0")
        p.alias["aux1"] = ("PSB", "aux1")
        p.alias[("auxo", 0)] = ("PSB", "aux1")
        p.alias[("auxob", 0)] = ("PSB", "aux1")
        p.alias[("auxo", 1)] = ("PSB", "aux2")
        p.alias[("auxob", 1)] = ("PSB", "aux2")
        p.alias[("aux2", 0)] = ("PSB", "aux2")
        p.alias[("aux2", 1)] = ("PSB", "aux2")
        p.alias[("tp", 0)] = ("PSB", "tp")
        p.alias[("tp", 1)] = ("PSB", "tp")

        def next_mm():
            i = mm_i[0] % 3
            mm_i[0] += 1
            return MM[i], ("mm", i)

        p.op("pool", lambda e: e.memset(identf[:], 0.0), writes=["identf"])
        p.op("pool", lambda e: e.affine_select(out=identf[:], in_=identf[:], compare_op=ALU.not_equal, fill=1.0,
                                               base=0, pattern=[[-1, 128]], channel_multiplier=1),
             reads=["identf"], writes=["identf"])
        p.op("pool", lambda e: e.tensor_copy(identb[:], identf[:]), reads=["identf"], writes=["identb"])
        p.op("pool", lambda e: e.memset(maskU[:], 1.0), writes=["maskU"])
        p.op("pool", lambda e: e.affine_select(out=maskU[:], in_=maskU[:], compare_op=ALU.is_ge, fill=0.0,
                                               base=0, pattern=[[1, 128]], channel_multiplier=-1),
             reads=["maskU"], writes=["maskU"])
        p.op("pool", lambda e: e.memset(maskL[:], 1.0), writes=["maskL"])
        p.op("pool", lambda e: e.affine_select(out=maskL[:], in_=maskL[:], compare_op=ALU.is_ge, fill=0.0,
                                               base=0, pattern=[[-1, 128]], channel_multiplier=1),
             reads=["maskL"], writes=["maskL"])
        p.op("pool", lambda e: e.memset(onesf[:], 1.0), writes=["onesf"])
        p.op("pool", lambda e: e.memset(onesb[:], 1.0), writes=["onesb"])
        p.op("pool", lambda e: e.memset(onesZ[:], 0.0), writes=["onesZ"])
        p.op("pool", lambda e: e.memset(onesZ[:, 0, 0:64], 1.0), reads=["onesZ"], writes=["onesZ"])
        p.op("pool", lambda e: e.memset(onesZ[:, 1, 64:128], 1.0), reads=["onesZ"], writes=["onesZ"])
        p.op("pool", lambda e: e.memset(stg, 0.0), writes=["stg"])

        pcol = {}
        rowptr = [0]

        def prow(key, src, nrows):
            r0 = rowptr[0]
            if r0 // 128 != (r0 + nrows - 1) // 128:
                r0 = (r0 // 128 + 1) * 128
            t, r = divmod(r0, 128)
            p.dma("sp", stg[r:r + nrows, t, :], src, writes=["stg"])
            pcol[key] = r0
            rowptr[0] = r0 + nrows

        for s in range(nseq):
            prow(("c", s), c_d[s].rearrange("(r q) -> r q", q=128), 8)
        prow("cctx", cctx_d.rearrange("(r q) -> r q", q=128), 8)
        rowptr[0] = NCT * 128
        for l in range(L):
            rv = lambda v, a, b: v[l, a:b].rearrange("(r q) -> r q", q=128)
            prow((l, "n1g"), rv(norm1_g, 0, D), 8)
            prow((l, "bq"), rv(b_in, 0, 512), 4)
            prow((l, "bk"), rv(b_in, 512, 1024), 4)
            prow((l, "bo"), rv(b_in, 1536, 2048), 4)
            prow((l, "bsu"), rv(b_in, 2064, 2320), 2)
            prow((l, "bcv"), rv(b_in, 2576, 3088), 4)
            prow((l, "mg"), rv(mlstm_g, 0, 512), 4)
            prow((l, "slg"), rv(sgu_ln_g, 0, 256), 2)
            prow((l, "slb"), rv(sgu_ln_b, 0, 256), 2)
            prow((l, "cw"), conv_w[l].rearrange("k (t q) -> (k t) q", q=128), 62)
            prow((l, "cb"), rv(conv_b, 0, 256), 2)
            prow((l, "clg"), rv(conv_ln_g, 0, 256), 2)
            prow((l, "clb"), rv(conv_ln_b, 0, 256), 2)
            prow((l, "n2g"), rv(norm2_g, 0, D), 8)
            prow((l, "bmod"), rv(b_mod, 0, 6 * D), 48)
            rowptr[0] = (NCT + 2 * (l + 1)) * 128
        for t in range(NSTG):
            ps, pk = next_mm()
            p.op("pe", lambda e, t=t, ps=ps: e.transpose(ps[:, 0:128], stg[:, t, :], identf[:]),
                 reads=["stg", "identf"], writes=[pk])
            p.op("dve", lambda e, t=t, ps=ps: e.tensor_copy(pT[:, t * 128:(t + 1) * 128], ps[:, 0:128]),
                 reads=[pk], writes=["pT"])

        def pc(key, j=0, n=1):
            c0 = pcol[key] + j
            return pT[:, c0:c0 + n]

        for l in range(L):
            p.dma("sp", gbias[:, l, :], b_in[l, 2048:2064].partition_broadcast(128), writes=["gbias"])
        brow_i = [0]

        def load_brow(l, blocks):
            i = brow_i[0] % 2
            brow_i[0] += 1
            o = 0
            for c0, n in blocks:
                p.dma("sp", browf[0:1, o:o + n], b_in[l:l + 1, c0:c0 + n], writes=["browf"])
                o += n
            p.op("dve", lambda e: e.tensor_copy(browt[i][0:1, :], browf[0:1, :]), reads=["browf"], writes=[("brow", i)])
            return i

        wl_i = [0]

        def wload(parts):
            i = wl_i[0] % 3
            wl_i[0] += 1
            for dst_fn, src in parts:
                p.dma("pool", dst_fn(WSL[i]), src, writes=[("W", i)])
            return i

        def wview(i, a, b):
            return WSL[i][:, 0:a * b].rearrange("p (a b) -> p a b", a=a)

        def cols_src(w, l, c0, n):
            return w[l, :, c0:c0 + n].rearrange("(k q) n -> q k n", q=128)

        def wload_cols(w, l, blocks):
            tot = sum(n for _, n in blocks)
            parts = []
            o = 0
            for c0, n in blocks:
                parts.append((lambda t, o=o, n=n, tot=tot: t[:, 0:8 * tot].rearrange("p (k n) -> p k n", k=8)[:, :, o:o + n],
                              cols_src(w, l, c0, n)))
                o += n
            return wload(parts), tot

        def run_tasks(tasks, ahead=2):
            loaded = []
            nxt = 0
            for i in range(len(tasks)):
                while nxt < len(tasks) and nxt <= i + ahead - 1:
                    loaded.append(tasks[nxt][0]())
                    nxt += 1
                tasks[i][1](loaded[i])

        TT_ALL = [(0, 256, True)] + [(256 + 512 * i, 512, False) for i in range(4)]

        def mm_group(ps_ap, pairs, reads, wkey):
            def fn(e):
                n = len(pairs)
                for i, (a, b) in enumerate(pairs):
                    ins = e.matmul(ps_ap, lhsT=a, rhs=b, start=(i == 0), stop=(i == n - 1))
                return ins
            p.op("pe", fn, reads=reads, writes=[wkey])

        def norm_to_hT(s, l, A, shj, tiles):
            for (c0, n, is_ctx) in tiles:
                w = 1 if is_ctx else 0
                for k in range(8):
                    p.op("act", lambda e, k=k: e.activation(out=sq[k % 2][:, 0:n], in_=xT[:, k, c0:c0 + n], func=AF.Square),
                         reads=[("xT", k, c0)], writes=[("sq", k % 2)])
                    p.op("pe", lambda e, k=k: e.matmul(ST0[:, 0:n], lhsT=onesb[:], rhs=sq[k % 2][:, 0:n], start=(k == 0), stop=(k == 7)),
                         reads=[("sq", k % 2), "onesb"], writes=["st0"])
                p.op("act", lambda e: e.activation(out=rstd[:, 0:n], in_=ST0[:, 0:n], func=AF.Sqrt, bias=D * EPS, scale=1.0),
                     reads=["st0"], writes=["rstd"])
                p.op("dve", lambda e: e.reciprocal(out=rstd[:, 0:n], in_=rstd[:, 0:n]), reads=["rstd"], writes=["rstd"])
                for k in range(8):
                    tb = tmpA[k % 2]
                    p.op("dve", lambda e, k=k, tb=tb: e.scalar_tensor_tensor(out=tb[:, 0:n], in0=xT[:, k, c0:c0 + n], scalar=A[:, l * 8 + k, w:w + 1],
                                                                              in1=rstd[:, 0:n], op0=ALU.mult, op1=ALU.mult),
                         reads=[("xT", k, c0), "rstd", "A"], writes=[("tmpA", k % 2)])
                    p.op("act", lambda e, k=k, tb=tb: e.activation(out=hT[:, k, c0:c0 + n], in_=tb[:, 0:n], func=AF.Identity,
                                                                    bias=modT[:, l * 48 + shj + k, w:w + 1], scale=1.0),
                         reads=[("tmpA", k % 2), "modT"], writes=[("hT", c0)])

        def resid_update(s, l, gj, ps, pk, m, c0, n, is_ctx):
            w = 1 if is_ctx else 0
            p.op("dve", lambda e: e.scalar_tensor_tensor(out=xT[:, m, c0:c0 + n], in0=ps[:, 0:n], scalar=modT[:, l * 48 + gj + m, w:w + 1],
                                                         in1=xT[:, m, c0:c0 + n], op0=ALU.mult, op1=ALU.add),
                 reads=[pk, "modT", ("xT", m, c0)], writes=[("xT", m, c0)])

        def wout_part(s, l, r0, nkt, yfn, ykeys, tiles):
            def load():
                return wload([(lambda t: t[:, 0:nkt * 1024].rearrange("p (k n) -> p k n", k=nkt),
                               w_out[l, r0:r0 + 128 * nkt, :].rearrange("(k q) n -> q k n", q=128))])

            def comp(i):
                wv = WSL[i][:, 0:nkt * 1024].rearrange("p (k n) -> p k n", k=nkt)
                for (c0, n, is_ctx) in tiles:
                    for m in range(8):
                        ps, pk = next_mm()
                        mm_group(ps[:, 0:n], [(wv[:, kt, m * 128:(m + 1) * 128], yfn(kt, c0, n)) for kt in range(nkt)],
                                 [("W", i)] + ykeys, pk)
                        resid_update(s, l, 16, ps, pk, m, c0, n, is_ctx)
            return (load, comp)

        for s in range(nseq):
            p.barrier()
            for blk in range(NCH):
                xi = xin[blk % 2]
                src = ctx_d[s, blk * 128:(blk + 1) * 128, :] if blk < 2 else x_d[s, (blk - 2) * 128:(blk - 1) * 128, :]
                p.dma("sp", xi, src, writes=[("xin", blk % 2)])
                for half in range(2):
                    ps, pk = next_mm()

                    def tr(e, ps=ps, xi=xi, half=half):
                        for j in range(4):
                            k = half * 4 + j
                            ins = e.transpose(ps[:, j * 128:(j + 1) * 128], xi[:, k * 128:(k + 1) * 128], identf[:])
                        return ins
                    p.op("pe", tr, reads=[("xin", blk % 2), "identf"], writes=[pk])
                    eng = "act" if half == 0 else "dve"
                    outv = xT[:, half * 4:half * 4 + 4, blk * 128:(blk + 1) * 128]
                    inv = ps[:].rearrange("p (j n) -> p j n", j=4)
                    c0t = 0 if blk < 2 else 256 + ((blk - 2) // 4) * 512
                    wk = [("xT", half * 4 + j, c0t) for j in range(4)]
                    if eng == "act":
                        p.op("act", lambda e, outv=outv, inv=inv: e.copy(outv, inv), reads=[pk], writes=wk)
                    else:
                        p.op("dve", lambda e, outv=outv, inv=inv: e.tensor_copy(outv, inv), reads=[pk], writes=wk)
            p.barrier()
            for j, key in enumerate((("c", s), "cctx")):
                p.op("act", lambda e, j=j, key=key: e.activation(out=cact[:, :, j], in_=pc(key, 0, 8), func=AF.Silu),
                     reads=["pT"], writes=["cact"])
            for l in range(L):
                if s == 0 or True:
                    tasks = []
                    for g in range(12):
                        def load(g=g, l=l):
                            return wload_cols(w_mod, l, [(g * 512, 512)])

                        def comp(info, g=g, l=l):
                            i, tot = info
                            wv = wview(i, 8, 512)
                            for m in range(4):
                                j = g * 4 + m
                                mm_group(AUX[0][:, 0:2], [(wv[:, k, m * 128:(m + 1) * 128], cact[:, k, :]) for k in range(8)],
                                         [("W", i), "cact"], "aux0")
                                p.op("act", lambda e, j=j, l=l: e.activation(out=modT[:, l * 48 + j, :], in_=AUX[0][:, 0:2], func=AF.Identity,
                                                                        bias=pc((l, "bmod"), j), scale=1.0),
                                     reads=["aux0", "pT"], writes=["modT"])
                        tasks.append((load, comp))
                    run_tasks(tasks)
                    for (Atile, gkey, scj) in ((A1, "n1g", 8), (A2, "n2g", 32)):
                        p.op("dve", lambda e, Atile=Atile, scj=scj, l=l: e.tensor_scalar(out=Atile[:, l * 8:(l + 1) * 8, :], in0=modT[:, l * 48 + scj:l * 48 + scj + 8, :], scalar1=1.0, scalar2=32.0,
                                                                                   op0=ALU.add, op1=ALU.mult),
                             reads=["modT"], writes=["A"])
                        p.op("dve", lambda e, Atile=Atile, gkey=gkey, l=l: e.tensor_tensor(out=Atile[:, l * 8:(l + 1) * 8, :], in0=Atile[:, l * 8:(l + 1) * 8, :],
                                                                                     in1=pc((l, gkey), 0, 8).unsqueeze(2).broadcast_to([128, 8, 2]),
                                                                                     op=ALU.mult),
                             reads=["A", "pT"], writes=["A"])
                    p.op("dve", lambda e, l=l: e.tensor_scalar(out=bks[:, l * 4:l * 4 + 4], in0=pc((l, "bk"), 0, 4), scalar1=KS, scalar2=None, op0=ALU.mult),
                         reads=["pT"], writes=["bks"])


            for l in range(L):
                last = (l == L - 1)
                FM_T = TT_ALL[1:] if last else TT_ALL
                fm_chunks = range(2, NCH) if last else range(NCH)

                if "norm1" in DBG:
                    norm_to_hT(s, l, A1, 0, TT_ALL)

                p.barrier()
                p.op("pool", lambda e: e.memset(ypad0[:], 0.0), writes=["ypad0"])
                p.op("pool", lambda e: e.memset(ypad1[:], 0.0), writes=["ypad1"])
                p.op("pool", lambda e: e.memset(yctx[:], 0.0), writes=["yctx"])
                for j in range(62):
                    p.op("pool" if j % 2 else "dve", lambda e, j=j: e.tensor_scalar(out=diag[:, j, :], in0=identb[:], scalar1=pc((l, "cw"), j), scalar2=None, op0=ALU.mult),
                         reads=["identb", "pT"], writes=[("diag", j)])

                def conv_load():
                    return wload_cols(w_in, l, [(2576, 512)])

                def conv_comp(info):
                    i, tot = info
                    wv = wview(i, 8, 512)
                    for ti, (c0, n, is_ctx) in enumerate(FM_T):
                        for ct in range(2):
                            psa, pka = next_mm()
                            mm_group(psa[:, 0:n], [(wv[:, k, ct * 128:(ct + 1) * 128], hT[:, k, c0:c0 + n]) for k in range(8)],
                                     [("W", i), ("hT", c0)], pka)
                            psg, pkg = next_mm()
                            mm_group(psg[:, 0:n], [(wv[:, k, 256 + ct * 128:256 + (ct + 1) * 128], hT[:, k, c0:c0 + n]) for k in range(8)],
                                     [("W", i), ("hT", c0)], pkg)
                            tb = tmpB[ct]
                            p.op("act", lambda e, tb=tb, psg=psg, ct=ct: e.activation(out=tb[:, 0:n], in_=psg[:, 0:n], func=AF.Sigmoid,
                                                                                     bias=pc((l, "bcv"), 2 + ct), scale=1.0),
                                 reads=[pkg, "pT"], writes=[("tmpB", ct)])
                            if is_ctx:
                                outv = yctx[:, ct, 15:15 + 256]
                                in0 = psa[:, 0:n]
                                in1 = tb[:, 0:n]
                                wk = "yctx"
                            elif ct == 0:
                                r0 = (c0 - 256) // 64
                                outv = ypad0[:, r0:r0 + 8, 15:79]
                                in0 = psa[:].rearrange("p (r c) -> p r c", c=64)
                                in1 = tb[:].rearrange("p (r c) -> p r c", c=64)
                                wk = "ypad0"
                            else:
                                r0 = (c0 - 256) // 64
                                outv = ypad1[:, 15 + r0:15 + r0 + 8, :]
                                in0 = psa[:].rearrange("p (r c) -> p r c", c=64)
                                in1 = tb[:].rearrange("p (r c) -> p r c", c=64)
                                wk = "ypad1"
                            p.op("dve", lambda e, outv=outv, in0=in0, in1=in1, ct=ct: e.scalar_tensor_tensor(
                                out=outv, in0=in0, scalar=pc((l, "bcv"), ct), in1=in1, op0=ALU.add, op1=ALU.mult),
                                reads=[pka, ("tmpB", ct), "pT", wk], writes=[wk])
                    for ti, (c0, n, is_ctx) in enumerate(FM_T):
                        pss = []
                        for ct in range(2):
                            ps, pk = next_mm()
                            pairs = []
                            for k in range(31):
                                if is_ctx:
                                    rhs = yctx[:, ct, k:k + 256]
                                elif ct == 0:
                                    r0 = (c0 - 256) // 64
                                    rhs = ypad0[:, r0:r0 + 8, k:k + 64]
                                else:
                                    r0 = (c0 - 256) // 64
                                    rhs = ypad1[:, r0 + k:r0 + k + 8, :]
                                pairs.append((diag[:, 2 * k + ct, :], rhs))
                            mm_group(ps[:, 0:n], pairs, ["yctx", "ypad0", "ypad1"] + [("diag", 2 * k + ct) for k in range(31)], pk)
                            pss.append((ps, pk))
                        for ct in range(2):
                            ps, pk = pss[ct]
                            p.op("act", lambda e, ps=ps, ct=ct: e.activation(out=cvb[ct][:, 0:n], in_=ps[:, 0:n], func=AF.Identity,
                                                                             bias=pc((l, "cb"), ct), scale=1.0),
                                 reads=[pk, "pT"], writes=[("cvb", ct)])
                            p.op("act", lambda e, ps=ps, ct=ct: e.activation(out=sq[ct][:, 0:n], in_=ps[:, 0:n], func=AF.Square,
                                                                             bias=pc((l, "cb"), ct), scale=1.0),
                                 reads=[pk, "pT"], writes=[("sq", ct)])
                        mm_group(ST0[:, 0:n], [(onesb[:], cvb[ct][:, 0:n]) for ct in range(2)], [("cvb", 0), ("cvb", 1), "onesb"], "st0")
                        mm_group(AUX[0][:, 0:n], [(onesb[:], sq[ct][:, 0:n]) for ct in range(2)], [("sq", 0), ("sq", 1), "onesb"], "aux0")
                        mean, msq = tmpA[0], tmpA[1]
                        p.op("dve", lambda e: e.tensor_scalar(out=mean[:, 0:n], in0=ST0[:, 0:n], scalar1=1.0 / 256, scalar2=None, op0=ALU.mult),
                             reads=["st0"], writes=[("tmpA", 0)])
                        p.op("dve", lambda e: e.tensor_tensor(out=msq[:, 0:n], in0=mean[:, 0:n], in1=mean[:, 0:n], op=ALU.mult),
                             reads=[("tmpA", 0)], writes=[("tmpA", 1)])
                        p.op("dve", lambda e: e.scalar_tensor_tensor(out=msq[:, 0:n], in0=AUX[0][:, 0:n], scalar=1.0 / 256, in1=msq[:, 0:n],
                                                                     op0=ALU.mult, op1=ALU.subtract),
                             reads=["aux0", ("tmpA", 1)], writes=[("tmpA", 1)])
                        p.op("act", lambda e: e.activation(out=rstd[:, 0:n], in_=msq[:, 0:n], func=AF.Sqrt, bias=EPS, scale=1.0),
                             reads=[("tmpA", 1)], writes=["rstd"])
                        p.op("dve", lambda e: e.reciprocal(out=rstd[:, 0:n], in_=rstd[:, 0:n]), reads=["rstd"], writes=["rstd"])
                        for ct in range(2):
                            tb = tmpB[ct]
                            p.op("dve", lambda e, tb=tb, ct=ct: e.tensor_tensor(out=tb[:, 0:n], in0=cvb[ct][:, 0:n], in1=mean[:, 0:n], op=ALU.subtract),
                                 reads=[("cvb", ct), ("tmpA", 0)], writes=[("tmpB", ct)])
                            p.op("dve", lambda e, tb=tb: e.tensor_tensor(out=tb[:, 0:n], in0=tb[:, 0:n], in1=rstd[:, 0:n], op=ALU.mult),
                                 reads=[("tmpB", ct), "rstd"], writes=[("tmpB", ct)])
                            p.op("act", lambda e, tb=tb, ct=ct: e.activation(out=ycT[:, ct, c0:c0 + n], in_=tb[:, 0:n], func=AF.Silu,
                                                                             bias=pc((l, "clb"), ct), scale=pc((l, "clg"), ct)),
                                 reads=[("tmpB", ct), "pT"], writes=[("ycT", c0)])

                if "conv" in DBG:
                  run_tasks([(conv_load, conv_comp),
                           wout_part(s, l, 768, 2, lambda kt, c0, n: ycT[:, kt, c0:c0 + n], [("ycT", t[0]) for t in FM_T], FM_T)])

                p.barrier()
                p.op("pool", lambda e: e.memset(vhatZ, 0.0), writes=["vhatZ"])
                for g in range(4):
                    p.dma("sp", wsw, sgu_w[l, g], writes=[("tmpA", 0)])
                    ps, pk = next_mm()
                    p.op("pe", lambda e, ps=ps: e.transpose(ps[:, 0:128], wsw, identf[:]), reads=[("tmpA", 0), "identf"], writes=[pk])
                    p.op("dve", lambda e, ps=ps, g=g: e.tensor_copy(WsT[:, g, :], ps[:, 0:128]), reads=[pk], writes=["WsT"])
                    p.dma("sp", sb_bc[(g % 2) * 64:(g % 2) * 64 + 64, g // 2, :], sgu_b[l, g, :].partition_broadcast(64), writes=["sb_bc"])
                for j in range(2):
                    mm_group(AUX[0][:, 0:128], [(onesZ[:, e_, :], WsT[:, 2 * j + e_, :]) for e_ in range(2)], ["onesZ", "WsT"], "aux0")
                    p.op("dve", lambda e, j=j: e.scalar_tensor_tensor(out=ADD[:, j, :], in0=AUX[0][:, 0:128], scalar=pc((l, "slb"), j),
                                                                      in1=sb_bc[:, j, :], op0=ALU.mult, op1=ALU.add),
                         reads=["aux0", "pT", "sb_bc"], writes=["sb_bc"])

                def sgu_load():
                    return wload_cols(w_in, l, [(2048, 528)])

                def sgu_comp(info):
                    i, tot = info
                    wv = wview(i, 8, 528)
                    bri = load_brow(l, [(2320, 256)])
                    for (c0, n, is_ctx) in FM_T:
                        for m in range(2):
                            ps, pk = next_mm()
                            mm_group(ps[:, 0:n], [(wv[:, k, 16 + m * 128:16 + (m + 1) * 128], hT[:, k, c0:c0 + n]) for k in range(8)],
                                     [("W", i), ("hT", c0)], pk)
                            p.op("act", lambda e, ps=ps, m=m: e.activation(out=uT[:, m, c0:c0 + n], in_=ps[:, 0:n], func=AF.Gelu,
                                                                           bias=pc((l, "bsu"), m), scale=1.0),
                                 reads=[pk, "pT"], writes=[("uT", c0)])
                    for c in range(NCH):
                        cc = c * 128
                        c0t = 0 if c < 2 else 256 + ((c - 2) // 4) * 512
                        mm_group(AUX[1][:, 0:16], [(hT[:, k, cc:cc + 128], wv[:, k, 0:16]) for k in range(8)], [("W", i), ("hT", c0t)], "aux1")
                        p.op("dve", lambda e, c=c: e.tensor_tensor(out=gates[:, c, :], in0=AUX[1][:, 0:16], in1=gbias[:, l, :], op=ALU.add),
                             reads=["aux1", "gbias"], writes=["gates"])
                        if c not in fm_chunks:
                            continue
                        ps, pk = next_mm()
                        mm_group(ps[:, 0:256], [(hT[:, k, cc:cc + 128], wv[:, k, 272:528]) for k in range(8)] + [(onesb[0:1, :], browt[bri][0:1, 0:256])],
                                 [("W", i), ("hT", c0t), ("brow", bri), "onesb"], pk)
                        p.op("act", lambda e, ps=ps: e.activation(out=vg, in_=ps[:, 0:256], func=AF.Gelu), reads=[pk], writes=["vg"])
                        p.op("act", lambda e: e.activation(out=junk, in_=vg, func=AF.Identity, accum_out=sm[:, 0:1]), reads=["vg"], writes=[("tmpA", 1), ("sm", 0)])
                        p.op("act", lambda e: e.activation(out=junk, in_=vg, func=AF.Square, accum_out=sm[:, 1:2]), reads=["vg", ("tmpA", 1)], writes=[("tmpA", 1), ("sm", 1)])
                        p.op("dve", lambda e: e.tensor_scalar(out=sm[:, 2:3], in0=sm[:, 0:1], scalar1=1.0 / 256, scalar2=None, op0=ALU.mult),
                             reads=[("sm", 0)], writes=[("sm", 2)])
                        p.op("dve", lambda e: e.tensor_tensor(out=sm[:, 3:4], in0=sm[:, 2:3], in1=sm[:, 2:3], op=ALU.mult),
                             reads=[("sm", 2)], writes=[("sm", 3)])
                        p.op("dve", lambda e: e.scalar_tensor_tensor(out=sm[:, 3:4], in0=sm[:, 1:2], scalar=1.0 / 256, in1=sm[:, 3:4], op0=ALU.mult, op1=ALU.subtract),
                             reads=[("sm", 1), ("sm", 3)], writes=[("sm", 3)])
                        p.op("act", lambda e: e.activation(out=sm[:, 4:5], in_=sm[:, 3:4], func=AF.Sqrt, bias=EPS, scale=1.0), reads=[("sm", 3)], writes=[("sm", 4)])
                        p.op("dve", lambda e: e.reciprocal(out=sm[:, 5:6], in_=sm[:, 4:5]), reads=[("sm", 4)], writes=[("sm", 5)])
                        p.op("dve", lambda e: e.scalar_tensor_tensor(out=sm[:, 6:7], in0=sm[:, 2:3], scalar=-1.0, in1=sm[:, 5:6], op0=ALU.mult, op1=ALU.mult),
                             reads=[("sm", 2), ("sm", 5)], writes=[("sm", 6)])
                        zv = vhatZ
                        zap = bass.AP(zv.tensor, zv.offset, [list(zv.ap[0]), [256, 2], [192, 2], [1, 64]])
                        p.op("act", lambda e, zap=zap: e.activation(out=zap, in_=vg.rearrange("p (a b c) -> p a b c", a=2, b=2), func=AF.Identity,
                                                                    scale=sm[:, 5:6], bias=sm[:, 6:7]),
                             reads=["vg", ("sm", 5), ("sm", 6), "vhatZ"], writes=["vhatZ"])
                        for j in range(2):
                            mm_group(AUX[2][:, j * 128:(j + 1) * 128], [(vhatZ[:, 2 * j + e_, :], WsT[:, 2 * j + e_, :]) for e_ in range(2)],
                                     ["vhatZ", "WsT"], ("aux2", j))
                            tb = tmpB[j]
                            p.op("dve", lambda e, j=j, tb=tb: e.scalar_tensor_tensor(out=tb[:, 0:128], in0=AUX[2][:, j * 128:(j + 1) * 128], scalar=pc((l, "slg"), j),
                                                                                    in1=ADD[:, j, :], op0=ALU.mult, op1=ALU.add),
                                 reads=[("aux2", j), "pT", "sb_bc"], writes=[("tmpB", j)])
                            p.op("pool", lambda e, j=j, tb=tb, cc=cc: e.tensor_tensor(out=ybT[:, j, cc:cc + 128], in0=tb[:, 0:128], in1=uT[:, j, cc:cc + 128], op=ALU.mult),
                                 reads=[("tmpB", j), ("uT", c0t)], writes=[("ybT", c0t)])

                if "sgu" in DBG:
                  run_tasks([(sgu_load, sgu_comp),
                           wout_part(s, l, 512, 2, lambda kt, c0, n: ybT[:, kt, c0:c0 + n], [("ybT", t[0]) for t in FM_T], FM_T)])

                gv = gates[:]
                if "gates" not in DBG:
                    continue
                _bf = gates[:, 0, 4:5]
                _bi = gates[:, 0, 0:1]
                gap_f = bass.AP(_bf.tensor, _bf.offset, [list(_bf.ap[0]), [8, 2], [16, NCH], [1, 4]])
                gap_i = bass.AP(_bi.tensor, _bi.offset, [list(_bi.ap[0]), [8, 2], [16, NCH], [1, 4]])
                p.op("act", lambda e: e.activation(out=SP[:], in_=gap_f, func=AF.Exp, scale=-1.0), reads=["gates"], writes=["SP"])
                p.op("act", lambda e: e.activation(out=SP[:], in_=SP[:], func=AF.Ln, bias=1.0, scale=1.0), reads=["SP"], writes=["SP"])
                spf = SP[:].rearrange("p d c h -> p (d c h)")
                p.op("pe", lambda e: e.matmul(AUX[0][:, 0:72], lhsT=maskU[:], rhs=spf[:, 0:72], start=True, stop=True), reads=["SP", "maskU"], writes=["aux0"])
                p.op("pe", lambda e: e.matmul(AUX[0][:, 72:144], lhsT=maskL[:], rhs=spf[:, 72:144], start=True, stop=True), reads=["SP", "maskL"], writes=["aux0"])
                p.op("pe", lambda e: e.matmul(AUX[0][:, 144:288], lhsT=onesf[:], rhs=spf[:, 0:144], start=True, stop=True), reads=["SP", "onesf"], writes=["aux0"])
                flat = lambda t: t[:].rearrange("p d c h -> p (d c h)")
                p.op("act", lambda e: e.activation(out=flat(EB), in_=AUX[0][:, 0:144], func=AF.Exp, scale=-1.0), reads=["aux0"], writes=["EB"])
                p.op("act", lambda e: e.activation(out=flat(ETOT), in_=AUX[0][:, 144:288], func=AF.Exp, scale=-1.0), reads=["aux0"], writes=["ETOT"])
                p.op("dve", lambda e: e.tensor_tensor(out=WS[:], in0=AUX[0][:, 0:144].rearrange("p (d c h) -> p d c h", d=2, c=NCH), in1=gap_i, op=ALU.add),
                     reads=["aux0", "gates"], writes=["WS"])
                p.op("act", lambda e: e.activation(out=flat(WS), in_=flat(WS), func=AF.Exp), reads=["WS"], writes=["WS"])
                p.op("dve", lambda e: e.tensor_tensor(out=flat(WSE), in0=flat(WS), in1=flat(ETOT), op=ALU.mult), reads=["WS", "ETOT", "SP"], writes=["WSE", "SP"])

                for h in range(4):
                    p.barrier()
                    if "h_ms1" in DBG:
                        p.op("pool", lambda e: e.memset(vaug_full, 1.0), writes=["vaug1"] + [("vaug", c_) for c_ in range(NCH)])

                    def head_load(h=h):
                        parts = []
                        for b_, c0_ in enumerate((h * 128, 512 + h * 128, 1024 + h * 128, 1536 + h * 128)):
                            parts.append((lambda t, b_=b_: t[:, b_ * 1024:(b_ + 1) * 1024].rearrange("p (k n) -> p k n", k=8), cols_src(w_in, l, c0_, 128)))
                        return wload(parts), 512

                    def head_comp(info, h=h):
                        i, tot = info
                        wb = [WSL[i][:, b_ * 1024:(b_ + 1) * 1024].rearrange("p (k n) -> p k n", k=8) for b_ in range(4)]
                        for (c0, n, is_ctx) in (FM_T if "h_fm" in DBG else []):
                            ps, pk = next_mm()
                            mm_group(ps[:, 0:n], [(wb[0][:, k, :], hT[:, k, c0:c0 + n]) for k in range(8)], [("W", i), ("hT", c0)], pk)
                            p.op("act", lambda e, ps=ps: e.activation(out=qT[:, c0:c0 + n], in_=ps[:, 0:n], func=AF.Identity, bias=pc((l, "bq"), h), scale=1.0),
                                 reads=[pk, "pT"], writes=[("qT", c0)])
                            ps, pk = next_mm()
                            mm_group(ps[:, 0:n], [(wb[1][:, k, :], hT[:, k, c0:c0 + n]) for k in range(8)], [("W", i), ("hT", c0)], pk)
                            p.op("act", lambda e, ps=ps: e.activation(out=kT[:, c0:c0 + n], in_=ps[:, 0:n], func=AF.Identity, bias=bks[:, l * 4 + h:l * 4 + h + 1], scale=KS),
                                 reads=[pk, "bks"], writes=[("kT", c0)])
                            ps, pk = next_mm()
                            mm_group(ps[:, 0:n], [(wb[3][:, k, :], hT[:, k, c0:c0 + n]) for k in range(8)], [("W", i), ("hT", c0)], pk)
                            p.op("act", lambda e, ps=ps: e.activation(out=sigoT[:, c0:c0 + n], in_=ps[:, 0:n], func=AF.Sigmoid, bias=pc((l, "bo"), h), scale=1.0),
                                 reads=[pk, "pT"], writes=[("sigoT", c0)])
                        bri = load_brow(l, [(512 + h * 128, 128), (1024 + h * 128, 128)])
                        for c in range(NCH if "h_tm" in DBG else 0):
                            cc = c * 128
                            c0t = 0 if c < 2 else 256 + ((c - 2) // 4) * 512
                            ps, pk = next_mm()
                            mm_group(ps[:, 0:128], [(hT[:, k, cc:cc + 128], wb[1][:, k, :]) for k in range(8)] + [(onesb[0:1, :], browt[bri][0:1, 0:128])],
                                     [("W", i), ("hT", c0t), ("brow", bri), "onesb"], pk)
                            mm_group(ps[:, 128:256], [(hT[:, k, cc:cc + 128], wb[2][:, k, :]) for k in range(8)] + [(onesb[0:1, :], browt[bri][0:1, 128:256])],
                                     [("W", i), ("hT", c0t), ("brow", bri), "onesb"], pk)
                            p.op("act", lambda e, ps=ps, c=c: e.activation(out=ktm[:, c, :], in_=ps[:, 0:128], func=AF.Identity, scale=KS),
                                 reads=[pk], writes=[("ktm", c)])
                            p.op("dve", lambda e, ps=ps, c=c: e.tensor_copy(vaug[:, c, 0:128], ps[:, 128:256]), reads=[pk], writes=[("vaug", c)])

                        Df = Dst[0]
                        if "h_ms2" in DBG:
                            p.op("pool", lambda e: e.memset(Df, 0.0), writes=["Df"])
                        for c in range(NCH if "h_fwd" in DBG else 0):
                            p.op("act", lambda e, c=c: e.copy(Dfb[:, c, :], Df), reads=["Df"], writes=[("Dfb", c)])
                            if c == NCH - 1:
                                break
                            kb = kk[c % 2]
                            p.op("dve", lambda e, c=c, kb=kb: e.tensor_scalar(out=kb, in0=ktm[:, c, :], scalar1=WSE[:, 0, c, h:h + 1], scalar2=None, op0=ALU.mult),
                                 reads=[("ktm", c), "WSE"], writes=[("kk", c % 2)])
                            ps_, pk_ = next_mm()
                            sl = ps_[:, 0:129]
                            mm_group(sl, [(kb, vaug[:, c, :])], [("kk", c % 2), ("vaug", c), "vaug1"], pk_)
                            p.op("dve", lambda e, c=c, sl=sl: e.scalar_tensor_tensor(out=Df, in0=Df, scalar=ETOT[:, 0, c, h:h + 1], in1=sl,
                                                                                    op0=ALU.mult, op1=ALU.add),
                                 reads=["Df", "ETOT", pk_], writes=["Df"])

                        Db = Dst[1]
                        if "h_ms2" in DBG:
                            p.op("pool", lambda e: e.memset(Db, 0.0), writes=["Db"])
                        order = [1, 0] + list(range(NCH - 1, 1, -1))
                        for it, c in enumerate(order if ("h_out" in DBG or "h_bwd" in DBG) else []):
                            cc = c * 128
                            c0t = 0 if c < 2 else 256 + ((c - 2) // 4) * 512
                            need_out = (c in fm_chunks) and ("h_out" in DBG)
                            if need_out:
                                p.op("act", lambda e: e.copy(Dbb, Db), reads=["Db"], writes=["Dbb"])
                                a = it % 2
                                mm_group(AUX[0][:, a * 128:(a + 1) * 128], [(kT[:, cc:cc + 128], qT[:, cc:cc + 128])], [("kT", c0t), ("qT", c0t)], ("aux0s", a))
                                p.op("dve", lambda e, a=a, c=c: e.scalar_tensor_tensor(out=Aft[a], in0=AUX[0][:, a * 128:(a + 1) * 128], scalar=WS[:, 0, c, h:h + 1],
                                                                                      in1=maskU[:], op0=ALU.mult, op1=ALU.mult),
                                     reads=[("aux0s", a), "WS", "maskU"], writes=[("Aft", a)])
                                p.op("dve", lambda e, a=a, c=c: e.scalar_tensor_tensor(out=Abt[a], in0=AUX[0][:, a * 128:(a + 1) * 128], scalar=WS[:, 1, c, h:h + 1],
                                                                                      in1=maskL[:], op0=ALU.mult, op1=ALU.mult),
                                     reads=[("aux0s", a), "WS", "maskL"], writes=[("Abt", a)])
                                po = AUX[1 + a]
                                pko = ("auxo", a)
                                mm_group(po[:, 0:129], [(Aft[a], vaug[:, c, :]), (qT[:, cc:cc + 128], Dfb[:, c, :])],
                                         [("Aft", a), ("vaug", c), "vaug1", ("qT", c0t), ("Dfb", c)], pko)
                                p.op("pe", lambda e, po=po, a=a, c=c, cc=cc: [e.matmul(po[:, 256:385], lhsT=Abt[a], rhs=vaug[:, c, :], start=True, stop=False),
                                                                              e.matmul(po[:, 256:385], lhsT=qT[:, cc:cc + 128], rhs=Dbb, start=False, stop=True)][-1],
                                     reads=[("Abt", a), ("vaug", c), "vaug1", ("qT", c0t), "Dbb"], writes=[("auxob", a)])
                                ebv = EB[:, :, c, h]
                                _b = po[:, 128:129]
                                denv = bass.AP(_b.tensor, _b.offset, [list(_b.ap[0]), [256, 2]])
                                p.op("dve", lambda e, ebv=ebv, denv=denv: e.tensor_tensor(out=sm[:, 8:10], in0=denv, in1=ebv, op=ALU.mult),
                                     reads=[pko, ("auxob", a), "EB"], writes=[("sm", 8)])
                                p.op("dve", lambda e: e.tensor_scalar(out=sm[:, 10:12], in0=sm[:, 8:10], scalar1=-1.0, scalar2=None, op0=ALU.mult),
                                     reads=[("sm", 8)], writes=[("sm", 10)])
                                p.op("dve", lambda e: e.tensor_tensor(out=sm[:, 10:12], in0=sm[:, 10:12], in1=sm[:, 8:10], op=ALU.max),
                                     reads=[("sm", 8), ("sm", 10)], writes=[("sm", 10)])
                                p.op("dve", lambda e: e.tensor_scalar_max(out=sm[:, 10:12], in0=sm[:, 10:12], scalar1=1.0), reads=[("sm", 10)], writes=[("sm", 10)])
                                p.op("dve", lambda e: e.reciprocal(out=sm[:, 10:12], in_=sm[:, 10:12]), reads=[("sm", 10)], writes=[("sm", 10)])
                                p.op("dve", lambda e, ebv=ebv: e.tensor_tensor(out=sm[:, 12:14], in0=sm[:, 10:12], in1=ebv, op=ALU.mult),
                                     reads=[("sm", 10), "EB"], writes=[("sm", 12)])
                                tb = tmpA[0]
                                p.op("act", lambda e, po=po, tb=tb: e.activation(out=tb[:, 0:128], in_=po[:, 0:128], func=AF.Identity, scale=sm[:, 12:13]),
                                     reads=[pko, ("sm", 12)], writes=[("tmpA", 0)])
                                p.op("dve", lambda e, po=po, tb=tb: e.scalar_tensor_tensor(out=hsum, in0=po[:, 256:384], scalar=sm[:, 13:14], in1=tb[:, 0:128],
                                                                                          op0=ALU.mult, op1=ALU.add),
                                     reads=[("auxob", a), ("sm", 12), ("tmpA", 0)], writes=["hsum"])
                                p.op("act", lambda e: e.activation(out=junk[:, 0:128], in_=hsum, func=AF.Square, accum_out=sm[:, 14:15]),
                                     reads=["hsum"], writes=[("tmpA", 1), ("sm", 14)])
                                p.op("act", lambda e: e.activation(out=sm[:, 15:16], in_=sm[:, 14:15], func=AF.Sqrt, bias=EPS, scale=1.0 / 128),
                                     reads=[("sm", 14)], writes=[("sm", 15)])
                                p.op("dve", lambda e: e.reciprocal(out=sm[:, 16:17], in_=sm[:, 15:16]), reads=[("sm", 15)], writes=[("sm", 16)])
                                p.op("act", lambda e, a=a: e.activation(out=hn[a], in_=hsum, func=AF.Identity, scale=sm[:, 16:17]),
                                     reads=["hsum", ("sm", 16)], writes=[("hn", a)])
                                p.op("pe", lambda e, a=a: e.transpose(TP[:, a * 128:(a + 1) * 128], hn[a], identb[:]), reads=[("hn", a), "identb"], writes=[("tp", a)])
                                p.op("dve", lambda e, a=a, cc=cc: e.scalar_tensor_tensor(out=yaT[:, cc:cc + 128], in0=TP[:, a * 128:(a + 1) * 128], scalar=pc((l, "mg"), h),
                                                                                        in1=sigoT[:, cc:cc + 128], op0=ALU.mult, op1=ALU.mult),
                                     reads=[("tp", a), "pT", ("sigoT", c0t)], writes=[("yaT", c0t)])
                            if it == len(order) - 1 or "h_bwd" not in DBG:
                                continue
                            kb = kk[it % 2]
                            p.op("dve", lambda e, c=c, kb=kb: e.tensor_scalar(out=kb, in0=ktm[:, c, :], scalar1=WSE[:, 1, c, h:h + 1], scalar2=None, op0=ALU.mult),
                                 reads=[("ktm", c), "WSE"], writes=[("kk", it % 2)])
                            ps_, pk_ = next_mm()
                            sl = ps_[:, 0:129]
                            mm_group(sl, [(kb, vaug[:, c, :])], [("kk", it % 2), ("vaug", c), "vaug1"], pk_)
                            p.op("dve", lambda e, c=c, sl=sl: e.scalar_tensor_tensor(out=Db, in0=Db, scalar=ETOT[:, 1, c, h:h + 1], in1=sl,
                                                                                    op0=ALU.mult, op1=ALU.add),
                                 reads=["Db", "ETOT", pk_, "Dbb"], writes=["Db"])

                    if "heads" in DBG:
                      run_tasks([(head_load, head_comp)] +
                               ([wout_part(s, l, h * 128, 1, lambda kt, c0, n: yaT[:, c0:c0 + n], [("yaT", t[0]) for t in FM_T], FM_T)] if "h_wout" in DBG else []))

                p.barrier()
                if "ffn" in DBG:
                    norm_to_hT(s, l, A2, 24, FM_T)
                if last:
                    supers = [FM_T[0:2], FM_T[2:4]]
                else:
                    supers = [TT_ALL[0:2], TT_ALL[2:4], TT_ALL[4:5]]
                for st_tiles in (supers if "ffn" in DBG else []):
                    offs = []
                    o = 0
                    for (c0, n, is_ctx) in st_tiles:
                        offs.append(o)
                        o += n
                    tasks = []
                    for fg in range(11):
                        nf = 2

                        def load(fg=fg, nf=nf):
                            return wload_cols(w_gu, l, [(fg * 256, nf * 128), (FH + fg * 256, nf * 128)])

                        def comp(info, fg=fg, nf=nf):
                            i, tot = info
                            wv = wview(i, 8, tot)
                            for ti, (c0, n, is_ctx) in enumerate(st_tiles):
                                for fi in range(nf):
                                    f = fg * 2 + fi
                                    psg, pkg = next_mm()
                                    mm_group(psg[:, 0:n], [(wv[:, k, fi * 128:(fi + 1) * 128], hT[:, k, c0:c0 + n]) for k in range(8)], [("W", i), ("hT", c0)], pkg)
                                    psu, pku = next_mm()
                                    mm_group(psu[:, 0:n], [(wv[:, k, (nf + fi) * 128:(nf + fi + 1) * 128], hT[:, k, c0:c0 + n]) for k in range(8)], [("W", i), ("hT", c0)], pku)
                                    tb = tmpB[f % 2]
                                    p.op("act", lambda e, psg=psg, tb=tb: e.activation(out=tb[:, 0:n], in_=psg[:, 0:n], func=AF.Silu), reads=[pkg], writes=[("tmpB", f % 2)])
                                    p.op("dve", lambda e, psu=psu, tb=tb, f=f, ti=ti: e.tensor_tensor(out=actT[:, f, offs[ti]:offs[ti] + n], in0=psu[:, 0:n], in1=tb[:, 0:n], op=ALU.mult),
                                         reads=[pku, ("tmpB", f % 2)], writes=[("actT", f, ti)])
                        tasks.append((load, comp))
                    for m in range(8):
                        def load(m=m):
                            return wload([(lambda t: t[:, 0:22 * 128].rearrange("p (k n) -> p k n", k=22),
                                           w_down[l, :, m * 128:(m + 1) * 128].rearrange("(k q) n -> q k n", q=128))])

                        def comp(i, m=m):
                            wv = WSL[i][:, 0:22 * 128].rearrange("p (k n) -> p k n", k=22)
                            for ti, (c0, n, is_ctx) in enumerate(st_tiles):
                                ps, pk = next_mm()
                                mm_group(ps[:, 0:n], [(wv[:, f, :], actT[:, f, offs[ti]:offs[ti] + n]) for f in range(22)],
                                         [("W", i)] + [("actT", f, ti) for f in range(22)], pk)
                                resid_update(s, l, 40, ps, pk, m, c0, n, is_ctx)
                        tasks.append((load, comp))
                    run_tasks(tasks)

            p.barrier()
            p.dma("sp", fgb, final_g.partition_broadcast(128), writes=["fgb"])
            for blk in range(16):
                cc = 256 + blk * 128
                b2 = blk % 2
                for half in range(2):
                    ps, pk = next_mm()

                    def tr(e, ps=ps, half=half, cc=cc):
                        for j in range(4):
                            k = half * 4 + j
                            ins = e.transpose(ps[:, j * 128:(j + 1) * 128], xT[:, k, cc:cc + 128], identf[:])
                        return ins
                    c0t = 256 + (blk // 4) * 512
                    p.op("pe", tr, reads=[("xT", half * 4 + j, c0t) for j in range(4)] + ["identf"], writes=[pk])
                    p.op("act", lambda e, ps=ps, half=half, b2=b2: e.copy(xo[b2][:, half * 512:(half + 1) * 512], ps[:]), reads=[pk], writes=[("xo", b2, half)])
                p.op("act", lambda e, b2=b2: e.activation(out=xo2[b2][:], in_=xo[b2][:], func=AF.Square, accum_out=sm[:, 20 + b2:21 + b2]),
                     reads=[("xo", b2, 0), ("xo", b2, 1), ("xo2", b2)], writes=[("xo2", b2), ("sm", 20 + b2)])
                p.op("act", lambda e, b2=b2: e.activation(out=sm[:, 22 + b2:23 + b2], in_=sm[:, 20 + b2:21 + b2], func=AF.Sqrt, bias=EPS, scale=1.0 / D),
                     reads=[("sm", 20 + b2)], writes=[("sm", 22 + b2)])
                p.op("dve", lambda e, b2=b2: e.reciprocal(out=sm[:, 24 + b2:25 + b2], in_=sm[:, 22 + b2:23 + b2]), reads=[("sm", 22 + b2)], writes=[("sm", 24 + b2)])
                p.op("dve", lambda e, b2=b2: e.scalar_tensor_tensor(out=xo2[b2][:], in0=xo[b2][:], scalar=sm[:, 24 + b2:25 + b2], in1=fgb, op0=ALU.mult, op1=ALU.mult),
                     reads=[("xo", b2, 0), ("xo", b2, 1), ("sm", 24 + b2), "fgb", ("xo2", b2)], writes=[("xo2", b2)])
                p.dma("sp", out_d[s, blk * 128:(blk + 1) * 128, :], xo2[b2][:], reads=[("xo2", b2)])

        p.emit()
    return nc


_NC_CACHE = {}


def kernel(**inputs):
    n_cores = 2
    nseq = 8
    names = ["x", "c", "ctx", "c_ctx", "w_mod", "b_mod", "norm1_g", "w_in", "b_in", "mlstm_g", "sgu_ln_g", "sgu_ln_b",
             "sgu_w", "sgu_b", "conv_w", "conv_b", "conv_ln_g", "conv_ln_b", "w_out", "norm2_g", "w_gu", "w_down", "final_g"]
    arrs = {k: np.ascontiguousarray(np.asarray(inputs[k], dtype=np.float32)) for k in names}
    if "nc" not in _NC_CACHE:
        _NC_CACHE["nc"] = build_nc(nseq=nseq, depth=2)
    nc = _NC_CACHE["nc"]
    in_maps = []
    for i in range(n_cores):
        m = dict(arrs)
        m["x"] = arrs["x"][i * nseq:(i + 1) * nseq]
        m["c"] = arrs["c"][i * nseq:(i + 1) * nseq]
        m["ctx"] = arrs["ctx"][i * nseq:(i + 1) * nseq]
        in_maps.append(m)
    res = run_bass_kernel_spmd(nc, in_maps, core_ids=list(range(n_cores)))
    return np.concatenate([r["out"] for r in res.results], axis=0)
```

```python
import contextlib
import types
import numpy as np
import concourse.bass as bass
import concourse.mybir as mybir
from concourse.bass_utils import run_bass_kernel_spmd

F32 = mybir.dt.float32
BF16 = mybir.dt.bfloat16
AF = mybir.ActivationFunctionType
ALU = mybir.AluOpType

D = 1024
SEQ = 2048
CTX = 256
T = SEQ + CTX
NCH = T // 128
INC = 3088
FH = 2816
EPS = 1e-6
KS = 128 ** -0.5
DBG = {"mods", "norm1", "conv", "sgu", "gates", "heads", "ffn", "h_fwd", "h_out", "h_bwd", "h_wout", "h_ms1", "h_fm", "h_tm", "h_ms2"}


class Op:
    __slots__ = ("eng", "fn", "deps", "is_dma", "stream", "sig", "sigval")

    def __init__(self, eng, fn, is_dma=False, stream=None):
        self.eng = eng
        self.fn = fn
        self.deps = []
        self.is_dma = is_dma
        self.stream = stream
        self.sig = is_dma
        self.sigval = None


def _freeze(fn):
    if fn.__closure__ is None:
        return fn
    cells = []
    for c in fn.__closure__:
        try:
            cells.append(types.CellType(c.cell_contents))
        except ValueError:
            cells.append(c)
    g = types.FunctionType(fn.__code__, fn.__globals__, fn.__name__, fn.__defaults__, tuple(cells))
    g.__kwdefaults__ = fn.__kwdefaults__
    return g


class Prog:
    ENGS = ("pe", "act", "dve", "pool", "sp")

    def __init__(self, nc):
        self.nc = nc
        self.ops = {e: [] for e in self.ENGS}
        self.last_write = {}
        self.readers = {}
        self.streams = {}
        self.alias = {}
        self.bar = None
        self.bar_pending = set()

    def _add_deps(self, op, reads, writes):
        reads = [self.alias.get(k, k) for k in reads]
        writes = [self.alias.get(k, k) for k in writes]
        psb = []
        for k in reads + writes:
            if isinstance(k, tuple) and k and k[0] == "PSB" and k not in psb:
                psb.append(k)
        reads = [k for k in reads if k not in psb]
        writes = [k for k in writes if k not in psb]
        for b in psb:
            w = self.last_write.get(b)
            if w is not None and w is not op and (w.eng != op.eng):
                w.sig = True
                op.deps.append(w)
            self.last_write[b] = op
        deps = []
        for b in reads:
            w = self.last_write.get(b)
            if w is not None:
                deps.append((w, True))
        for b in writes:
            w = self.last_write.get(b)
            if w is not None:
                deps.append((w, True))
            for r in self.readers.get(b, {}).values():
                deps.append((r, False))
        if self.bar is not None and op.eng in self.bar_pending:
            self.bar_pending.discard(op.eng)
            for d in self.bar:
                deps.append((d, True))
        for d, true_dep in deps:
            if d is op:
                continue
            if d.is_dma or op.is_dma or d.eng != op.eng or (true_dep and op.eng != "pe"):
                if not d.is_dma:
                    d.sig = True
                op.deps.append(d)
        for b in reads:
            key = ("dma", id(op)) if op.is_dma else op.eng
            self.readers.setdefault(b, {})[key] = op
        for b in writes:
            self.last_write[b] = op
            self.readers[b] = {}

    def op(self, eng, fn, reads=(), writes=()):
        o = Op(eng, _freeze(fn))
        self._add_deps(o, reads, writes)
        self.ops[eng].append(o)
        return o

    def dma(self, eng, out, in_, reads=(), writes=(), **kw):
        stream = tuple(writes) if writes else tuple(reads)
        o = Op(eng, None, is_dma=True, stream=stream)
        o.fn = lambda e, out=out, in_=in_, kw=kw: e.dma_start(out=out, in_=in_, **kw)
        self._add_deps(o, reads, writes)
        self.ops[eng].append(o)
        self.streams.setdefault(stream, []).append(o)
        o.sigval = 16 * len(self.streams[stream])
        return o

    def barrier(self):
        bar = []
        for e in self.ENGS:
            comp = [o for o in self.ops[e] if not o.is_dma]
            if comp:
                bar.append(comp[-1])
        for lst in self.streams.values():
            bar.append(lst[-1])
        self.bar = bar
        self.bar_pending = set(self.ENGS)

    def emit(self):
        nc = self.nc
        with contextlib.ExitStack() as es:
            esem = {e: es.enter_context(nc.semaphore("s_" + e)) for e in self.ENGS}
            ssem = {}
            for i, k in enumerate(self.streams):
                ssem[k] = es.enter_context(nc.semaphore("d%d" % i))
            for e in self.ENGS:
                c = 0
                for o in self.ops[e]:
                    if o.is_dma:
                        continue
                    if o.sig:
                        c += 1
                        o.sigval = c
            block = es.enter_context(nc.Block())
            engobj = {"pe": block.tensor, "act": block.scalar, "dve": block.vector,
                      "pool": block.gpsimd, "sp": block.sync}

            def make(ename):
                ops = self.ops[ename]

                def body(eng):
                    seen = {}
                    for o in ops:
                        need = {}
                        for d in o.deps:
                            s = ssem[d.stream] if d.is_dma else esem[d.eng]
                            key = id(s)
                            if seen.get(key, 0) >= d.sigval:
                                continue
                            if key not in need or need[key][1] < d.sigval:
                                need[key] = (s, d.sigval)
                        for key, (s, v) in need.items():
                            eng.wait_ge(s, v)
                            seen[key] = v
                        ins = o.fn(eng)
                        if o.is_dma:
                            ins.then_inc(ssem[o.stream], 16)
                        elif o.sig:
                            ins.then_inc(esem[ename], 1)
                    if ename == "sp":
                        for k, lst in self.streams.items():
                            eng.wait_ge(ssem[k], 16 * len(lst))
                return body

            for e in self.ENGS:
                engobj[e](make(e))


def build_nc(nseq=2, depth=2):
    nc = bass.Bass("TRN2", target_bir_lowering=False)
    L = depth

    def din(name, shape):
        return nc.dram_tensor(name, shape, F32, kind="ExternalInput").ap()
    x_d = din("x", [nseq, SEQ, D])
    c_d = din("c", [nseq, D])
    ctx_d = din("ctx", [nseq, CTX, D])
    cctx_d = din("c_ctx", [D])
    w_mod = din("w_mod", [L, D, 6 * D])
    b_mod = din("b_mod", [L, 6 * D])
    norm1_g = din("norm1_g", [L, D])
    w_in = din("w_in", [L, D, INC])
    b_in = din("b_in", [L, INC])
    mlstm_g = din("mlstm_g", [L, 512])
    sgu_ln_g = din("sgu_ln_g", [L, 256])
    sgu_ln_b = din("sgu_ln_b", [L, 256])
    sgu_w = din("sgu_w", [L, 4, 128, 128])
    sgu_b = din("sgu_b", [L, 4, 128])
    conv_w = din("conv_w", [L, 31, 256])
    conv_b = din("conv_b", [L, 256])
    conv_ln_g = din("conv_ln_g", [L, 256])
    conv_ln_b = din("conv_ln_b", [L, 256])
    w_out = din("w_out", [L, D, D])
    norm2_g = din("norm2_g", [L, D])
    w_gu = din("w_gu", [L, D, 2 * FH])
    w_down = din("w_down", [L, FH, D])
    final_g = din("final_g", [D])
    out_d = nc.dram_tensor("out", [nseq, SEQ, D], F32, kind="ExternalOutput").ap()

    es = contextlib.ExitStack()
    with es:
        def TS(name, shape, dt):
            return es.enter_context(nc.sbuf_tensor(name, shape, dt))

        def PS(name, shape, dt):
            return es.enter_context(nc.psum_tensor(name, shape, dt))

        p = Prog(nc)

        xT = TS("xT", [128, 8, T], F32)
        hT = TS("hT", [128, 8, T], BF16)
        WSL = [TS("wsl%d" % i, [128, 4224], BF16) for i in range(3)]
        ARENA_B = 45056
        arena = TS("arena", [128, ARENA_B // 2], BF16)
        identf = TS("identf", [128, 128], F32)
        identb = TS("identb", [128, 128], BF16)
        maskU = TS("maskU", [128, 128], F32)
        maskL = TS("maskL", [128, 128], F32)
        onesf = TS("onesf", [128, 128], F32)
        onesb = TS("onesb", [128, 128], BF16)
        onesZ = TS("onesZ", [128, 2, 128], BF16)
        NCT = ((nseq + 1) * 8 + 127) // 128
        NSTG = NCT + 2 * L
        pT = TS("pT", [128, NSTG * 128], F32)
        cact = TS("cact", [128, 8, 2], BF16)
        modT = TS("modT", [128, L * 48, 2], F32)
        A1 = TS("A1", [128, L * 8, 2], F32)
        A2 = TS("A2", [128, L * 8, 2], F32)
        bks = TS("bks", [128, L * 4], F32)
        browt = [TS("brow%d" % i, [1, 256], BF16) for i in range(2)]
        browf = TS("browf", [1, 256], F32)
        gbias = TS("gbias", [128, L, 16], F32)
        gates = TS("gates", [128, NCH, 16], F32)
        SP = TS("SP", [128, 2, NCH, 4], F32)
        EB = TS("EB", [128, 2, NCH, 4], F32)
        ETOT = TS("ETOT", [128, 2, NCH, 4], F32)
        WS = TS("WS", [128, 2, NCH, 4], F32)
        WsT = TS("WsT", [128, 4, 128], BF16)
        sb_bc = TS("sb_bc", [128, 2, 128], F32)
        rstd = TS("rstd", [128, 512], F32)
        tmpA = [TS("tmpA%d" % i, [128, 512], F32) for i in range(2)]
        tmpB = [TS("tmpB%d" % i, [128, 512], F32) for i in range(2)]
        sq = [TS("sq%d" % i, [128, 512], BF16) for i in range(2)]
        sm = TS("sm", [128, 32], F32)

        def av(off_b, shape, dt):
            esz = 2 if dt == BF16 else 4
            n = int(np.prod(shape))
            assert off_b % 4 == 0 and off_b + n * esz <= ARENA_B, (off_b, shape)
            a = arena[:, off_b // 2: off_b // 2 + n * esz // 2]
            if dt == F32:
                a = a.bitcast(F32)
            if len(shape) == 2:
                a = a.rearrange("p (a b) -> p a b", a=shape[0])
            elif len(shape) == 3:
                a = a.rearrange("p (a b c) -> p a b c", a=shape[0], b=shape[1])
            return a
        qT = av(0, [T], BF16)
        kT = av(4608, [T], BF16)
        ktm = av(9216, [NCH, 128], BF16)
        vaug_full = av(13824, [NCH * 130], BF16)
        vaug = av(13824, [NCH, 130], BF16)[:, :, 0:129]
        Dfb = av(18504, [NCH, 130], BF16)[:, :, 0:129]
        sigoT = av(23184, [T], BF16)
        yaT = av(27792, [T], BF16)
        ypad0 = av(0, [32, 94], BF16)
        ypad1 = av(6016, [62, 64], BF16)
        yctx = av(13952, [2, 286], BF16)
        ycT = av(15096, [2, T], BF16)
        diag = av(24312, [62, 128], BF16)
        uT = av(0, [2, T], BF16)
        ybT = av(9216, [2, T], BF16)
        actT = av(0, [22, 1024], BF16)
        xin = [av(i * 4096, [D], F32) for i in range(2)]
        xo = [av(8192 + i * 4096, [D], F32) for i in range(2)]
        xo2 = [av(16384 + i * 4096, [D], F32) for i in range(2)]
        fgb = av(24576, [D], F32)
        _o = 32768
        Dst = [av(_o + i * 520, [129], F32) for i in range(2)]
        Dbb = av(_o + 1040, [130], BF16)[:, 0:129]
        kk = [av(_o + 1304 + i * 256, [128], BF16) for i in range(2)]
        Aft = [av(_o + 1816 + i * 256, [128], BF16) for i in range(2)]
        Abt = [av(_o + 2328 + i * 256, [128], BF16) for i in range(2)]
        hsum = av(_o + 2840, [128], F32)
        hn = [av(_o + 3352 + i * 256, [128], BF16) for i in range(2)]
        vg = av(18432, [256], F32)
        vhatZ = av(19456, [4, 128], BF16)
        cvb = [av(40192 + i * 1024, [512], BF16) for i in range(2)]
        WSE = SP
        ADD = sb_bc
        wsw = tmpA[0][:, 0:128]
        junk = tmpA[1][:, 0:256]
        stg = av(0, [NSTG, 128], F32)

        MM = [PS("mm%d" % i, [128, 512], F32) for i in range(3)]
        ST0 = PS("st0", [128, 512], F32)
        AUX = [PS("aux%d" % i, [128, 512], F32) for i in range(3)]
        TP = PS("tp0", [128, 1024], BF16)
        mm_i = [0]
        for i_ in range(3):
            p.alias[("mm", i_)] = ("PSB", "mm", i_)
        p.alias["st0"] = ("PSB", "st0")
        p.alias["aux0"] = ("PSB", "aux0")
        p.alias[("aux0s", 0)] = ("PSB", "aux0")
        p.alias[("aux0s", 1)] = ("PSB", "aux0")
        p.alias["aux1"] = ("PSB", "aux1")
        p.alias[("auxo", 0)] = ("PSB", "aux1")
        p.alias[("auxob", 0)] = ("PSB", "aux1")
        p.alias[("auxo", 1)] = ("PSB", "aux2")
        p.alias[("auxob", 1)] = ("PSB", "aux2")
        p.alias[("aux2", 0)] = ("PSB", "aux2")
        p.alias[("aux2", 1)] = ("PSB", "aux2")
        p.alias[("tp", 0)] = ("PSB", "tp")
        p.alias[("tp", 1)] = ("PSB", "tp")

        def next_mm():
            i = mm_i[0] % 3
            mm_i[0] += 1
            return MM[i], ("mm", i)

        p.op("pool", lambda e: e.memset(identf[:], 0.0), writes=["identf"])
        p.op("pool", lambda e: e.affine_select(out=identf[:], in_=identf[:], compare_op=ALU.not_equal, fill=1.0,
                                               base=0, pattern=[[-1, 128]], channel_multiplier=1),
             reads=["identf"], writes=["identf"])
        p.op("pool", lambda e: e.tensor_copy(identb[:], identf[:]), reads=["identf"], writes=["identb"])
        p.op("pool", lambda e: e.memset(maskU[:], 1.0), writes=["maskU"])
        p.op("pool", lambda e: e.affine_select(out=maskU[:], in_=maskU[:], compare_op=ALU.is_ge, fill=0.0,
                                               base=0, pattern=[[1, 128]], channel_multiplier=-1),
             reads=["maskU"], writes=["maskU"])
        p.op("pool", lambda e: e.memset(maskL[:], 1.0), writes=["maskL"])
        p.op("pool", lambda e: e.affine_select(out=maskL[:], in_=maskL[:], compare_op=ALU.is_ge, fill=0.0,
                                               base=0, pattern=[[-1, 128]], channel_multiplier=1),
             reads=["maskL"], writes=["maskL"])
        p.op("pool", lambda e: e.memset(onesf[:], 1.0), writes=["onesf"])
        p.op("pool", lambda e: e.memset(onesb[:], 1.0), writes=["onesb"])
        p.op("pool", lambda e: e.memset(onesZ[:], 0.0), writes=["onesZ"])
        p.op("pool", lambda e: e.memset(onesZ[:, 0, 0:64], 1.0), reads=["onesZ"], writes=["onesZ"])
        p.op("pool", lambda e: e.memset(onesZ[:, 1, 64:128], 1.0), reads=["onesZ"], writes=["onesZ"])
        p.op("pool", lambda e: e.memset(stg, 0.0), writes=["stg"])

        pcol = {}
        rowptr = [0]

        def prow(key, src, nrows):
            r0 = rowptr[0]
            if r0 // 128 != (r0 + nrows - 1) // 128:
                r0 = (r0 // 128 + 1) * 128
            t, r = divmod(r0, 128)
            p.dma("sp", stg[r:r + nrows, t, :], src, writes=["stg"])
            pcol[key] = r0
            rowptr[0] = r0 + nrows

        for s in range(nseq):
            prow(("c", s), c_d[s].rearrange("(r q) -> r q", q=128), 8)
        prow("cctx", cctx_d.rearrange("(r q) -> r q", q=128), 8)
        rowptr[0] = NCT * 128
        for l in range(L):
            rv = lambda v, a, b: v[l, a:b].rearrange("(r q) -> r q", q=128)
            prow((l, "n1g"), rv(norm1_g, 0, D), 8)
            prow((l, "bq"), rv(b_in, 0, 512), 4)
            prow((l, "bk"), rv(b_in, 512, 1024), 4)
            prow((l, "bo"), rv(b_in, 1536, 2048), 4)
            prow((l, "bsu"), rv(b_in, 2064, 2320), 2)
            prow((l, "bcv"), rv(b_in, 2576, 3088), 4)
            prow((l, "mg"), rv(mlstm_g, 0, 512), 4)
            prow((l, "slg"), rv(sgu_ln_g, 0, 256), 2)
            prow((l, "slb"), rv(sgu_ln_b, 0, 256), 2)
            prow((l, "cw"), conv_w[l].rearrange("k (t q) -> (k t) q", q=128), 62)
            prow((l, "cb"), rv(conv_b, 0, 256), 2)
            prow((l, "clg"), rv(conv_ln_g, 0, 256), 2)
            prow((l, "clb"), rv(conv_ln_b, 0, 256), 2)
            prow((l, "n2g"), rv(norm2_g, 0, D), 8)
            prow((l, "bmod"), rv(b_mod, 0, 6 * D), 48)
            rowptr[0] = (NCT + 2 * (l + 1)) * 128
        for t in range(NSTG):
            ps, pk = next_mm()
            p.op("pe", lambda e, t=t, ps=ps: e.transpose(ps[:, 0:128], stg[:, t, :], identf[:]),
                 reads=["stg", "identf"], writes=[pk])
            p.op("dve", lambda e, t=t, ps=ps: e.tensor_copy(pT[:, t * 128:(t + 1) * 128], ps[:, 0:128]),
                 reads=[pk], writes=["pT"])

        def pc(key, j=0, n=1):
            c0 = pcol[key] + j
            return pT[:, c0:c0 + n]

        for l in range(L):
            p.dma("sp", gbias[:, l, :], b_in[l, 2048:2064].partition_broadcast(128), writes=["gbias"])
        brow_i = [0]

        def load_brow(l, blocks):
            i = brow_i[0] % 2
            brow_i[0] += 1
            o = 0
            for c0, n in blocks:
                p.dma("sp", browf[0:1, o:o + n], b_in[l:l + 1, c0:c0 + n], writes=["browf"])
                o += n
            p.op("dve", lambda e: e.tensor_copy(browt[i][0:1, :], browf[0:1, :]), reads=["browf"], writes=[("brow", i)])
            return i

        wl_i = [0]

        def wload(parts):
            i = wl_i[0] % 3
            wl_i[0] += 1
            for dst_fn, src in parts:
                p.dma("pool", dst_fn(WSL[i]), src, writes=[("W", i)])
            return i

        def wview(i, a, b):
            return WSL[i][:, 0:a * b].rearrange("p (a b) -> p a b", a=a)

        def cols_src(w, l, c0, n):
            return w[l, :, c0:c0 + n].rearrange("(k q) n -> q k n", q=128)

        def wload_cols(w, l, blocks):
            tot = sum(n for _, n in blocks)
            parts = []
            o = 0
            for c0, n in blocks:
                parts.append((lambda t, o=o, n=n, tot=tot: t[:, 0:8 * tot].rearrange("p (k n) -> p k n", k=8)[:, :, o:o + n],
                              cols_src(w, l, c0, n)))
                o += n
            return wload(parts), tot

        def run_tasks(tasks, ahead=2):
            loaded = []
            nxt = 0
            for i in range(len(tasks)):
                while nxt < len(tasks) and nxt <= i + ahead - 1:
                    loaded.append(tasks[nxt][0]())
                    nxt += 1
                tasks[i][1](loaded[i])

        TT_ALL = [(0, 256, True)] + [(256 + 512 * i, 512, False) for i in range(4)]

        def mm_group(ps_ap, pairs, reads, wkey):
            def fn(e):
                n = len(pairs)
                for i, (a, b) in enumerate(pairs):
                    ins = e.matmul(ps_ap, lhsT=a, rhs=b, start=(i == 0), stop=(i == n - 1))
                return ins
            p.op("pe", fn, reads=reads, writes=[wkey])

        def norm_to_hT(s, l, A, shj, tiles):
            for (c0, n, is_ctx) in tiles:
                w = 1 if is_ctx else 0
                for k in range(8):
                    p.op("act", lambda e, k=k: e.activation(out=sq[k % 2][:, 0:n], in_=xT[:, k, c0:c0 + n], func=AF.Square),
                         reads=[("xT", k, c0)], writes=[("sq", k % 2)])
                    p.op("pe", lambda e, k=k: e.matmul(ST0[:, 0:n], lhsT=onesb[:], rhs=sq[k % 2][:, 0:n], start=(k == 0), stop=(k == 7)),
                         reads=[("sq", k % 2), "onesb"], writes=["st0"])
                p.op("act", lambda e: e.activation(out=rstd[:, 0:n], in_=ST0[:, 0:n], func=AF.Sqrt, bias=D * EPS, scale=1.0),
                     reads=["st0"], writes=["rstd"])
                p.op("dve", lambda e: e.reciprocal(out=rstd[:, 0:n], in_=rstd[:, 0:n]), reads=["rstd"], writes=["rstd"])
                for k in range(8):
                    tb = tmpA[k % 2]
                    p.op("dve", lambda e, k=k, tb=tb: e.scalar_tensor_tensor(out=tb[:, 0:n], in0=xT[:, k, c0:c0 + n], scalar=A[:, l * 8 + k, w:w + 1],
                                                                              in1=rstd[:, 0:n], op0=ALU.mult, op1=ALU.mult),
                         reads=[("xT", k, c0), "rstd", "A"], writes=[("tmpA", k % 2)])
                    p.op("act", lambda e, k=k, tb=tb: e.activation(out=hT[:, k, c0:c0 + n], in_=tb[:, 0:n], func=AF.Identity,
                                                                    bias=modT[:, l * 48 + shj + k, w:w + 1], scale=1.0),
                         reads=[("tmpA", k % 2), "modT"], writes=[("hT", c0)])

        def resid_update(s, l, gj, ps, pk, m, c0, n, is_ctx):
            w = 1 if is_ctx else 0
            p.op("dve", lambda e: e.scalar_tensor_tensor(out=xT[:, m, c0:c0 + n], in0=ps[:, 0:n], scalar=modT[:, l * 48 + gj + m, w:w + 1],
                                                         in1=xT[:, m, c0:c0 + n], op0=ALU.mult, op1=ALU.add),
                 reads=[pk, "modT", ("xT", m, c0)], writes=[("xT", m, c0)])

        def wout_part(s, l, r0, nkt, yfn, ykeys, tiles):
            def load():
                return wload([(lambda t: t[:, 0:nkt * 1024].rearrange("p (k n) -> p k n", k=nkt),
                               w_out[l, r0:r0 + 128 * nkt, :].rearrange("(k q) n -> q k n", q=128))])

            def comp(i):
                wv = WSL[i][:, 0:nkt * 1024].rearrange("p (k n) -> p k n", k=nkt)
                for (c0, n, is_ctx) in tiles:
                    for m in range(8):
                        ps, pk = next_mm()
                        mm_group(ps[:, 0:n], [(wv[:, kt, m * 128:(m + 1) * 128], yfn(kt, c0, n)) for kt in range(nkt)],
                                 [("W", i)] + ykeys, pk)
                        resid_update(s, l, 16, ps, pk, m, c0, n, is_ctx)
            return (load, comp)

        for s in range(nseq):
            p.barrier()
            for blk in range(NCH):
                xi = xin[blk % 2]
                src = ctx_d[s, blk * 128:(blk + 1) * 128, :] if blk < 2 else x_d[s, (blk - 2) * 128:(blk - 1) * 128, :]
                p.dma("sp", xi, src, writes=[("xin", blk % 2)])
                for half in range(2):
                    ps, pk = next_mm()

                    def tr(e, ps=ps, xi=xi, half=half):
                        for j in range(4):
                            k = half * 4 + j
                            ins = e.transpose(ps[:, j * 128:(j + 1) * 128], xi[:, k * 128:(k + 1) * 128], identf[:])
                        return ins
                    p.op("pe", tr, reads=[("xin", blk % 2), "identf"], writes=[pk])
                    eng = "act" if half == 0 else "dve"
                    outv = xT[:, half * 4:half * 4 + 4, blk * 128:(blk + 1) * 128]
                    inv = ps[:].rearrange("p (j n) -> p j n", j=4)
                    c0t = 0 if blk < 2 else 256 + ((blk - 2) // 4) * 512
                    wk = [("xT", half * 4 + j, c0t) for j in range(4)]
                    if eng == "act":
                        p.op("act", lambda e, outv=outv, inv=inv: e.copy(outv, inv), reads=[pk], writes=wk)
                    else:
                        p.op("dve", lambda e, outv=outv, inv=inv: e.tensor_copy(outv, inv), reads=[pk], writes=wk)
            p.barrier()
            for j, key in enumerate((("c", s), "cctx")):
                p.op("act", lambda e, j=j, key=key: e.activation(out=cact[:, :, j], in_=pc(key, 0, 8), func=AF.Silu),
                     reads=["pT"], writes=["cact"])
            for l in range(L):
                if s == 0 or True:
                    tasks = []
                    for g in range(12):
                        def load(g=g, l=l):
                            return wload_cols(w_mod, l, [(g * 512, 512)])

                        def comp(info, g=g, l=l):
                            i, tot = info
                            wv = wview(i, 8, 512)
                            for m in range(4):
                                j = g * 4 + m
                                mm_group(AUX[0][:, 0:2], [(wv[:, k, m * 128:(m + 1) * 128], cact[:, k, :]) for k in range(8)],
                                         [("W", i), "cact"], "aux0")
                                p.op("act", lambda e, j=j, l=l: e.activation(out=modT[:, l * 48 + j, :], in_=AUX[0][:, 0:2], func=AF.Identity,
                                                                        bias=pc((l, "bmod"), j), scale=1.0),
                                     reads=["aux0", "pT"], writes=["modT"])
                        tasks.append((load, comp))
                    run_tasks(tasks)
                    for (Atile, gkey, scj) in ((A1, "n1g", 8), (A2, "n2g", 32)):
                        p.op("dve", lambda e, Atile=Atile, scj=scj, l=l: e.tensor_scalar(out=Atile[:, l * 8:(l + 1) * 8, :], in0=modT[:, l * 48 + scj:l * 48 + scj + 8, :], scalar1=1.0, scalar2=32.0,
                                                                                   op0=ALU.add, op1=ALU.mult),
                             reads=["modT"], writes=["A"])
                        p.op("dve", lambda e, Atile=Atile, gkey=gkey, l=l: e.tensor_tensor(out=Atile[:, l * 8:(l + 1) * 8, :], in0=Atile[:, l * 8:(l + 1) * 8, :],
                                                                                     in1=pc((l, gkey), 0, 8).unsqueeze(2).broadcast_to([128, 8, 2]),
                                                                                     op=ALU.mult),
                             reads=["A", "pT"], writes=["A"])
                    p.op("dve", lambda e, l=l: e.tensor_scalar(out=bks[:, l * 4:l * 4 + 4], in0=pc((l, "bk"), 0, 4), scalar1=KS, scalar2=None, op0=ALU.mult),
                         reads=["pT"], writes=["bks"])


            for l in range(L):
                last = (l == L - 1)
                FM_T = TT_ALL[1:] if last else TT_ALL
                fm_chunks = range(2, NCH) if last else range(NCH)

                if "norm1" in DBG:
                    norm_to_hT(s, l, A1, 0, TT_ALL)

                p.barrier()
                p.op("pool", lambda e: e.memset(ypad0[:], 0.0), writes=["ypad0"])
                p.op("pool", lambda e: e.memset(ypad1[:], 0.0), writes=["ypad1"])
                p.op("pool", lambda e: e.memset(yctx[:], 0.0), writes=["yctx"])
                for j in range(62):
                    p.op("pool" if j % 2 else "dve", lambda e, j=j: e.tensor_scalar(out=diag[:, j, :], in0=identb[:], scalar1=pc((l, "cw"), j), scalar2=None, op0=ALU.mult),
                         reads=["identb", "pT"], writes=[("diag", j)])

                def conv_load():
                    return wload_cols(w_in, l, [(2576, 512)])

                def conv_comp(info):
                    i, tot = info
                    wv = wview(i, 8, 512)
                    for ti, (c0, n, is_ctx) in enumerate(FM_T):
                        for ct in range(2):
                            psa, pka = next_mm()
                            mm_group(psa[:, 0:n], [(wv[:, k, ct * 128:(ct + 1) * 128], hT[:, k, c0:c0 + n]) for k in range(8)],
                                     [("W", i), ("hT", c0)], pka)
                            psg, pkg = next_mm()
                            mm_group(psg[:, 0:n], [(wv[:, k, 256 + ct * 128:256 + (ct + 1) * 128], hT[:, k, c0:c0 + n]) for k in range(8)],
                                     [("W", i), ("hT", c0)], pkg)
                            tb = tmpB[ct]
                            p.op("act", lambda e, tb=tb, psg=psg, ct=ct: e.activation(out=tb[:, 0:n], in_=psg[:, 0:n], func=AF.Sigmoid,
                                                                                     bias=pc((l, "bcv"), 2 + ct), scale=1.0),
                                 reads=[pkg, "pT"], writes=[("tmpB", ct)])
                            if is_ctx:
                                outv = yctx[:, ct, 15:15 + 256]
                                in0 = psa[:, 0:n]
                                in1 = tb[:, 0:n]
                                wk = "yctx"
                            elif ct == 0:
                                r0 = (c0 - 256) // 64
                                outv = ypad0[:, r0:r0 + 8, 15:79]
                                in0 = psa[:].rearrange("p (r c) -> p r c", c=64)
                                in1 = tb[:].rearrange("p (r c) -> p r c", c=64)
                                wk = "ypad0"
                            else:
                                r0 = (c0 - 256) // 64
                                outv = ypad1[:, 15 + r0:15 + r0 + 8, :]
                                in0 = psa[:].rearrange("p (r c) -> p r c", c=64)
                                in1 = tb[:].rearrange("p (r c) -> p r c", c=64)
                                wk = "ypad1"
                            p.op("dve", lambda e, outv=outv, in0=in0, in1=in1, ct=ct: e.scalar_tensor_tensor(
                                out=outv, in0=in0, scalar=pc((l, "bcv"), ct), in1=in1, op0=ALU.add, op1=ALU.mult),
                                reads=[pka, ("tmpB", ct), "pT", wk], writes=[wk])
                    for ti, (c0, n, is_ctx) in enumerate(FM_T):
                        pss = []
                        for ct in range(2):
                            ps, pk = next_mm()
                            pairs = []
                            for k in range(31):
                                if is_ctx:
                                    rhs = yctx[:, ct, k:k + 256]
                                elif ct == 0:
                                    r0 = (c0 - 256) // 64
                                    rhs = ypad0[:, r0:r0 + 8, k:k + 64]
                                else:
                                    r0 = (c0 - 256) // 64
                                    rhs = ypad1[:, r0 + k:r0 + k + 8, :]
                                pairs.append((diag[:, 2 * k + ct, :], rhs))
                            mm_group(ps[:, 0:n], pairs, ["yctx", "ypad0", "ypad1"] + [("diag", 2 * k + ct) for k in range(31)], pk)
                            pss.append((ps, pk))
                        for ct in range(2):
                            ps, pk = pss[ct]
                            p.op("act", lambda e, ps=ps, ct=ct: e.activation(out=cvb[ct][:, 0:n], in_=ps[:, 0:n], func=AF.Identity,
                                                                             bias=pc((l, "cb"), ct), scale=1.0),
                                 reads=[pk, "pT"], writes=[("cvb", ct)])
                            p.op("act", lambda e, ps=ps, ct=ct: e.activation(out=sq[ct][:, 0:n], in_=ps[:, 0:n], func=AF.Square,
                                                                             bias=pc((l, "cb"), ct), scale=1.0),
                                 reads=[pk, "pT"], writes=[("sq", ct)])
                        mm_group(ST0[:, 0:n], [(onesb[:], cvb[ct][:, 0:n]) for ct in range(2)], [("cvb", 0), ("cvb", 1), "onesb"], "st0")
                        mm_group(AUX[0][:, 0:n], [(onesb[:], sq[ct][:, 0:n]) for ct in range(2)], [("sq", 0), ("sq", 1), "onesb"], "aux0")
                        mean, msq = tmpA[0], tmpA[1]
                        p.op("dve", lambda e: e.tensor_scalar(out=mean[:, 0:n], in0=ST0[:, 0:n], scalar1=1.0 / 256, scalar2=None, op0=ALU.mult),
                             reads=["st0"], writes=[("tmpA", 0)])
                        p.op("dve", lambda e: e.tensor_tensor(out=msq[:, 0:n], in0=mean[:, 0:n], in1=mean[:, 0:n], op=ALU.mult),
                             reads=[("tmpA", 0)], writes=[("tmpA", 1)])
                        p.op("dve", lambda e: e.scalar_tensor_tensor(out=msq[:, 0:n], in0=AUX[0][:, 0:n], scalar=1.0 / 256, in1=msq[:, 0:n],
                                                                     op0=ALU.mult, op1=ALU.subtract),
                             reads=["aux0", ("tmpA", 1)], writes=[("tmpA", 1)])
                        p.op("act", lambda e: e.activation(out=rstd[:, 0:n], in_=msq[:, 0:n], func=AF.Sqrt, bias=EPS, scale=1.0),
                             reads=[("tmpA", 1)], writes=["rstd"])
                        p.op("dve", lambda e: e.reciprocal(out=rstd[:, 0:n], in_=rstd[:, 0:n]), reads=["rstd"], writes=["rstd"])
                        for ct in range(2):
                            tb = tmpB[ct]
                            p.op("dve", lambda e, tb=tb, ct=ct: e.tensor_tensor(out=tb[:, 0:n], in0=cvb[ct][:, 0:n], in1=mean[:, 0:n], op=ALU.subtract),
                                 reads=[("cvb", ct), ("tmpA", 0)], writes=[("tmpB", ct)])
                            p.op("dve", lambda e, tb=tb: e.tensor_tensor(out=tb[:, 0:n], in0=tb[:, 0:n], in1=rstd[:, 0:n], op=ALU.mult),
                                 reads=[("tmpB", ct), "rstd"], writes=[("tmpB", ct)])
                            p.op("act", lambda e, tb=tb, ct=ct: e.activation(out=ycT[:, ct, c0:c0 + n], in_=tb[:, 0:n], func=AF.Silu,
                                                                             bias=pc((l, "clb"), ct), scale=pc((l, "clg"), ct)),
                                 reads=[("tmpB", ct), "pT"], writes=[("ycT", c0)])

                if "conv" in DBG:
                  run_tasks([(conv_load, conv_comp),
                           wout_part(s, l, 768, 2, lambda kt, c0, n: ycT[:, kt, c0:c0 + n], [("ycT", t[0]) for t in FM_T], FM_T)])

                p.barrier()
                p.op("pool", lambda e: e.memset(vhatZ, 0.0), writes=["vhatZ"])
                for g in range(4):
                    p.dma("sp", wsw, sgu_w[l, g], writes=[("tmpA", 0)])
                    ps, pk = next_mm()
                    p.op("pe", lambda e, ps=ps: e.transpose(ps[:, 0:128], wsw, identf[:]), reads=[("tmpA", 0), "identf"], writes=[pk])
                    p.op("dve", lambda e, ps=ps, g=g: e.tensor_copy(WsT[:, g, :], ps[:, 0:128]), reads=[pk], writes=["WsT"])
                    p.dma("sp", sb_bc[(g % 2) * 64:(g % 2) * 64 + 64, g // 2, :], sgu_b[l, g, :].partition_broadcast(64), writes=["sb_bc"])
                for j in range(2):
                    mm_group(AUX[0][:, 0:128], [(onesZ[:, e_, :], WsT[:, 2 * j + e_, :]) for e_ in range(2)], ["onesZ", "WsT"], "aux0")
                    p.op("dve", lambda e, j=j: e.scalar_tensor_tensor(out=ADD[:, j, :], in0=AUX[0][:, 0:128], scalar=pc((l, "slb"), j),
                                                                      in1=sb_bc[:, j, :], op0=ALU.mult, op1=ALU.add),
                         reads=["aux0", "pT", "sb_bc"], writes=["sb_bc"])

                def sgu_load():
                    return wload_cols(w_in, l, [(2048, 528)])

                def sgu_comp(info):
                    i, tot = info
                    wv = wview(i, 8, 528)
                    bri = load_brow(l, [(2320, 256)])
                    for (c0, n, is_ctx) in FM_T:
                        for m in range(2):
                            ps, pk = next_mm()
                            mm_group(ps[:, 0:n], [(wv[:, k, 16 + m * 128:16 + (m + 1) * 128], hT[:, k, c0:c0 + n]) for k in range(8)],
                                     [("W", i), ("hT", c0)], pk)
                            p.op("act", lambda e, ps=ps, m=m: e.activation(out=uT[:, m, c0:c0 + n], in_=ps[:, 0:n], func=AF.Gelu,
                                                                           bias=pc((l, "bsu"), m), scale=1.0),
                                 reads=[pk, "pT"], writes=[("uT", c0)])
                    for c in range(NCH):
                        cc = c * 128
                        c0t = 0 if c < 2 else 256 + ((c - 2) // 4) * 512
                        mm_group(AUX[1][:, 0:16], [(hT[:, k, cc:cc + 128], wv[:, k, 0:16]) for k in range(8)], [("W", i), ("hT", c0t)], "aux1")
                        p.op("dve", lambda e, c=c: e.tensor_tensor(out=gates[:, c, :], in0=AUX[1][:, 0:16], in1=gbias[:, l, :], op=ALU.add),
                             reads=["aux1", "gbias"], writes=["gates"])
                        if c not in fm_chunks:
                            continue
                        ps, pk = next_mm()
                        mm_group(ps[:, 0:256], [(hT[:, k, cc:cc + 128], wv[:, k, 272:528]) for k in range(8)] + [(onesb[0:1, :], browt[bri][0:1, 0:256])],
                                 [("W", i), ("hT", c0t), ("brow", bri), "onesb"], pk)
                        p.op("act", lambda e, ps=ps: e.activation(out=vg, in_=ps[:, 0:256], func=AF.Gelu), reads=[pk], writes=["vg"])
                        p.op("act", lambda e: e.activation(out=junk, in_=vg, func=AF.Identity, accum_out=sm[:, 0:1]), reads=["vg"], writes=[("tmpA", 1), ("sm", 0)])
                        p.op("act", lambda e: e.activation(out=junk, in_=vg, func=AF.Square, accum_out=sm[:, 1:2]), reads=["vg", ("tmpA", 1)], writes=[("tmpA", 1), ("sm", 1)])
                        p.op("dve", lambda e: e.tensor_scalar(out=sm[:, 2:3], in0=sm[:, 0:1], scalar1=1.0 / 256, scalar2=None, op0=ALU.mult),
                             reads=[("sm", 0)], writes=[("sm", 2)])
                        p.op("dve", lambda e: e.tensor_tensor(out=sm[:, 3:4], in0=sm[:, 2:3], in1=sm[:, 2:3], op=ALU.mult),
                             reads=[("sm", 2)], writes=[("sm", 3)])
                        p.op("dve", lambda e: e.scalar_tensor_tensor(out=sm[:, 3:4], in0=sm[:, 1:2], scalar=1.0 / 256, in1=sm[:, 3:4], op0=ALU.mult, op1=ALU.subtract),
                             reads=[("sm", 1), ("sm", 3)], writes=[("sm", 3)])
                        p.op("act", lambda e: e.activation(out=sm[:, 4:5], in_=sm[:, 3:4], func=AF.Sqrt, bias=EPS, scale=1.0), reads=[("sm", 3)], writes=[("sm", 4)])
                        p.op("dve", lambda e: e.reciprocal(out=sm[:, 5:6], in_=sm[:, 4:5]), reads=[("sm", 4)], writes=[("sm", 5)])
                        p.op("dve", lambda e: e.scalar_tensor_tensor(out=sm[:, 6:7], in0=sm[:, 2:3], scalar=-1.0, in1=sm[:, 5:6], op0=ALU.mult, op1=ALU.mult),
                             reads=[("sm", 2), ("sm", 5)], writes=[("sm", 6)])
                        zv = vhatZ
                        zap = bass.AP(zv.tensor, zv.offset, [list(zv.ap[0]), [256, 2], [192, 2], [1, 64]])
                        p.op("act", lambda e, zap=zap: e.activation(out=zap, in_=vg.rearrange("p (a b c) -> p a b c", a=2, b=2), func=AF.Identity,
                                                                    scale=sm[:, 5:6], bias=sm[:, 6:7]),
                             reads=["vg", ("sm", 5), ("sm", 6), "vhatZ"], writes=["vhatZ"])
                        for j in range(2):
                            mm_group(AUX[2][:, j * 128:(j + 1) * 128], [(vhatZ[:, 2 * j + e_, :], WsT[:, 2 * j + e_, :]) for e_ in range(2)],
                                     ["vhatZ", "WsT"], ("aux2", j))
                            tb = tmpB[j]
                            p.op("dve", lambda e, j=j, tb=tb: e.scalar_tensor_tensor(out=tb[:, 0:128], in0=AUX[2][:, j * 128:(j + 1) * 128], scalar=pc((l, "slg"), j),
                                                                                    in1=ADD[:, j, :], op0=ALU.mult, op1=ALU.add),
                                 reads=[("aux2", j), "pT", "sb_bc"], writes=[("tmpB", j)])
                            p.op("pool", lambda e, j=j, tb=tb, cc=cc: e.tensor_tensor(out=ybT[:, j, cc:cc + 128], in0=tb[:, 0:128], in1=uT[:, j, cc:cc + 128], op=ALU.mult),
                                 reads=[("tmpB", j), ("uT", c0t)], writes=[("ybT", c0t)])

                if "sgu" in DBG:
                  run_tasks([(sgu_load, sgu_comp),
                           wout_part(s, l, 512, 2, lambda kt, c0, n: ybT[:, kt, c0:c0 + n], [("ybT", t[0]) for t in FM_T], FM_T)])

                gv = gates[:]
                if "gates" not in DBG:
                    continue
                _bf = gates[:, 0, 4:5]
                _bi = gates[:, 0, 0:1]
                gap_f = bass.AP(_bf.tensor, _bf.offset, [list(_bf.ap[0]), [8, 2], [16, NCH], [1, 4]])
                gap_i = bass.AP(_bi.tensor, _bi.offset, [list(_bi.ap[0]), [8, 2], [16, NCH], [1, 4]])
                p.op("act", lambda e: e.activation(out=SP[:], in_=gap_f, func=AF.Exp, scale=-1.0), reads=["gates"], writes=["SP"])
                p.op("act", lambda e: e.activation(out=SP[:], in_=SP[:], func=AF.Ln, bias=1.0, scale=1.0), reads=["SP"], writes=["SP"])
                spf = SP[:].rearrange("p d c h -> p (d c h)")
                p.op("pe", lambda e: e.matmul(AUX[0][:, 0:72], lhsT=maskU[:], rhs=spf[:, 0:72], start=True, stop=True), reads=["SP", "maskU"], writes=["aux0"])
                p.op("pe", lambda e: e.matmul(AUX[0][:, 72:144], lhsT=maskL[:], rhs=spf[:, 72:144], start=True, stop=True), reads=["SP", "maskL"], writes=["aux0"])
                p.op("pe", lambda e: e.matmul(AUX[0][:, 144:288], lhsT=onesf[:], rhs=spf[:, 0:144], start=True, stop=True), reads=["SP", "onesf"], writes=["aux0"])
                flat = lambda t: t[:].rearrange("p d c h -> p (d c h)")
                p.op("act", lambda e: e.activation(out=flat(EB), in_=AUX[0][:, 0:144], func=AF.Exp, scale=-1.0), reads=["aux0"], writes=["EB"])
                p.op("act", lambda e: e.activation(out=flat(ETOT), in_=AUX[0][:, 144:288], func=AF.Exp, scale=-1.0), reads=["aux0"], writes=["ETOT"])
                p.op("dve", lambda e: e.tensor_tensor(out=WS[:], in0=AUX[0][:, 0:144].rearrange("p (d c h) -> p d c h", d=2, c=NCH), in1=gap_i, op=ALU.add),
                     reads=["aux0", "gates"], writes=["WS"])
                p.op("act", lambda e: e.activation(out=flat(WS), in_=flat(WS), func=AF.Exp), reads=["WS"], writes=["WS"])
                p.op("dve", lambda e: e.tensor_tensor(out=flat(WSE), in0=flat(WS), in1=flat(ETOT), op=ALU.mult), reads=["WS", "ETOT", "SP"], writes=["WSE", "SP"])

                for h in range(4):
                    p.barrier()
                    if "h_ms1" in DBG:
                        p.op("pool", lambda e: e.memset(vaug_full, 1.0), writes=["vaug1"] + [("vaug", c_) for c_ in range(NCH)])

                    def head_load(h=h):
                        parts = []
                        for b_, c0_ in enumerate((h * 128, 512 + h * 128, 1024 + h * 128, 1536 + h * 128)):
                            parts.append((lambda t, b_=b_: t[:, b_ * 1024:(b_ + 1) * 1024].rearrange("p (k n) -> p k n", k=8), cols_src(w_in, l, c0_, 128)))
                        return wload(parts), 512

                    def head_comp(info, h=h):
                        i, tot = info
                        wb = [WSL[i][:, b_ * 1024:(b_ + 1) * 1024].rearrange("p (k n) -> p k n", k=8) for b_ in range(4)]
                        for (c0, n, is_ctx) in (FM_T if "h_fm" in DBG else []):
                            ps, pk = next_mm()
                            mm_group(ps[:, 0:n], [(wb[0][:, k, :], hT[:, k, c0:c0 + n]) for k in range(8)], [("W", i), ("hT", c0)], pk)
                            p.op("act", lambda e, ps=ps: e.activation(out=qT[:, c0:c0 + n], in_=ps[:, 0:n], func=AF.Identity, bias=pc((l, "bq"), h), scale=1.0),
                                 reads=[pk, "pT"], writes=[("qT", c0)])
                            ps, pk = next_mm()
                            mm_group(ps[:, 0:n], [(wb[1][:, k, :], hT[:, k, c0:c0 + n]) for k in range(8)], [("W", i), ("hT", c0)], pk)
                            p.op("act", lambda e, ps=ps: e.activation(out=kT[:, c0:c0 + n], in_=ps[:, 0:n], func=AF.Identity, bias=bks[:, l * 4 + h:l * 4 + h + 1], scale=KS),
                                 reads=[pk, "bks"], writes=[("kT", c0)])
                            ps, pk = next_mm()
                            mm_group(ps[:, 0:n], [(wb[3][:, k, :], hT[:, k, c0:c0 + n]) for k in range(8)], [("W", i), ("hT", c0)], pk)
                            p.op("act", lambda e, ps=ps: e.activation(out=sigoT[:, c0:c0 + n], in_=ps[:, 0:n], func=AF.Sigmoid, bias=pc((l, "bo"), h), scale=1.0),
                                 reads=[pk, "pT"], writes=[("sigoT", c0)])
                        bri = load_brow(l, [(512 + h * 128, 128), (1024 + h * 128, 128)])
                        for c in range(NCH if "h_tm" in DBG else 0):
                            cc = c * 128
                            c0t = 0 if c < 2 else 256 + ((c - 2) // 4) * 512
                            ps, pk = next_mm()
                            mm_group(ps[:, 0:128], [(hT[:, k, cc:cc + 128], wb[1][:, k, :]) for k in range(8)] + [(onesb[0:1, :], browt[bri][0:1, 0:128])],
                                     [("W", i), ("hT", c0t), ("brow", bri), "onesb"], pk)
                            mm_group(ps[:, 128:256], [(hT[:, k, cc:cc + 128], wb[2][:, k, :]) for k in range(8)] + [(onesb[0:1, :], browt[bri][0:1, 128:256])],
                                     [("W", i), ("hT", c0t), ("brow", bri), "onesb"], pk)
                            p.op("act", lambda e, ps=ps, c=c: e.activation(out=ktm[:, c, :], in_=ps[:, 0:128], func=AF.Identity, scale=KS),
                                 reads=[pk], writes=[("ktm", c)])
                            p.op("dve", lambda e, ps=ps, c=c: e.tensor_copy(vaug[:, c, 0:128], ps[:, 128:256]), reads=[pk], writes=[("vaug", c)])

                        Df = Dst[0]
                        if "h_ms2" in DBG:
                            p.op("pool", lambda e: e.memset(Df, 0.0), writes=["Df"])
                        for c in range(NCH if "h_fwd" in DBG else 0):
                            p.op("act", lambda e, c=c: e.copy(Dfb[:, c, :], Df), reads=["Df"], writes=[("Dfb", c)])
                            if c == NCH - 1:
                                break
                            kb = kk[c % 2]
                            p.op("dve", lambda e, c=c, kb=kb: e.tensor_scalar(out=kb, in0=ktm[:, c, :], scalar1=WSE[:, 0, c, h:h + 1], scalar2=None, op0=ALU.mult),
                                 reads=[("ktm", c), "WSE"], writes=[("kk", c % 2)])
                            ps_, pk_ = next_mm()
                            sl = ps_[:, 0:129]
                            mm_group(sl, [(kb, vaug[:, c, :])], [("kk", c % 2), ("vaug", c), "vaug1"], pk_)
                            p.op("dve", lambda e, c=c, sl=sl: e.scalar_tensor_tensor(out=Df, in0=Df, scalar=ETOT[:, 0, c, h:h + 1], in1=sl,
                                                                                    op0=ALU.mult, op1=ALU.add),
                                 reads=["Df", "ETOT", pk_], writes=["Df"])

                        Db = Dst[1]
                        if "h_ms2" in DBG:
                            p.op("pool", lambda e: e.memset(Db, 0.0), writes=["Db"])
                        order = [1, 0] + list(range(NCH - 1, 1, -1))
                        for it, c in enumerate(order if ("h_out" in DBG or "h_bwd" in DBG) else []):
                            cc = c * 128
                            c0t = 0 if c < 2 else 256 + ((c - 2) // 4) * 512
                            need_out = (c in fm_chunks) and ("h_out" in DBG)
                            if need_out:
                                p.op("act", lambda e: e.copy(Dbb, Db), reads=["Db"], writes=["Dbb"])
                                a = it % 2
                                mm_group(AUX[0][:, a * 128:(a + 1) * 128], [(kT[:, cc:cc + 128], qT[:, cc:cc + 128])], [("kT", c0t), ("qT", c0t)], ("aux0s", a))
                                p.op("dve", lambda e, a=a, c=c: e.scalar_tensor_tensor(out=Aft[a], in0=AUX[0][:, a * 128:(a + 1) * 128], scalar=WS[:, 0, c, h:h + 1],
                                                                                      in1=maskU[:], op0=ALU.mult, op1=ALU.mult),
                                     reads=[("aux0s", a), "WS", "maskU"], writes=[("Aft", a)])
                                p.op("dve", lambda e, a=a, c=c: e.scalar_tensor_tensor(out=Abt[a], in0=AUX[0][:, a * 128:(a + 1) * 128], scalar=WS[:, 1, c, h:h + 1],
                                                                                      in1=maskL[:], op0=ALU.mult, op1=ALU.mult),
                                     reads=[("aux0s", a), "WS", "maskL"], writes=[("Abt", a)])
                                po = AUX[1 + a]
                                pko = ("auxo", a)
                                mm_group(po[:, 0:129], [(Aft[a], vaug[:, c, :]), (qT[:, cc:cc + 128], Dfb[:, c, :])],
                                         [("Aft", a), ("vaug", c), "vaug1", ("qT", c0t), ("Dfb", c)], pko)
                                p.op("pe", lambda e, po=po, a=a, c=c, cc=cc: [e.matmul(po[:, 256:385], lhsT=Abt[a], rhs=vaug[:, c, :], start=True, stop=False),
                                                                              e.matmul(po[:, 256:385], lhsT=qT[:, cc:cc + 128], rhs=Dbb, start=False, stop=True)][-1],
                                     reads=[("Abt", a), ("vaug", c), "vaug1", ("qT", c0t), "Dbb"], writes=[("auxob", a)])
                                ebv = EB[:, :, c, h]
                                _b = po[:, 128:129]
                                denv = bass.AP(_b.tensor, _b.offset, [list(_b.ap[0]), [256, 2]])
                                p.op("dve", lambda e, ebv=ebv, denv=denv: e.tensor_tensor(out=sm[:, 8:10], in0=denv, in1=ebv, op=ALU.mult),
                                     reads=[pko, ("auxob", a), "EB"], writes=[("sm", 8)])
                                p.op("dve", lambda e: e.tensor_scalar(out=sm[:, 10:12], in0=sm[:, 8:10], scalar1=-1.0, scalar2=None, op0=ALU.mult),
                                     reads=[("sm", 8)], writes=[("sm", 10)])
                                p.op("dve", lambda e: e.tensor_tensor(out=sm[:, 10:12], in0=sm[:, 10:12], in1=sm[:, 8:10], op=ALU.max),
                                     reads=[("sm", 8), ("sm", 10)], writes=[("sm", 10)])
                                p.op("dve", lambda e: e.tensor_scalar_max(out=sm[:, 10:12], in0=sm[:, 10:12], scalar1=1.0), reads=[("sm", 10)], writes=[("sm", 10)])
                                p.op("dve", lambda e: e.reciprocal(out=sm[:, 10:12], in_=sm[:, 10:12]), reads=[("sm", 10)], writes=[("sm", 10)])
                                p.op("dve", lambda e, ebv=ebv: e.tensor_tensor(out=sm[:, 12:14], in0=sm[:, 10:12], in1=ebv, op=ALU.mult),
                                     reads=[("sm", 10), "EB"], writes=[("sm", 12)])
                                tb = tmpA[0]
                                p.op("act", lambda e, po=po, tb=tb: e.activation(out=tb[:, 0:128], in_=po[:, 0:128], func=AF.Identity, scale=sm[:, 12:13]),
                                     reads=[pko, ("sm", 12)], writes=[("tmpA", 0)])
                                p.op("dve", lambda e, po=po, tb=tb: e.scalar_tensor_tensor(out=hsum, in0=po[:, 256:384], scalar=sm[:, 13:14], in1=tb[:, 0:128],
                                                                                          op0=ALU.mult, op1=ALU.add),
                                     reads=[("auxob", a), ("sm", 12), ("tmpA", 0)], writes=["hsum"])
                                p.op("act", lambda e: e.activation(out=junk[:, 0:128], in_=hsum, func=AF.Square, accum_out=sm[:, 14:15]),
                                     reads=["hsum"], writes=[("tmpA", 1), ("sm", 14)])
                                p.op("act", lambda e: e.activation(out=sm[:, 15:16], in_=sm[:, 14:15], func=AF.Sqrt, bias=EPS, scale=1.0 / 128),
                                     reads=[("sm", 14)], writes=[("sm", 15)])
                                p.op("dve", lambda e: e.reciprocal(out=sm[:, 16:17], in_=sm[:, 15:16]), reads=[("sm", 15)], writes=[("sm", 16)])
                                p.op("act", lambda e, a=a: e.activation(out=hn[a], in_=hsum, func=AF.Identity, scale=sm[:, 16:17]),
                                     reads=["hsum", ("sm", 16)], writes=[("hn", a)])
                                p.op("pe", lambda e, a=a: e.transpose(TP[:, a * 128:(a + 1) * 128], hn[a], identb[:]), reads=[("hn", a), "identb"], writes=[("tp", a)])
                                p.op("dve", lambda e, a=a, cc=cc: e.scalar_tensor_tensor(out=yaT[:, cc:cc + 128], in0=TP[:, a * 128:(a + 1) * 128], scalar=pc((l, "mg"), h),
                                                                                        in1=sigoT[:, cc:cc + 128], op0=ALU.mult, op1=ALU.mult),
                                     reads=[("tp", a), "pT", ("sigoT", c0t)], writes=[("yaT", c0t)])
                            if it == len(order) - 1 or "h_bwd" not in DBG:
                                continue
                            kb = kk[it % 2]
                            p.op("dve", lambda e, c=c, kb=kb: e.tensor_scalar(out=kb, in0=ktm[:, c, :], scalar1=WSE[:, 1, c, h:h + 1], scalar2=None, op0=ALU.mult),
                                 reads=[("ktm", c), "WSE"], writes=[("kk", it % 2)])
                            ps_, pk_ = next_mm()
                            sl = ps_[:, 0:129]
                            mm_group(sl, [(kb, vaug[:, c, :])], [("kk", it % 2), ("vaug", c), "vaug1"], pk_)
                            p.op("dve", lambda e, c=c, sl=sl: e.scalar_tensor_tensor(out=Db, in0=Db, scalar=ETOT[:, 1, c, h:h + 1], in1=sl,
                                                                                    op0=ALU.mult, op1=ALU.add),
                                 reads=["Db", "ETOT", pk_, "Dbb"], writes=["Db"])

                    if "heads" in DBG:
                      run_tasks([(head_load, head_comp)] +
                               ([wout_part(s, l, h * 128, 1, lambda kt, c0, n: yaT[:, c0:c0 + n], [("yaT", t[0]) for t in FM_T], FM_T)] if "h_wout" in DBG else []))

                p.barrier()
                if "ffn" in DBG:
                    norm_to_hT(s, l, A2, 24, FM_T)
                if last:
                    supers = [FM_T[0:2], FM_T[2:4]]
                else:
                    supers = [TT_ALL[0:2], TT_ALL[2:4], TT_ALL[4:5]]
                for st_tiles in (supers if "ffn" in DBG else []):
                    offs = []
                    o = 0
                    for (c0, n, is_ctx) in st_tiles:
                        offs.append(o)
                        o += n
                    tasks = []
                    for fg in range(11):
                        nf = 2

                        def load(fg=fg, nf=nf):
                            return wload_cols(w_gu, l, [(fg * 256, nf * 128), (FH + fg * 256, nf * 128)])

                        def comp(info, fg=fg, nf=nf):
                            i, tot = info
                            wv = wview(i, 8, tot)
                            for ti, (c0, n, is_ctx) in enumerate(st_tiles):
                                for fi in range(nf):
                                    f = fg * 2 + fi
                                    psg, pkg = next_mm()
                                    mm_group(psg[:, 0:n], [(wv[:, k, fi * 128:(fi + 1) * 128], hT[:, k, c0:c0 + n]) for k in range(8)], [("W", i), ("hT", c0)], pkg)
                                    psu, pku = next_mm()
                                    mm_group(psu[:, 0:n], [(wv[:, k, (nf + fi) * 128:(nf + fi + 1) * 128], hT[:, k, c0:c0 + n]) for k in range(8)], [("W", i), ("hT", c0)], pku)
                                    tb = tmpB[f % 2]
                                    p.op("act", lambda e, psg=psg, tb=tb: e.activation(out=tb[:, 0:n], in_=psg[:, 0:n], func=AF.Silu), reads=[pkg], writes=[("tmpB", f % 2)])
                                    p.op("dve", lambda e, psu=psu, tb=tb, f=f, ti=ti: e.tensor_tensor(out=actT[:, f, offs[ti]:offs[ti] + n], in0=psu[:, 0:n], in1=tb[:, 0:n], op=ALU.mult),
                                         reads=[pku, ("tmpB", f % 2)], writes=[("actT", f, ti)])
                        tasks.append((load, comp))
                    for m in range(8):
                        def load(m=m):
                            return wload([(lambda t: t[:, 0:22 * 128].rearrange("p (k n) -> p k n", k=22),
                                           w_down[l, :, m * 128:(m + 1) * 128].rearrange("(k q) n -> q k n", q=128))])

                        def comp(i, m=m):
                            wv = WSL[i][:, 0:22 * 128].rearrange("p (k n) -> p k n", k=22)
                            for ti, (c0, n, is_ctx) in enumerate(st_tiles):
                                ps, pk = next_mm()
                                mm_group(ps[:, 0:n], [(wv[:, f, :], actT[:, f, offs[ti]:offs[ti] + n]) for f in range(22)],
                                         [("W", i)] + [("actT", f, ti) for f in range(22)], pk)
                                resid_update(s, l, 40, ps, pk, m, c0, n, is_ctx)
                        tasks.append((load, comp))
                    run_tasks(tasks)

            p.barrier()
            p.dma("sp", fgb, final_g.partition_broadcast(128), writes=["fgb"])
            for blk in range(16):
                cc = 256 + blk * 128
                b2 = blk % 2
                for half in range(2):
                    ps, pk = next_mm()

                    def tr(e, ps=ps, half=half, cc=cc):
                        for j in range(4):
                            k = half * 4 + j
                            ins = e.transpose(ps[:, j * 128:(j + 1) * 128], xT[:, k, cc:cc + 128], identf[:])
                        return ins
                    c0t = 256 + (blk // 4) * 512
                    p.op("pe", tr, reads=[("xT", half * 4 + j, c0t) for j in range(4)] + ["identf"], writes=[pk])
                    p.op("act", lambda e, ps=ps, half=half, b2=b2: e.copy(xo[b2][:, half * 512:(half + 1) * 512], ps[:]), reads=[pk], writes=[("xo", b2, half)])
                p.op("act", lambda e, b2=b2: e.activation(out=xo2[b2][:], in_=xo[b2][:], func=AF.Square, accum_out=sm[:, 20 + b2:21 + b2]),
                     reads=[("xo", b2, 0), ("xo", b2, 1), ("xo2", b2)], writes=[("xo2", b2), ("sm", 20 + b2)])
                p.op("act", lambda e, b2=b2: e.activation(out=sm[:, 22 + b2:23 + b2], in_=sm[:, 20 + b2:21 + b2], func=AF.Sqrt, bias=EPS, scale=1.0 / D),
                     reads=[("sm", 20 + b2)], writes=[("sm", 22 + b2)])
                p.op("dve", lambda e, b2=b2: e.reciprocal(out=sm[:, 24 + b2:25 + b2], in_=sm[:, 22 + b2:23 + b2]), reads=[("sm", 22 + b2)], writes=[("sm", 24 + b2)])
                p.op("dve", lambda e, b2=b2: e.scalar_tensor_tensor(out=xo2[b2][:], in0=xo[b2][:], scalar=sm[:, 24 + b2:25 + b2], in1=fgb, op0=ALU.mult, op1=ALU.mult),
                     reads=[("xo", b2, 0), ("xo", b2, 1), ("sm", 24 + b2), "fgb", ("xo2", b2)], writes=[("xo2", b2)])
                p.dma("sp", out_d[s, blk * 128:(blk + 1) * 128, :], xo2[b2][:], reads=[("xo2", b2)])

        p.emit()
    return nc


_NC_CACHE = {}


def kernel(**inputs):
    n_cores = 4
    nseq = 4
    names = ["x", "c", "ctx", "c_ctx", "w_mod", "b_mod", "norm1_g", "w_in", "b_in", "mlstm_g", "sgu_ln_g", "sgu_ln_b",
             "sgu_w", "sgu_b", "conv_w", "conv_b", "conv_ln_g", "conv_ln_b", "w_out", "norm2_g", "w_gu", "w_down", "final_g"]
    arrs = {k: np.ascontiguousarray(np.asarray(inputs[k], dtype=np.float32)) for k in names}
    if "nc" not in _NC_CACHE:
        _NC_CACHE["nc"] = build_nc(nseq=nseq, depth=2)
    nc = _NC_CACHE["nc"]
    in_maps = []
    for i in range(n_cores):
        m = dict(arrs)
        m["x"] = arrs["x"][i * nseq:(i + 1) * nseq]
        m["c"] = arrs["c"][i * nseq:(i + 1) * nseq]
        m["ctx"] = arrs["ctx"][i * nseq:(i + 1) * nseq]
        in_maps.append(m)
    res = run_bass_kernel_spmd(nc, in_maps, core_ids=list(range(n_cores)))
    return np.concatenate([r["out"] for r in res.results], axis=0)
```
